# Optimizing a Trainium2 kernel written in Bass

```python
import jax, jax.numpy as jnp
from jax import lax
import numpy as np

D_MODEL = 1024
BATCH = 2
SEQ = 8192
DEPTH = 2

N_MIXERS = 2
CONV_WIDTH = 31
POOL_WINDOWS = (2, 4, 8, 16)
N_POOL_GROUPS = len(POOL_WINDOWS)
POOL_GROUP_DIM = D_MODEL // N_POOL_GROUPS
D_FF = ((8 * D_MODEL // 3 + 255) // 256) * 256
N_ADA = 6
EPS = 1e-6
N_CONV_LAYERS = (DEPTH + 1) // 2
N_POOL_LAYERS = DEPTH // 2

kernel_name = "hybrid_conformer_conv_multiscale_pool_trunk"


def rms_norm(x, g):
    xf = x.astype(jnp.float32)
    y = xf * lax.rsqrt(jnp.mean(xf * xf, axis=-1, keepdims=True) + EPS)
    return (y * g.astype(jnp.float32)).astype(x.dtype)


def layer_norm(x, g, b):
    xf = x.astype(jnp.float32)
    mu = jnp.mean(xf, axis=-1, keepdims=True)
    var = jnp.mean(jnp.square(xf - mu), axis=-1, keepdims=True)
    y = (xf - mu) * lax.rsqrt(var + EPS)
    return (y * g.astype(jnp.float32) + b.astype(jnp.float32)).astype(x.dtype)


def modulate(h, shift, scale):
    return h * (1.0 + scale[:, None, :]) + shift[:, None, :]


def conformer_conv(h, w1, b1, w_dw, b_dw, ln_g, ln_b, w2, b2):
    u = h @ w1 + b1
    a, g = jnp.split(u, 2, axis=-1)
    u = a * jax.nn.sigmoid(g)
    u = lax.conv_general_dilated(
        u, w_dw[:, None, :].astype(u.dtype),
        window_strides=(1,), padding=[(CONV_WIDTH - 1, 0)],
        dimension_numbers=("NWC", "WIO", "NWC"),
        feature_group_count=D_MODEL) + b_dw
    u = jax.nn.silu(layer_norm(u, ln_g, ln_b))
    return u @ w2 + b2


def multiscale_pool(h, w_grp, ls):
    B, S, D = h.shape
    hf = h.astype(jnp.float32).reshape(B, S, N_POOL_GROUPS, POOL_GROUP_DIM)
    cs = jnp.cumsum(hf, axis=1)
    cs = jnp.concatenate([jnp.zeros_like(cs[:, :1]), cs], axis=1)
    t = jnp.arange(S)
    pooled = []
    for gi, w in enumerate(POOL_WINDOWS):
        lo = jnp.maximum(t + 1 - w, 0)
        win_sum = cs[:, t + 1, gi] - cs[:, lo, gi]
        cnt = (t + 1 - lo).astype(jnp.float32)
        pooled.append(win_sum / cnt[None, :, None])
    pooled = jnp.stack(pooled, axis=2)
    mixed = (pooled - hf).astype(h.dtype)
    y = jnp.einsum("bsgc,gcd->bsgd", mixed, w_grp).reshape(B, S, D)
    return y * ls


def swiglu_ffn(h, w_gate, w_up, w_down):
    return (jax.nn.silu(h @ w_gate) * (h @ w_up)) @ w_down


def setup_inputs(seed: int = 0) -> dict:
    key = jax.random.key(seed)
    ks = iter(jax.random.split(key, 32))
    D, F = D_MODEL, D_FF
    nrm = lambda shape, s: jax.random.normal(next(ks), shape, jnp.float32) * s
    return {
        "x": nrm((BATCH, SEQ, D), 1.0),
        "c": nrm((BATCH, D), 1.0),
        "ada_w": nrm((DEPTH, D, N_ADA * D), 0.5 * D ** -0.5),
        "ada_b": nrm((DEPTH, N_ADA * D), 0.01),
        "norm_mix_g": 1.0 + nrm((DEPTH, D), 0.02),
        "norm_ffn_g": 1.0 + nrm((DEPTH, D), 0.02),
        "conv_w1": nrm((N_CONV_LAYERS, D, 2 * D), D ** -0.5),
        "conv_b1": nrm((N_CONV_LAYERS, 2 * D), 0.01),
        "conv_wdw": nrm((N_CONV_LAYERS, CONV_WIDTH, D), CONV_WIDTH ** -0.5),
        "conv_bdw": nrm((N_CONV_LAYERS, D), 0.01),
        "conv_ln_g": 1.0 + nrm((N_CONV_LAYERS, D), 0.02),
        "conv_ln_b": nrm((N_CONV_LAYERS, D), 0.01),
        "conv_w2": nrm((N_CONV_LAYERS, D, D), D ** -0.5),
        "conv_b2": nrm((N_CONV_LAYERS, D), 0.01),
        "pool_w": nrm((N_POOL_LAYERS, N_POOL_GROUPS, POOL_GROUP_DIM, POOL_GROUP_DIM), POOL_GROUP_DIM ** -0.5),
        "pool_ls": 1.0 + nrm((N_POOL_LAYERS, D), 0.02),
        "ffn_w_gate": nrm((DEPTH, D, F), D ** -0.5),
        "ffn_w_up": nrm((DEPTH, D, F), D ** -0.5),
        "ffn_w_down": nrm((DEPTH, F, D), F ** -0.5),
        "final_g": 1.0 + nrm((D,), 0.02),
    }


def reference(x, c, ada_w, ada_b, norm_mix_g, norm_ffn_g,
              conv_w1, conv_b1, conv_wdw, conv_bdw, conv_ln_g, conv_ln_b,
              conv_w2, conv_b2, pool_w, pool_ls,
              ffn_w_gate, ffn_w_up, ffn_w_down, final_g):
    c_act = jax.nn.silu(c)
    for i in range(DEPTH):
        mod = c_act @ ada_w[i] + ada_b[i]
        sh_m, sc_m, g_m, sh_f, sc_f, g_f = jnp.split(mod, N_ADA, axis=-1)
        h = modulate(rms_norm(x, norm_mix_g[i]), sh_m, sc_m)
        j = i // N_MIXERS
        if i % N_MIXERS == 0:
            y = conformer_conv(h, conv_w1[j], conv_b1[j], conv_wdw[j], conv_bdw[j],
                               conv_ln_g[j], conv_ln_b[j], conv_w2[j], conv_b2[j])
        else:
            y = multiscale_pool(h, pool_w[j], pool_ls[j])
        x = x + (1.0 + g_m)[:, None, :] * y
        h = modulate(rms_norm(x, norm_ffn_g[i]), sh_f, sc_f)
        y = swiglu_ffn(h, ffn_w_gate[i], ffn_w_up[i], ffn_w_down[i])
        x = x + (1.0 + g_f)[:, None, :] * y
    return rms_norm(x, final_g)
```

```python
import numpy as np
import concourse.bass as bass
import concourse.mybir as mybir
from concourse.bass_utils import run_bass_kernel_spmd

F32 = mybir.dt.float32
BF16 = mybir.dt.bfloat16
U8 = mybir.dt.uint8
AF = mybir.ActivationFunctionType
ALU = mybir.AluOpType
DSIZE = {F32: 4, BF16: 2, U8: 1}

ENGINES = ("pe", "act", "dve", "pool", "sp")
import os
OPT_A3 = os.environ.get("K_OPT_A3", "1") == "1"
OPT_ADA = os.environ.get("K_OPT_ADA", "1") == "1"
OPT_FINAL = os.environ.get("K_OPT_FINAL", "1") == "1"


def _ap_intervals(ap, cap=64):
    es = DSIZE[ap.dtype]
    pat = ap.ap
    pstep = pat[0][0]
    off = ap.offset
    base = off % pstep if pstep > 0 else off
    dims = [(s, n) for (s, n) in pat[1:] if n > 1]
    dims.sort(key=lambda d: -d[0])
    run = 1
    while dims and dims[-1][0] == run:
        run *= dims[-1][1]
        dims.pop()
    nout = 1
    for _, n in dims:
        nout *= n
    if nout > cap or any(s < run for s, _ in dims):
        ext = run + sum(s * (n - 1) for s, n in dims)
        return ap.tensor.name, [(base * es, (base + ext) * es)]
    starts = [base]
    for s, n in dims:
        starts = [b + s * i for b in starts for i in range(n)]
    return ap.tensor.name, [(b * es, (b + run) * es) for b in starts]


class _Op:
    __slots__ = ("eng", "fn", "idx", "waits", "sig", "semval", "dma_key", "name")

    def __init__(self, eng, fn, dma_key=None, name=""):
        self.eng = eng
        self.fn = fn
        self.idx = -1
        self.waits = []
        self.sig = False
        self.semval = 0
        self.dma_key = dma_key
        self.name = name


class Prog:
    def __init__(self, nc, tracked=("sb", "ps")):
        self.nc = nc
        self.tracked = set(tracked)
        self.ops = {e: [] for e in ENGINES}
        self.wr = {s: [] for s in tracked}
        self.rd = {s: [] for s in tracked}
        self.waited = {e: {} for e in ENGINES}
        self.dma_count = {}
        self.all_ops = []

    PS_BANK = 2048

    def _deps_for(self, ins, outs, eng=None):
        deps = []
        for ap in ins:
            sp, ivs = _ap_intervals(ap)
            if sp not in self.tracked:
                continue
            for lo, hi in ivs:
                for (a, b, op) in self.wr[sp]:
                    if a < hi and lo < b:
                        deps.append(op)
        for ap in outs:
            sp, ivs = _ap_intervals(ap)
            if sp not in self.tracked:
                continue
            for lo, hi in ivs:
                for (a, b, op) in self.wr[sp]:
                    if a < hi and lo < b:
                        deps.append(op)
                for (a, b, op) in self.rd[sp]:
                    if a < hi and lo < b:
                        deps.append(op)
        if "ps" in self.tracked:
            B = self.PS_BANK
            for ap in list(ins) + list(outs):
                sp, ivs = _ap_intervals(ap)
                if sp != "ps":
                    continue
                for lo, hi in ivs:
                    b0, b1 = lo // B, (hi - 1) // B
                    for lst in (self.wr[sp], self.rd[sp]):
                        for (a, b, op) in lst:
                            if op.eng != eng and a // B <= b1 and b0 <= (b - 1) // B:
                                deps.append(op)
        return deps

    @staticmethod
    def _cut(lst, lo, hi):
        out = []
        for (a, b, op) in lst:
            if a < hi and lo < b:
                if a < lo:
                    out.append((a, lo, op))
                if hi < b:
                    out.append((hi, b, op))
            else:
                out.append((a, b, op))
        return out

    def _update(self, op, ins, outs):
        for ap in outs:
            sp, ivs = _ap_intervals(ap)
            if sp not in self.tracked:
                continue
            for lo, hi in ivs:
                self.wr[sp] = self._cut(self.wr[sp], lo, hi)
                self.rd[sp] = self._cut(self.rd[sp], lo, hi)
                self.wr[sp].append((lo, hi, op))
        for ap in ins:
            sp, ivs = _ap_intervals(ap)
            if sp not in self.tracked:
                continue
            for lo, hi in ivs:
                if op.dma_key is None:
                    self.rd[sp] = [
                        (a, b, o) for (a, b, o) in self.rd[sp]
                        if not (o.eng == op.eng and o.dma_key is None and lo <= a and b <= hi)
                    ]
                self.rd[sp].append((lo, hi, op))

    def add(self, eng, fn, ins=(), outs=(), dma_key=None, extra_deps=(), name=""):
        op = _Op(eng, fn, dma_key, name)
        lst = self.ops[eng]
        op.idx = len(lst)
        deps = self._deps_for(ins, outs, eng) + list(extra_deps)
        best = {}
        dma_deps = []
        for d in deps:
            if d is op:
                continue
            if d.dma_key is not None:
                if d not in dma_deps:
                    dma_deps.append(d)
                continue
            if d.eng == eng and eng == "pe":
                continue
            cur = best.get(d.eng)
            if cur is None or d.idx > cur.idx:
                best[d.eng] = d
        w = self.waited[eng]
        for peng, d in best.items():
            if w.get(peng, -1) >= d.idx:
                continue
            w[peng] = d.idx
            d.sig = True
            op.waits.append(d)
        for d in dma_deps:
            key = ("dma", d.dma_key)
            if w.get(key, -1) >= d.semval:
                continue
            w[key] = d.semval
            op.waits.append(d)
        if dma_key is not None:
            n = self.dma_count.get(dma_key, 0) + 1
            self.dma_count[dma_key] = n
            op.semval = 16 * n
        lst.append(op)
        self.all_ops.append(op)
        self._update(op, ins, outs)
        return op

    def mm(self, out, lhsT, rhs, start=True, stop=True, name=""):
        return self.add("pe", lambda e: e.matmul(out, lhsT, rhs, start=start, stop=stop),
                        ins=[lhsT, rhs], outs=[out], name=name)

    def act(self, out, in_, func, bias=None, scale=None, eng="act", name=""):
        ins = [in_]
        kw = {}
        if bias is not None:
            kw["bias"] = bias
            if not isinstance(bias, (int, float)):
                ins.append(bias)
        if scale is not None:
            kw["scale"] = scale
            if not isinstance(scale, (int, float)):
                ins.append(scale)
        return self.add(eng, lambda e: e.activation(out, in_, func, **kw), ins=ins, outs=[out], name=name)

    def tt(self, eng, out, a, b, op, name=""):
        return self.add(eng, lambda e: e.tensor_tensor(out, a, b, op), ins=[a, b], outs=[out], name=name)

    def ts(self, eng, out, a, s1, s2, op0, op1=None, name=""):
        ins = [a] + [s for s in (s1, s2) if s is not None and not isinstance(s, (int, float))]
        if op1 is None:
            return self.add(eng, lambda e: e.tensor_scalar(out, a, s1, None, op0), ins=ins, outs=[out], name=name)
        return self.add(eng, lambda e: e.tensor_scalar(out, a, s1, s2, op0, op1), ins=ins, outs=[out], name=name)

    def stt(self, eng, out, in0, scalar, in1, op0, op1, name=""):
        ins = [in0, in1] + ([] if isinstance(scalar, (int, float)) else [scalar])
        return self.add(eng, lambda e: e.scalar_tensor_tensor(out, in0, scalar, in1, op0, op1),
                        ins=ins, outs=[out], name=name)

    def copy(self, eng, out, in_, name=""):
        if eng == "act":
            return self.add(eng, lambda e: e.copy(out, in_), ins=[in_], outs=[out], name=name)
        return self.add(eng, lambda e: e.tensor_copy(out, in_), ins=[in_], outs=[out], name=name)

    def memset(self, eng, ap, val, name=""):
        return self.add(eng, lambda e: e.memset(ap, val), ins=[], outs=[ap], name=name)

    def dma(self, queue, out, in_, key, name="", after=()):
        return self.add(queue, lambda e: e.dma_start(out=out, in_=in_), ins=[in_], outs=[out],
                        dma_key=key, extra_deps=after, name=name)

    def barrier_wait(self, eng, ops):
        return self.add(eng, None, extra_deps=ops)

    def emit(self):
        nc = self.nc
        for e in ENGINES:
            n = 0
            for op in self.ops[e]:
                if op.dma_key is None and op.sig:
                    n += 1
                    op.semval = n
        import contextlib
        with contextlib.ExitStack() as st:
            esem = {e: st.enter_context(nc.semaphore("s_" + e)) for e in ENGINES}
            dsem = {k: st.enter_context(nc.semaphore("d_%d" % i))
                    for i, k in enumerate(self.dma_count)}
            block = st.enter_context(nc.Block())

            def run(ename, eng):
                for op in self.ops[ename]:
                    for d in op.waits:
                        if d.dma_key is not None:
                            eng.wait_ge(dsem[d.dma_key], d.semval)
                        else:
                            eng.wait_ge(esem[d.eng], d.semval)
                    if op.fn is None:
                        continue
                    ins = op.fn(eng)
                    if op.dma_key is not None:
                        ins.then_inc(dsem[op.dma_key], 16)
                    elif op.sig:
                        ins.then_inc(esem[ename], 1)

            @block.tensor
            def _(e):
                run("pe", e)

            @block.scalar
            def _(e):
                run("act", e)

            @block.vector
            def _(e):
                run("dve", e)

            @block.gpsimd
            def _(e):
                run("pool", e)

            @block.sync
            def _(e):
                run("sp", e)


D = 1024
NCH = 8
FF = 2816
NFC = 22
KW = 31
HALO = 64
TOK = 2048
NT = HALO + TOK
PAD = 32
TILES = [(0, 64)] + [(HALO + 512 * i, 512) for i in range(4)]
PIECES_L = {0: [(0, 3), (3, 3), (6, 4), (10, 4), (14, 4), (18, 4)],
            1: [(0, 4), (4, 4), (8, 4), (12, 4), (16, 3), (19, 3)]}
N_PIECES = 6
NF = 4
EPS = 1e-6
ADA_BLK = 512
N_ADA_BLK = 6 * D // ADA_BLK
POOL_W = (2, 4, 8, 16)

PV = {}
_o = 0
for _n, _w in (("c", 8), ("adab", 96), ("nmg", 16), ("nfg", 16), ("b1", 16), ("wdw", 248),
               ("bdw", 8), ("lng", 8), ("lnb", 8), ("b2", 8), ("pls", 8), ("fing", 8),
               ("mask", 1), ("pscale", 64)):
    PV[_n] = _o
    _o += _w
NPV = _o


def build_nc(stop_after=None):
    nc = bass.Bass("TRN2", target_bir_lowering=False)
    x_d = nc.dram_tensor("x", [128, NCH, NT], F32, kind="ExternalInput").ap()
    pv_d = nc.dram_tensor("pv", [128, NPV], F32, kind="ExternalInput").ap()
    adaw_d = nc.dram_tensor("adaw", [2, N_ADA_BLK, 128, NCH * ADA_BLK], F32, kind="ExternalInput").ap()
    w1_d = nc.dram_tensor("w1", [4, 128, NCH * 512], F32, kind="ExternalInput").ap()
    w2_d = nc.dram_tensor("w2", [128, NCH * D], F32, kind="ExternalInput").ap()
    wg_d = nc.dram_tensor("wg", [2, 128, NCH * FF], F32, kind="ExternalInput").ap()
    wu_d = nc.dram_tensor("wu", [2, 128, NCH * FF], F32, kind="ExternalInput").ap()
    wd_d = nc.dram_tensor("wd", [2, 128, NFC * D], F32, kind="ExternalInput").ap()
    pw_d = nc.dram_tensor("pw", [128, 4 * 2 * 256], F32, kind="ExternalInput").ap()
    y_d = nc.dram_tensor("y", [128, NCH, TOK], F32, kind="ExternalOutput").ap()

    import contextlib
    with contextlib.ExitStack() as st:
        SB_BYTES = 211500
        sb = st.enter_context(nc.sbuf_tensor("sb", [128, SB_BYTES], U8))
        ps = st.enter_context(nc.psum_tensor("ps", [128, 4096], F32))
        P = Prog(nc)
        cur = [0]

        def carve(nbytes):
            a = cur[0]
            cur[0] = a + (nbytes + 63) // 64 * 64
            assert cur[0] <= SB_BYTES, ("SBUF overflow", cur[0])
            return a

        def view(off, shape, dt):
            n = int(np.prod(shape)) * DSIZE[dt]
            a = sb[:, off:off + n].bitcast(dt)
            if len(shape) == 2:
                a = a.rearrange("p (a b) -> p a b", a=shape[0])
            elif len(shape) == 3:
                a = a.rearrange("p (a b c) -> p a b c", a=shape[0], b=shape[1])
            return a

        X = view(carve(NCH * NT * 4), [NCH, NT], F32)
        AW = PAD + NT
        ACTB = view(carve(NCH * AW * 2), [NCH, AW], BF16)
        PVS = view(carve(NPV * 4), [NPV], F32)
        MOD = view(carve(2 * 48 * 4), [2, 48], F32)
        DER = view(carve(2 * 6 * 8 * 4), [2, 6, 8], F32)
        CACT = view(carve(8 * 2), [8], BF16)
        CSIL = view(carve(8 * 4), [8], F32)
        IDENT = view(carve(128 * 2), [128], BF16)
        IDENTF = view(carve(128 * 4), [128], F32)
        ONESD = view(carve(128 * 2), [128], BF16)
        EPSB = view(carve(4), [1], F32)
        SQ_off = carve(NCH * 512 * 2)
        SQ = view(SQ_off, [NCH, 512], BF16)
        Z = [view(SQ_off + i * NF * 512 * 2, [NF, 512], BF16) for i in range(2)]
        H_off = carve(2 * NCH * 512 * 2)
        Hb = [view(H_off + i * NCH * 512 * 2, [NCH, 512], BF16) for i in range(2)]
        DIAG = [view(H_off + i * KW * 128 * 2, [KW, 128], BF16) for i in range(2)]
        FT = [view(carve(528 * 4), [528], F32) for _ in range(6)]
        RING_SLOT = 3 * NCH * NF * 128 * 2
        W_off = carve(2 * RING_SLOT)
        ring = []
        for s in range(2):
            b = W_off + s * RING_SLOT
            ring.append((view(b, [NCH, NF * 128], BF16),
                         view(b + NCH * NF * 128 * 2, [NCH, NF * 128], BF16),
                         view(b + 2 * NCH * NF * 128 * 2, [NF, D], BF16)))
        W1 = view(W_off, [4, NCH, 512], BF16)
        W2 = view(W_off + 4 * NCH * 512 * 2, [NCH, D], BF16)
        ADA_off = carve(2 * NCH * ADA_BLK * 2)
        ADA = [view(ADA_off + i * NCH * ADA_BLK * 2, [NCH, ADA_BLK], BF16) for i in range(2)]
        PW = view(ADA_off, [4, 2, 256], BF16)
        PSB = [ps[:, b * 512:(b + 1) * 512] for b in range(8)]

        def pv(name, i=0, n=1):
            return PVS[:, PV[name] + i: PV[name] + i + n]

        P.dma("sp", PVS, pv_d, "pv")
        for j, (t0, n) in enumerate(TILES[:2]):
            P.dma("sp", X[:, :, t0:t0 + n], x_d[:, :, t0:t0 + n], "x%d" % j)
        P.memset("pool", IDENTF, 0.0)
        P.memset("dve", ONESD, 1.0 / D)
        P.memset("dve", EPSB, EPS)
        P.add("pool", lambda e: e.affine_select(IDENTF, IDENTF, pattern=[[-1, 128]], compare_op=ALU.not_equal,
                                                fill=1.0, base=0, channel_multiplier=1),
              ins=[IDENTF], outs=[IDENTF])
        P.copy("pool", IDENT, IDENTF)
        P.act(CSIL, pv("c", 0, 8), AF.Silu)
        P.copy("dve", CACT, CSIL)

        ada_ctr = [0]

        def ada_dma(L, blk, buf=None, key=None):
            if buf is None:
                slot = ada_ctr[0] % 2
                ada_ctr[0] += 1
                buf, key = ADA[slot], "ada%d" % slot
            P.dma("pool", buf, adaw_d[L, blk].rearrange("p (k c) -> p k c", k=NCH), key)
            return buf

        ada_slots = {}

        def ada_dma_after(L, blk, after):
            slot = ada_ctr[0] % 2
            ada_ctr[0] += 1
            P.dma("pool", ADA[slot], adaw_d[L, blk].rearrange("p (k c) -> p k c", k=NCH), "ada%d" % slot, after=after)
            return ADA[slot]

        def mod_mm(L, blk):
            abuf = ada_slots.pop((L, blk))
            for j in range(4):
                col = 4 * blk + j
                for kc in range(NCH):
                    P.mm(PSB[1][:, L * 64 + col: L * 64 + col + 1], abuf[:, kc, j * 128:(j + 1) * 128],
                         CACT[:, kc:kc + 1], kc == 0, kc == NCH - 1)
            P.tt("dve", MOD[:, L, 4 * blk:4 * blk + 4], PSB[1][:, L * 64 + 4 * blk: L * 64 + 4 * blk + 4],
                 pv("adab", L * 48 + 4 * blk, 4), ALU.add)

        def mod_vec(L, v):
            return MOD[:, L, v * 8:(v + 1) * 8]

        def derive(L, which):
            if which == "mix":
                P.stt("dve", DER[:, L, 0, :], mod_vec(L, 1), 1.0, pv("nmg", L * 8, 8), ALU.add, ALU.mult)
            elif which == "gate_m":
                P.ts("dve", DER[:, L, 1, :], mod_vec(L, 2), 1.0, None, ALU.add)
                if L == 0:
                    P.tt("dve", DER[:, L, 2, :], DER[:, L, 1, :], pv("b2", 0, 8), ALU.mult)
                else:
                    P.tt("dve", DER[:, L, 2, :], DER[:, L, 1, :], pv("pls", 0, 8), ALU.mult)
            elif which == "ffn":
                P.stt("dve", DER[:, L, 3, :], mod_vec(L, 4), 1.0, pv("nfg", L * 8, 8), ALU.add, ALU.mult)
            elif which == "gate_f":
                P.ts("dve", DER[:, L, 4, :], mod_vec(L, 5), 1.0, None, ALU.add)

        ft_ctr = [0]
        ft_n = [4]

        def ft():
            ft_ctr[0] += 1
            return FT[2 + ft_ctr[0] % ft_n[0]]

        RS = FT[0]
        MEAN = FT[1]

        def rsqrt_into(dst, src):
            P.act(dst, src, AF.Ln, bias=EPSB[:, 0:1])
            P.act(dst, dst, AF.Exp, scale=-0.5)

        def prep_steps(t0, n, gm, sh, dst_fn, pool_style=False, rs=None, rs_off=0, sq=None):
            rs_t = RS if rs is None else rs
            SQb = SQ if sq is None else sq
            steps = []

            def s_sq():
                for c in range(NCH):
                    P.act(SQb[:, c, :n], X[:, c, t0:t0 + n], AF.Square)
            steps.append(s_sq)

            def s_stat():
                for c in range(NCH):
                    P.mm(PSB[0][:, :n], ONESD, SQb[:, c, :n], c == 0, c == NCH - 1)
                rsqrt_into(rs_t[:, rs_off:rs_off + n], PSB[0][:, :n])
            steps.append(s_stat)
            if pool_style:
                return steps
            for c in range(NCH):
                def s_h(c=c):
                    t = ft()
                    P.tt("dve", t[:, :n], X[:, c, t0:t0 + n], rs_t[:, rs_off:rs_off + n], ALU.mult)
                    P.act(dst_fn(c), t[:, :n], AF.Identity, bias=sh[:, c:c + 1], scale=gm[:, c:c + 1])
                steps.append(s_h)
            return steps

        def run_interleaved(groups, steps, start=0):
            steps = list(steps)
            ng = len(groups)
            for gi, g in enumerate(groups):
                g()
                if gi >= start and steps:
                    remaining_groups = ng - gi
                    k = -(-len(steps) // remaining_groups)
                    for _ in range(k):
                        if steps:
                            steps.pop(0)()
            for s in steps:
                s()

        for blk in range(2):
            ada_slots[(0, blk)] = ada_dma(0, blk)
        for i, blk in enumerate((2, 3)):
            base = (NCH * NT * 4 + 63) // 64 * 64 + i * NCH * ADA_BLK * 2
            tv = view(base, [NCH, ADA_BLK], BF16)
            ada_slots[(0, blk)] = ada_dma(0, blk, buf=tv, key="adat%d" % i)
        w1_ops = []
        for b in range(4):
            w1_ops.append(P.dma("pool", W1[:, b], w1_d[b].rearrange("p (k c) -> p k c", k=NCH), "w1_%d" % b))
        for j, (t0, n) in enumerate(TILES):
            if j >= 2:
                P.dma("sp", X[:, :, t0:t0 + n], x_d[:, :, t0:t0 + n], "x%d" % j, after=[w1_ops[1]])
        for blk in range(4):
            mod_mm(0, blk)
        P.memset("dve", ACTB[:, :, 0:PAD], 0.0)
        for blk in range(2):
            ada_slots[(0, blk + 4)] = ada_dma(0, blk + 4) if blk else ada_dma_after(0, blk + 4, [w1_ops[3]])
        P.dma("pool", W2, w2_d.rearrange("p (k c) -> p k c", k=NCH), "w2")
        derive(0, "mix")
        next_ada = [4]

        def more_mod0(k):
            for _ in range(k):
                blk = next_ada[0]
                if blk >= N_ADA_BLK:
                    return
                mod_mm(0, blk)
                if blk + 2 < N_ADA_BLK and (0, blk + 2) not in ada_slots:
                    ada_slots[(0, blk + 2)] = ada_dma(0, blk + 2)
                next_ada[0] += 1

        gm0 = DER[:, 0, 0, :]
        sh0 = mod_vec(0, 0)
        U = ACTB

        def a1_prep(j):
            t0, n = TILES[j]
            hb = Hb[j % 2]
            return prep_steps(t0, n, gm0, sh0, lambda c: hb[:, c, :n])

        for s in a1_prep(0):
            s()
        bank_ctr = [0]
        bank_set = [[2, 3, 4, 5, 6, 7]]

        def nb():
            bank_ctr[0] += 1
            bs = bank_set[0]
            return PSB[bs[bank_ctr[0] % len(bs)]]

        for j, (t0, n) in enumerate(TILES):
            hb = Hb[j % 2]
            groups = []
            for oc in range(NCH):
                def g(oc=oc):
                    blk, jj = oc // 2, oc % 2
                    pa, pg = nb(), nb()
                    for kc in range(NCH):
                        P.mm(pa[:, :n], W1[:, blk, kc, jj * 128:(jj + 1) * 128], hb[:, kc, :n], kc == 0, kc == NCH - 1)
                    for kc in range(NCH):
                        P.mm(pg[:, :n], W1[:, blk, kc, 256 + jj * 128:256 + (jj + 1) * 128], hb[:, kc, :n],
                             kc == 0, kc == NCH - 1)
                    sg = ft()
                    P.act(sg[:, :n], pg[:, :n], AF.Sigmoid, bias=pv("b1", 8 + oc))
                    P.stt("dve", U[:, oc, PAD + t0:PAD + t0 + n], pa[:, :n], pv("b1", oc), sg[:, :n], ALU.add, ALU.mult)
                groups.append(g)
            steps = a1_prep(j + 1) if j + 1 < len(TILES) else []
            run_interleaved(groups, steps)
            if j == 0:
                P.ts("dve", U[:, :, PAD:PAD + HALO], U[:, :, PAD:PAD + HALO], pv("mask"), None, ALU.mult)
            more_mod0(2)
        more_mod0(12)
        for blk in range(2):
            ada_slots[(1, blk)] = ada_dma(1, blk)
        derive(0, "gate_m")
        derive(0, "ffn")
        derive(0, "gate_f")

        ring_ctr = [0]

        def ring_dma(L, p):
            f0, nf = PIECES_L[L][p]
            slot = ring_ctr[0] % 2
            ring_ctr[0] += 1
            wg_s, wu_s, wd_s = ring[slot]
            off = NCH * 128 * f0
            P.dma("pool", wg_s[:, :, :nf * 128],
                  wg_d[L, :, off:off + NCH * nf * 128].rearrange("p (k c) -> p k c", k=NCH), "rg%d" % slot)
            P.dma("pool", wu_s[:, :, :nf * 128],
                  wu_d[L, :, off:off + NCH * nf * 128].rearrange("p (k c) -> p k c", k=NCH), "ru%d" % slot)
            P.dma("pool", wd_s[:, :nf, :],
                  wd_d[L, :, f0 * D:(f0 + nf) * D].rearrange("p (f c) -> p f c", f=nf), "rd%d" % slot)
            return slot

        T_D = 4

        def build_diag(c):
            dg = DIAG[c % 2]
            for k in range(T_D, KW):
                P.ts("dve", dg[:, k, :], IDENT, pv("wdw", k * 8 + c), None, ALU.mult)

        build_diag(0)
        ring_slots = {}
        for c in range(NCH):
            if c + 1 < NCH:
                build_diag(c + 1)
            if c == 1:
                ring_slots[(0, 0)] = ring_dma(0, 0)
            dg = DIAG[c % 2]
            for j in reversed(range(len(TILES))):
                t0, n = TILES[j]
                pc = nb()
                for k in range(T_D, KW):
                    P.mm(pc[:, :n], dg[:, k, :], U[:, c, PAD + t0 - 30 + k:PAD + t0 - 30 + k + n], k == T_D, k == KW - 1)
                acc = ft()
                for k in range(T_D):
                    src = U[:, c, PAD + t0 - 30 + k:PAD + t0 - 30 + k + n]
                    if k == 0:
                        P.ts("dve", acc[:, :n], src, pv("wdw", k * 8 + c), None, ALU.mult)
                    else:
                        P.stt("dve", acc[:, :n], src, pv("wdw", k * 8 + c), acc[:, :n], ALU.mult, ALU.add)
                P.stt("dve", U[:, c, PAD + t0:PAD + t0 + n], pc[:, :n], pv("bdw", c), acc[:, :n], ALU.add, ALU.add)
                if OPT_A3:
                    P.act(X[:, c, t0:t0 + n], X[:, c, t0:t0 + n], AF.Identity, bias=DER[:, 0, 2, c:c + 1])

        V = ACTB
        g_m1 = DER[:, 0, 1, :]
        gb2 = DER[:, 0, 2, :]

        MEANb = [FT[1], FT[5]]
        RSb = [FT[0], FT[4]]

        def a3_stats(j):
            t0, n = TILES[j]
            mean_t, rs_t = MEANb[j % 2], RSb[j % 2]
            steps = []

            def s_sq():
                for c in range(NCH):
                    P.act(SQ[:, c, :n], V[:, c, PAD + t0:PAD + t0 + n], AF.Square)
            steps.append(s_sq)

            def s_stat():
                for c in range(NCH):
                    P.mm(PSB[0][:, :n], ONESD, V[:, c, PAD + t0:PAD + t0 + n], c == 0, c == NCH - 1)
                for c in range(NCH):
                    P.mm(PSB[1][:, :n], ONESD, SQ[:, c, :n], c == 0, c == NCH - 1)
                P.copy("act", mean_t[:, :n], PSB[0][:, :n])
                t = ft()
                P.tt("dve", t[:, :n], mean_t[:, :n], mean_t[:, :n], ALU.mult)
                P.tt("dve", t[:, :n], PSB[1][:, :n], t[:, :n], ALU.subtract)
                P.ts("dve", t[:, :n], t[:, :n], 0.0, None, ALU.max)
                rsqrt_into(rs_t[:, :n], t[:, :n])
            steps.append(s_stat)
            return steps

        def a3_ln(j):
            t0, n = TILES[j]
            mean_t, rs_t = MEANb[j % 2], RSb[j % 2]
            sbuf = Hb[j % 2]
            steps = []
            for c in range(NCH):
                def s_ln(c=c):
                    t = ft()
                    P.tt("dve", t[:, :n], V[:, c, PAD + t0:PAD + t0 + n], mean_t[:, :n], ALU.subtract)
                    P.tt("dve", t[:, :n], t[:, :n], rs_t[:, :n], ALU.mult)
                    P.act(sbuf[:, c, :n], t[:, :n], AF.Silu, bias=pv("lnb", c), scale=pv("lng", c))
                steps.append(s_ln)
            return steps

        ft_n[0] = 2
        for s_ in a3_stats(0) + a3_stats(1) + a3_ln(0):
            s_()
        for j, (t0, n) in enumerate(TILES):
            sbuf = Hb[j % 2]
            groups = []
            for do in range(NCH):
                def g(do=do):
                    po = nb()
                    for kc in range(NCH):
                        P.mm(po[:, :n], W2[:, kc, do * 128:(do + 1) * 128], sbuf[:, kc, :n], kc == 0, kc == NCH - 1)
                    P.stt("dve", X[:, do, t0:t0 + n], po[:, :n], g_m1[:, do:do + 1], X[:, do, t0:t0 + n],
                          ALU.mult, ALU.add)
                groups.append(g)
            steps = []
            if j + 1 < len(TILES):
                steps += a3_ln(j + 1)
            if j + 2 < len(TILES):
                steps += a3_stats(j + 2)
            if j + 1 == len(TILES) and stop_after != "A":
                t0f, nf0 = TILES[0]
                steps += prep_steps(t0f, nf0, DER[:, 0, 3, :], mod_vec(0, 3),
                                    lambda c: ACTB[:, c, PAD + t0f:PAD + t0f + nf0], sq=SQ, rs=FT[4])
            run_interleaved(groups, steps)
        ft_n[0] = 4
        bank_set[0] = [2, 3, 4, 5]

        H2 = ACTB

        def ffn(L, tiles, l1_mod=False, final=False, first_prepped=False, all_prepped=False, post_down=None, prep_fn=None, lead_steps=None):
            gmf = DER[:, L, 3, :]
            shf = mod_vec(L, 3)
            g_f1 = DER[:, L, 4, :]

            def h2_prep(j):
                t0, n = TILES[j]
                return prep_steps(t0, n, gmf, shf, lambda c: H2[:, c, PAD + t0:PAD + t0 + n], sq=Hb[0])

            if not (first_prepped or all_prepped):
                for s in h2_prep(tiles[0]):
                    s()
            leftover = []
            for p, (f0, nf) in enumerate(PIECES_L[L]):
                if (L, p) not in ring_slots:
                    ring_slots[(L, p)] = ring_dma(L, p)
                slot = ring_slots.pop((L, p))
                wg_s, wu_s, wd_s = ring[slot]
                nxt = None
                if l1_mod:
                    for blk in (2 * p, 2 * p + 1):
                        if (1, blk) not in ada_slots:
                            ada_slots[(1, blk)] = ada_dma(1, blk)
                    for blk in (2 * p, 2 * p + 1):
                        mod_mm(1, blk)
                    for blk in (2 * p + 2, 2 * p + 3):
                        if OPT_ADA and blk < N_ADA_BLK and (1, blk) not in ada_slots:
                            ada_slots[(1, blk)] = ada_dma(1, blk)

                def gu(j, zb):
                    t0, n = TILES[j]
                    groups = []
                    for fi in range(nf):
                        def g(fi=fi):
                            pg, pu = nb(), nb()
                            for kc in range(NCH):
                                P.mm(pg[:, :n], wg_s[:, kc, fi * 128:(fi + 1) * 128], H2[:, kc, PAD + t0:PAD + t0 + n],
                                     kc == 0, kc == NCH - 1)
                            for kc in range(NCH):
                                P.mm(pu[:, :n], wu_s[:, kc, fi * 128:(fi + 1) * 128], H2[:, kc, PAD + t0:PAD + t0 + n],
                                     kc == 0, kc == NCH - 1)
                            sg = ft()
                            P.act(sg[:, :n], pg[:, :n], AF.Silu)
                            P.tt("dve", Z[zb][:, fi, :n], pu[:, :n], sg[:, :n], ALU.mult)
                        groups.append(g)
                    return groups

                def down(j, zb):
                    t0, n = TILES[j]
                    groups = []
                    for do in range(NCH):
                        def g(do=do):
                            pd = PSB[6 + do % 2]
                            for fi in range(nf):
                                P.mm(pd[:, :n], wd_s[:, fi, do * 128:(do + 1) * 128], Z[zb][:, fi, :n],
                                     fi == 0, fi == nf - 1)
                            P.stt("dve", X[:, do, t0:t0 + n], pd[:, :n], g_f1[:, do:do + 1], X[:, do, t0:t0 + n],
                                  ALU.mult, ALU.add)
                        groups.append(g)
                    return groups

                prev = None
                last_piece = (p == N_PIECES - 1) and post_down is not None
                if p >= 1 and not last_piece:
                    bank_set[0] = [0, 2, 3, 4, 5] if l1_mod else [0, 1, 2, 3, 4, 5]
                else:
                    bank_set[0] = [2, 3, 4, 5]
                finished = []
                for ti, j in enumerate(tiles):
                    zb = ti % 2
                    steps = []
                    if p == 0 and ti + 1 < len(tiles) and not all_prepped:
                        steps = (prep_fn or h2_prep)(tiles[ti + 1])
                    if last_piece and finished:
                        steps = steps + post_down(finished.pop(0))
                    if p == 0 and lead_steps:
                        k_ = -(-len(lead_steps) // max(1, len(tiles) - 1 - ti))
                        steps = steps + lead_steps[:k_]
                        del lead_steps[:k_]
                    groups = gu(j, zb)
                    if prev is not None:
                        groups = groups + down(*prev)
                        finished.append(prev[0])
                    run_interleaved(groups, steps)
                    prev = (j, zb)
                dgroups = down(*prev)
                finished.append(prev[0])
                if last_piece:
                    run_interleaved(dgroups, post_down(finished.pop(0)))
                    for jj in finished:
                        leftover.extend(post_down(jj))
                else:
                    for g in dgroups:
                        g()
                nn = None
                if p + 2 < N_PIECES:
                    nn = (L, p + 2)
                elif L == 0:
                    nn = (1, p + 2 - N_PIECES)
                if nn is not None and nn not in ring_slots:
                    ring_slots[nn] = ring_dma(*nn)
                if l1_mod:
                    for blk in (2 * p + 2, 2 * p + 3):
                        if blk < N_ADA_BLK and (1, blk) not in ada_slots:
                            ada_slots[(1, blk)] = ada_dma(1, blk)
            bank_set[0] = [2, 3, 4, 5]
            return leftover

        gm1 = DER[:, 1, 0, :]
        gpool = DER[:, 1, 2, :]
        RSX = FT[0]
        Hb1_off = H_off + NCH * 512 * 2
        HS = [view(Hb1_off + i * 528 * 4, [528], F32) for i in range(2)]
        XC = view(Hb1_off + 2 * 528 * 4, [NCH, 16], F32)
        T16 = view(Hb1_off + 2 * 528 * 4 + NCH * 16 * 4, [16], F32)
        SA = [FT[4], FT[5]]

        def pool_setup():
            derive(1, "mix")
            derive(1, "gate_m")
            derive(1, "ffn")
            derive(1, "gate_f")
            P.dma("pool", PW, pw_d.rearrange("p (g k c) -> p g k c", g=4, k=2), "pw")
            t0, n = TILES[0]
            for s in prep_steps(t0, n, None, None, None, pool_style=True, rs=RSX, rs_off=16, sq=Hb[0]):
                s()
            P.ts("dve", RSX[:, 0:16], RSX[:, 16 + n - 16:16 + n], pv("mask"), None, ALU.mult)
            P.copy("dve", XC, X[:, :, t0 + n - 16:t0 + n])

        def pool_tile_steps(j, with_h2=True):
            t0, n = TILES[j]
            steps = list(prep_steps(t0, n, None, None, None, pool_style=True, rs=RSX, rs_off=16, sq=Hb[0]))
            for c in range(NCH):
                g = c // 2
                w = POOL_W[g]
                hs = HS[c % 2]
                mdst = ACTB[:, c, PAD + t0:PAD + t0 + n]
                steps.append(lambda c=c, hs=hs: P.stt("dve", hs[:, 0:16], XC[:, c, :], gm1[:, c:c + 1], RSX[:, 0:16],
                                                      ALU.mult, ALU.mult))
                steps.append(lambda c=c, hs=hs: P.stt("dve", hs[:, 16:16 + n], X[:, c, t0:t0 + n], gm1[:, c:c + 1],
                                                      RSX[:, 16:16 + n], ALU.mult, ALU.mult))
                a = hs
                s_ = 1
                k = 0
                while s_ < w:
                    b = SA[k % 2]
                    steps.append(lambda a=a, b=b, s_=s_: P.tt("dve", b[:, s_:16 + n], a[:, s_:16 + n],
                                                              a[:, 0:16 + n - s_], ALU.add))
                    a = b
                    s_ *= 2
                    k += 1
                steps.append(lambda a=a, hs=hs, mdst=mdst, w=w: P.stt("dve", mdst, a[:, 16:16 + n], 1.0 / w,
                                                                      hs[:, 16:16 + n], ALU.mult, ALU.subtract))
                if j == 1:
                    def s_edge(a=a, hs=hs, c=c, g=g):
                        P.tt("dve", T16, a[:, 16:32], pv("pscale", g * 16, 16), ALU.mult)
                        P.tt("dve", ACTB[:, c, PAD + t0:PAD + t0 + 16], T16, hs[:, 16:32], ALU.subtract)
                    steps.append(s_edge)

            def s_carry():
                P.copy("dve", T16, RSX[:, n:n + 16])
                P.copy("dve", RSX[:, 0:16], T16)
                P.copy("dve", XC, X[:, :, t0 + n - 16:t0 + n])
            steps.append(s_carry)
            for g in range(4):
                def s_mm(g=g):
                    for do in range(2):
                        pp = nb()
                        for kc in range(2):
                            P.mm(pp[:, :n], PW[:, g, kc, do * 128:(do + 1) * 128],
                                 ACTB[:, 2 * g + kc, PAD + t0:PAD + t0 + n], kc == 0, kc == 1)
                        co = 2 * g + do
                        P.stt("dve", X[:, co, t0:t0 + n], pp[:, :n], gpool[:, co:co + 1], X[:, co, t0:t0 + n],
                              ALU.mult, ALU.add)
                steps.append(s_mm)
            if with_h2:
                steps += prep_steps(t0, n, DER[:, 1, 3, :], mod_vec(1, 3),
                                    lambda c: ACTB[:, c, PAD + t0:PAD + t0 + n], sq=Hb[0], rs=FT[1])
            return steps

        def pool_post_down(j):
            if j == 0:
                return [pool_setup]
            return pool_tile_steps(j)

        ring_slots[(0, 1)] = ring_dma(0, 1)
        ft_n[0] = 2
        pool_left = []
        if stop_after == "F0":
            ffn(0, list(range(len(TILES))), l1_mod=True, first_prepped=True)
        elif stop_after == "P":
            ffn(0, list(range(len(TILES))), l1_mod=True, first_prepped=True)
            pool_setup()
            for j in range(1, len(TILES)):
                for st_ in pool_tile_steps(j, with_h2=False):
                    st_()
        elif stop_after != "A":
            pool_left = ffn(0, list(range(len(TILES))), l1_mod=True, first_prepped=True, post_down=pool_post_down)

        outs = []

        def final_steps(j):
            t0, n = TILES[j]
            steps = []
            if stop_after is None:
                steps += prep_steps(t0, n, None, None, None, pool_style=True, sq=Hb[0])
                for c in range(NCH):
                    def s_f(c=c):
                        P.stt("dve", X[:, c, t0:t0 + n], X[:, c, t0:t0 + n], pv("fing", c), RS[:, :n], ALU.mult, ALU.mult)
                    steps.append(s_f)

            def s_out():
                outs.append(P.dma("sp", y_d[:, :, t0 - HALO:t0 - HALO + n], X[:, :, t0:t0 + n], "y%d" % j))
            steps.append(s_out)
            return steps

        if stop_after not in ("A", "F0", "P"):
            for st_ in ffn(1, list(range(1, len(TILES))), all_prepped=True, lead_steps=pool_left,
                           post_down=final_steps if OPT_FINAL else None):
                st_()
            if not OPT_FINAL:
                for j in range(1, len(TILES)):
                    for st_ in final_steps(j):
                        st_()
        else:
            for j in range(1, len(TILES)):
                for st_ in final_steps(j):
                    st_()
        P.barrier_wait("sp", outs)
        P.emit()
    return nc


def _chunked(v):
    v = np.asarray(v, np.float32)
    lead = v.shape[:-1]
    n = v.shape[-1] // 128
    return np.moveaxis(v.reshape(lead + (n, 128)), -1, 0)


def _kmajor(w):
    K, N = w.shape
    return np.ascontiguousarray(w.reshape(K // 128, 128, N).transpose(1, 0, 2))


def prepare_inputs(x, c, ada_w, ada_b, norm_mix_g, norm_ffn_g, conv_w1, conv_b1, conv_wdw, conv_bdw,
                   conv_ln_g, conv_ln_b, conv_w2, conv_b2, pool_w, pool_ls, ffn_w_gate, ffn_w_up,
                   ffn_w_down, final_g):
    f = np.float32
    x = np.asarray(x, f)
    B, S, _ = x.shape
    n_cores = 8
    per_seq = S // TOK
    adaw = np.stack([np.stack([_kmajor(np.asarray(ada_w[L], f)[:, b * ADA_BLK:(b + 1) * ADA_BLK]).reshape(128, -1)
                               for b in range(N_ADA_BLK)]) for L in range(2)])
    w1 = np.asarray(conv_w1[0], f)
    blocks = []
    for b in range(4):
        cols = np.concatenate([np.arange((2 * b) * 128, (2 * b + 2) * 128),
                               D + np.arange((2 * b) * 128, (2 * b + 2) * 128)])
        blocks.append(_kmajor(w1[:, cols]).reshape(128, -1))
    w1l = np.stack(blocks)
    w2l = _kmajor(np.asarray(conv_w2[0], f)).reshape(128, -1)

    def piece_major(w, L):
        parts = []
        for (f0, nf) in PIECES_L[L]:
            parts.append(_kmajor(w[:, f0 * 128:(f0 + nf) * 128]).reshape(128, -1))
        return np.concatenate(parts, axis=1)
    wgl = np.stack([piece_major(np.asarray(ffn_w_gate[L], f), L) for L in range(2)])
    wul = np.stack([piece_major(np.asarray(ffn_w_up[L], f), L) for L in range(2)])
    wdl = np.stack([_kmajor(np.asarray(ffn_w_down[L], f)).reshape(128, -1) for L in range(2)])
    pw = np.asarray(pool_w[0], f)
    pwl = np.ascontiguousarray(pw.reshape(4, 2, 128, 256).transpose(2, 0, 1, 3)).reshape(128, -1)

    in_maps = []
    for core in range(n_cores):
        b = core // per_seq
        k = core % per_seq
        start = k * TOK
        xs = np.zeros((NT, D), f)
        if k > 0:
            xs[:] = x[b, start - HALO:start + TOK]
        else:
            xs[HALO:] = x[b, :TOK]
        xl = np.ascontiguousarray(xs.reshape(NT, NCH, 128).transpose(2, 1, 0))
        pvv = np.zeros((128, NPV), f)

        def put(name, arr):
            arr = np.asarray(arr, f).reshape(128, -1)
            pvv[:, PV[name]:PV[name] + arr.shape[1]] = arr
        put("c", _chunked(np.asarray(c, f)[b]))
        put("adab", _chunked(np.asarray(ada_b, f)))
        put("nmg", _chunked(np.asarray(norm_mix_g, f)))
        put("nfg", _chunked(np.asarray(norm_ffn_g, f)))
        put("b1", _chunked(np.asarray(conv_b1[0], f)))
        put("wdw", _chunked(np.asarray(conv_wdw[0], f)))
        put("bdw", _chunked(np.asarray(conv_bdw[0], f)))
        put("lng", _chunked(np.asarray(conv_ln_g[0], f)))
        put("lnb", _chunked(np.asarray(conv_ln_b[0], f)))
        put("b2", _chunked(np.asarray(conv_b2[0], f)))
        put("pls", _chunked(np.asarray(pool_ls[0], f)))
        put("fing", _chunked(np.asarray(final_g, f)))
        pvv[:, PV["mask"]] = 0.0 if k == 0 else 1.0
        psc = np.zeros((4, 16), f)
        for g, w in enumerate(POOL_W):
            for t in range(16):
                psc[g, t] = 1.0 / (min(w, t + 1) if k == 0 else w)
        pvv[:, PV["pscale"]:PV["pscale"] + 64] = psc.reshape(1, 64)
        in_maps.append({"x": xl, "pv": pvv, "adaw": adaw, "w1": w1l, "w2": w2l, "wg": wgl, "wu": wul,
                        "wd": wdl, "pw": pwl})
    return in_maps


def assemble(results, B, S):
    per_seq = S // TOK
    out = np.empty((B, S, D), np.float32)
    for core, r in enumerate(results):
        b = core // per_seq
        k = core % per_seq
        y = r["y"]
        out[b, k * TOK:(k + 1) * TOK] = y.transpose(2, 1, 0).reshape(TOK, D)
    return out


_NC_CACHE = {}
STOP_AFTER = None


def kernel(**inputs):
    x = inputs["x"]
    B, S, _ = x.shape
    in_maps = prepare_inputs(**inputs)
    if STOP_AFTER not in _NC_CACHE:
        _NC_CACHE[STOP_AFTER] = build_nc(STOP_AFTER)
    res = run_bass_kernel_spmd(_NC_CACHE[STOP_AFTER], in_maps, core_ids=list(range(8)))
    return assemble(res.results, B, S)
```

```python
import numpy as np
import concourse.bass as bass
import concourse.mybir as mybir
from concourse.bass_utils import run_bass_kernel_spmd

F32 = mybir.dt.float32
BF16 = mybir.dt.bfloat16
U8 = mybir.dt.uint8
AF = mybir.ActivationFunctionType
ALU = mybir.AluOpType
DSIZE = {F32: 4, BF16: 2, U8: 1}

ENGINES = ("pe", "act", "dve", "pool", "sp")
import os
OPT_A3 = os.environ.get("K_OPT_A3", "1") == "1"
OPT_ADA = os.environ.get("K_OPT_ADA", "1") == "1"
OPT_FINAL = os.environ.get("K_OPT_FINAL", "1") == "1"


def _ap_intervals(ap, cap=64):
    es = DSIZE[ap.dtype]
    pat = ap.ap
    pstep = pat[0][0]
    off = ap.offset
    base = off % pstep if pstep > 0 else off
    dims = [(s, n) for (s, n) in pat[1:] if n > 1]
    dims.sort(key=lambda d: -d[0])
    run = 1
    while dims and dims[-1][0] == run:
        run *= dims[-1][1]
        dims.pop()
    nout = 1
    for _, n in dims:
        nout *= n
    if nout > cap or any(s < run for s, _ in dims):
        ext = run + sum(s * (n - 1) for s, n in dims)
        return ap.tensor.name, [(base * es, (base + ext) * es)]
    starts = [base]
    for s, n in dims:
        starts = [b + s * i for b in starts for i in range(n)]
    return ap.tensor.name, [(b * es, (b + run) * es) for b in starts]


class _Op:
    __slots__ = ("eng", "fn", "idx", "waits", "sig", "semval", "dma_key", "name")

    def __init__(self, eng, fn, dma_key=None, name=""):
        self.eng = eng
        self.fn = fn
        self.idx = -1
        self.waits = []
        self.sig = False
        self.semval = 0
        self.dma_key = dma_key
        self.name = name


class Prog:
    def __init__(self, nc, tracked=("sb", "ps")):
        self.nc = nc
        self.tracked = set(tracked)
        self.ops = {e: [] for e in ENGINES}
        self.wr = {s: [] for s in tracked}
        self.rd = {s: [] for s in tracked}
        self.waited = {e: {} for e in ENGINES}
        self.dma_count = {}
        self.all_ops = []

    PS_BANK = 2048

    def _deps_for(self, ins, outs, eng=None):
        deps = []
        for ap in ins:
            sp, ivs = _ap_intervals(ap)
            if sp not in self.tracked:
                continue
            for lo, hi in ivs:
                for (a, b, op) in self.wr[sp]:
                    if a < hi and lo < b:
                        deps.append(op)
        for ap in outs:
            sp, ivs = _ap_intervals(ap)
            if sp not in self.tracked:
                continue
            for lo, hi in ivs:
                for (a, b, op) in self.wr[sp]:
                    if a < hi and lo < b:
                        deps.append(op)
                for (a, b, op) in self.rd[sp]:
                    if a < hi and lo < b:
                        deps.append(op)
        if "ps" in self.tracked:
            B = self.PS_BANK
            for ap in list(ins) + list(outs):
                sp, ivs = _ap_intervals(ap)
                if sp != "ps":
                    continue
                for lo, hi in ivs:
                    b0, b1 = lo // B, (hi - 1) // B
                    for lst in (self.wr[sp], self.rd[sp]):
                        for (a, b, op) in lst:
                            if op.eng != eng and a // B <= b1 and b0 <= (b - 1) // B:
                                deps.append(op)
        return deps

    @staticmethod
    def _cut(lst, lo, hi):
        out = []
        for (a, b, op) in lst:
            if a < hi and lo < b:
                if a < lo:
                    out.append((a, lo, op))
                if hi < b:
                    out.append((hi, b, op))
            else:
                out.append((a, b, op))
        return out

    def _update(self, op, ins, outs):
        for ap in outs:
            sp, ivs = _ap_intervals(ap)
            if sp not in self.tracked:
                continue
            for lo, hi in ivs:
                self.wr[sp] = self._cut(self.wr[sp], lo, hi)
                self.rd[sp] = self._cut(self.rd[sp], lo, hi)
                self.wr[sp].append((lo, hi, op))
        for ap in ins:
            sp, ivs = _ap_intervals(ap)
            if sp not in self.tracked:
                continue
            for lo, hi in ivs:
                if op.dma_key is None:
                    self.rd[sp] = [
                        (a, b, o) for (a, b, o) in self.rd[sp]
                        if not (o.eng == op.eng and o.dma_key is None and lo <= a and b <= hi)
                    ]
                self.rd[sp].append((lo, hi, op))

    def add(self, eng, fn, ins=(), outs=(), dma_key=None, extra_deps=(), name=""):
        op = _Op(eng, fn, dma_key, name)
        lst = self.ops[eng]
        op.idx = len(lst)
        deps = self._deps_for(ins, outs, eng) + list(extra_deps)
        best = {}
        dma_deps = []
        for d in deps:
            if d is op:
                continue
            if d.dma_key is not None:
                if d not in dma_deps:
                    dma_deps.append(d)
                continue
            if d.eng == eng and eng == "pe":
                continue
            cur = best.get(d.eng)
            if cur is None or d.idx > cur.idx:
                best[d.eng] = d
        w = self.waited[eng]
        for peng, d in best.items():
            if w.get(peng, -1) >= d.idx:
                continue
            w[peng] = d.idx
            d.sig = True
            op.waits.append(d)
        for d in dma_deps:
            key = ("dma", d.dma_key)
            if w.get(key, -1) >= d.semval:
                continue
            w[key] = d.semval
            op.waits.append(d)
        if dma_key is not None:
            n = self.dma_count.get(dma_key, 0) + 1
            self.dma_count[dma_key] = n
            op.semval = 16 * n
        lst.append(op)
        self.all_ops.append(op)
        self._update(op, ins, outs)
        return op

    def mm(self, out, lhsT, rhs, start=True, stop=True, name=""):
        return self.add("pe", lambda e: e.matmul(out, lhsT, rhs, start=start, stop=stop),
                        ins=[lhsT, rhs], outs=[out], name=name)

    def act(self, out, in_, func, bias=None, scale=None, eng="act", name=""):
        ins = [in_]
        kw = {}
        if bias is not None:
            kw["bias"] = bias
            if not isinstance(bias, (int, float)):
                ins.append(bias)
        if scale is not None:
            kw["scale"] = scale
            if not isinstance(scale, (int, float)):
                ins.append(scale)
        return self.add(eng, lambda e: e.activation(out, in_, func, **kw), ins=ins, outs=[out], name=name)

    def tt(self, eng, out, a, b, op, name=""):
        return self.add(eng, lambda e: e.tensor_tensor(out, a, b, op), ins=[a, b], outs=[out], name=name)

    def ts(self, eng, out, a, s1, s2, op0, op1=None, name=""):
        ins = [a] + [s for s in (s1, s2) if s is not None and not isinstance(s, (int, float))]
        if op1 is None:
            return self.add(eng, lambda e: e.tensor_scalar(out, a, s1, None, op0), ins=ins, outs=[out], name=name)
        return self.add(eng, lambda e: e.tensor_scalar(out, a, s1, s2, op0, op1), ins=ins, outs=[out], name=name)

    def stt(self, eng, out, in0, scalar, in1, op0, op1, name=""):
        ins = [in0, in1] + ([] if isinstance(scalar, (int, float)) else [scalar])
        return self.add(eng, lambda e: e.scalar_tensor_tensor(out, in0, scalar, in1, op0, op1),
                        ins=ins, outs=[out], name=name)

    def copy(self, eng, out, in_, name=""):
        if eng == "act":
            return self.add(eng, lambda e: e.copy(out, in_), ins=[in_], outs=[out], name=name)
        return self.add(eng, lambda e: e.tensor_copy(out, in_), ins=[in_], outs=[out], name=name)

    def memset(self, eng, ap, val, name=""):
        return self.add(eng, lambda e: e.memset(ap, val), ins=[], outs=[ap], name=name)

    def dma(self, queue, out, in_, key, name="", after=()):
        return self.add(queue, lambda e: e.dma_start(out=out, in_=in_), ins=[in_], outs=[out],
                        dma_key=key, extra_deps=after, name=name)

    def barrier_wait(self, eng, ops):
        return self.add(eng, None, extra_deps=ops)

    def emit(self):
        nc = self.nc
        for e in ENGINES:
            n = 0
            for op in self.ops[e]:
                if op.dma_key is None and op.sig:
                    n += 1
                    op.semval = n
        import contextlib
        with contextlib.ExitStack() as st:
            esem = {e: st.enter_context(nc.semaphore("s_" + e)) for e in ENGINES}
            dsem = {k: st.enter_context(nc.semaphore("d_%d" % i))
                    for i, k in enumerate(self.dma_count)}
            block = st.enter_context(nc.Block())

            def run(ename, eng):
                for op in self.ops[ename]:
                    for d in op.waits:
                        if d.dma_key is not None:
                            eng.wait_ge(dsem[d.dma_key], d.semval)
                        else:
                            eng.wait_ge(esem[d.eng], d.semval)
                    if op.fn is None:
                        continue
                    ins = op.fn(eng)
                    if op.dma_key is not None:
                        ins.then_inc(dsem[op.dma_key], 16)
                    elif op.sig:
                        ins.then_inc(esem[ename], 1)

            @block.tensor
            def _(e):
                run("pe", e)

            @block.scalar
            def _(e):
                run("act", e)

            @block.vector
            def _(e):
                run("dve", e)

            @block.gpsimd
            def _(e):
                run("pool", e)

            @block.sync
            def _(e):
                run("sp", e)


D = 1024
NCH = 8
FF = 2816
NFC = 22
KW = 31
HALO = 64
TOK = 2048
NT = HALO + TOK
PAD = 32
TILES = [(0, 64)] + [(HALO + 512 * i, 512) for i in range(4)]
PIECES_L = {0: [(0, 3), (3, 3), (6, 4), (10, 4), (14, 4), (18, 4)],
            1: [(0, 4), (4, 4), (8, 4), (12, 4), (16, 3), (19, 3)]}
N_PIECES = 6
NF = 4
EPS = 1e-6
ADA_BLK = 512
N_ADA_BLK = 6 * D // ADA_BLK
POOL_W = (2, 4, 8, 16)

PV = {}
_o = 0
for _n, _w in (("c", 8), ("adab", 96), ("nmg", 16), ("nfg", 16), ("b1", 16), ("wdw", 248),
               ("bdw", 8), ("lng", 8), ("lnb", 8), ("b2", 8), ("pls", 8), ("fing", 8),
               ("mask", 1), ("pscale", 64)):
    PV[_n] = _o
    _o += _w
NPV = _o


def build_nc(stop_after=None):
    nc = bass.Bass("TRN2", target_bir_lowering=False)
    x_d = nc.dram_tensor("x", [128, NCH, NT], F32, kind="ExternalInput").ap()
    pv_d = nc.dram_tensor("pv", [128, NPV], F32, kind="ExternalInput").ap()
    adaw_d = nc.dram_tensor("adaw", [2, N_ADA_BLK, 128, NCH * ADA_BLK], F32, kind="ExternalInput").ap()
    w1_d = nc.dram_tensor("w1", [4, 128, NCH * 512], F32, kind="ExternalInput").ap()
    w2_d = nc.dram_tensor("w2", [128, NCH * D], F32, kind="ExternalInput").ap()
    wg_d = nc.dram_tensor("wg", [2, 128, NCH * FF], F32, kind="ExternalInput").ap()
    wu_d = nc.dram_tensor("wu", [2, 128, NCH * FF], F32, kind="ExternalInput").ap()
    wd_d = nc.dram_tensor("wd", [2, 128, NFC * D], F32, kind="ExternalInput").ap()
    pw_d = nc.dram_tensor("pw", [128, 4 * 2 * 256], F32, kind="ExternalInput").ap()
    y_d = nc.dram_tensor("y", [128, NCH, TOK], F32, kind="ExternalOutput").ap()

    import contextlib
    with contextlib.ExitStack() as st:
        SB_BYTES = 211500
        sb = st.enter_context(nc.sbuf_tensor("sb", [128, SB_BYTES], U8))
        ps = st.enter_context(nc.psum_tensor("ps", [128, 4096], F32))
        P = Prog(nc)
        cur = [0]

        def carve(nbytes):
            a = cur[0]
            cur[0] = a + (nbytes + 63) // 64 * 64
            assert cur[0] <= SB_BYTES, ("SBUF overflow", cur[0])
            return a

        def view(off, shape, dt):
            n = int(np.prod(shape)) * DSIZE[dt]
            a = sb[:, off:off + n].bitcast(dt)
            if len(shape) == 2:
                a = a.rearrange("p (a b) -> p a b", a=shape[0])
            elif len(shape) == 3:
                a = a.rearrange("p (a b c) -> p a b c", a=shape[0], b=shape[1])
            return a

        X = view(carve(NCH * NT * 4), [NCH, NT], F32)
        AW = PAD + NT
        ACTB = view(carve(NCH * AW * 2), [NCH, AW], BF16)
        PVS = view(carve(NPV * 4), [NPV], F32)
        MOD = view(carve(2 * 48 * 4), [2, 48], F32)
        DER = view(carve(2 * 6 * 8 * 4), [2, 6, 8], F32)
        CACT = view(carve(8 * 2), [8], BF16)
        CSIL = view(carve(8 * 4), [8], F32)
        IDENT = view(carve(128 * 2), [128], BF16)
        IDENTF = view(carve(128 * 4), [128], F32)
        ONESD = view(carve(128 * 2), [128], BF16)
        EPSB = view(carve(4), [1], F32)
        SQ_off = carve(NCH * 512 * 2)
        SQ = view(SQ_off, [NCH, 512], BF16)
        Z = [view(SQ_off + i * NF * 512 * 2, [NF, 512], BF16) for i in range(2)]
        H_off = carve(2 * NCH * 512 * 2)
        Hb = [view(H_off + i * NCH * 512 * 2, [NCH, 512], BF16) for i in range(2)]
        DIAG = [view(H_off + i * KW * 128 * 2, [KW, 128], BF16) for i in range(2)]
        FT = [view(carve(528 * 4), [528], F32) for _ in range(6)]
        RING_SLOT = 3 * NCH * NF * 128 * 2
        W_off = carve(2 * RING_SLOT)
        ring = []
        for s in range(2):
            b = W_off + s * RING_SLOT
            ring.append((view(b, [NCH, NF * 128], BF16),
                         view(b + NCH * NF * 128 * 2, [NCH, NF * 128], BF16),
                         view(b + 2 * NCH * NF * 128 * 2, [NF, D], BF16)))
        W1 = view(W_off, [4, NCH, 512], BF16)
        W2 = view(W_off + 4 * NCH * 512 * 2, [NCH, D], BF16)
        ADA_off = carve(2 * NCH * ADA_BLK * 2)
        ADA = [view(ADA_off + i * NCH * ADA_BLK * 2, [NCH, ADA_BLK], BF16) for i in range(2)]
        PW = view(ADA_off, [4, 2, 256], BF16)
        PSB = [ps[:, b * 512:(b + 1) * 512] for b in range(8)]

        def pv(name, i=0, n=1):
            return PVS[:, PV[name] + i: PV[name] + i + n]

        P.dma("sp", PVS, pv_d, "pv")
        for j, (t0, n) in enumerate(TILES[:2]):
            P.dma("sp", X[:, :, t0:t0 + n], x_d[:, :, t0:t0 + n], "x%d" % j)
        P.memset("pool", IDENTF, 0.0)
        P.memset("dve", ONESD, 1.0 / D)
        P.memset("dve", EPSB, EPS)
        P.add("pool", lambda e: e.affine_select(IDENTF, IDENTF, pattern=[[-1, 128]], compare_op=ALU.not_equal,
                                                fill=1.0, base=0, channel_multiplier=1),
              ins=[IDENTF], outs=[IDENTF])
        P.copy("pool", IDENT, IDENTF)
        P.act(CSIL, pv("c", 0, 8), AF.Silu)
        P.copy("dve", CACT, CSIL)

        ada_ctr = [0]

        def ada_dma(L, blk, buf=None, key=None):
            if buf is None:
                slot = ada_ctr[0] % 2
                ada_ctr[0] += 1
                buf, key = ADA[slot], "ada%d" % slot
            P.dma("pool", buf, adaw_d[L, blk].rearrange("p (k c) -> p k c", k=NCH), key)
            return buf

        ada_slots = {}

        def ada_dma_after(L, blk, after):
            slot = ada_ctr[0] % 2
            ada_ctr[0] += 1
            P.dma("pool", ADA[slot], adaw_d[L, blk].rearrange("p (k c) -> p k c", k=NCH), "ada%d" % slot, after=after)
            return ADA[slot]

        def mod_mm(L, blk):
            abuf = ada_slots.pop((L, blk))
            for j in range(4):
                col = 4 * blk + j
                for kc in range(NCH):
                    P.mm(PSB[1][:, L * 64 + col: L * 64 + col + 1], abuf[:, kc, j * 128:(j + 1) * 128],
                         CACT[:, kc:kc + 1], kc == 0, kc == NCH - 1)
            P.tt("dve", MOD[:, L, 4 * blk:4 * blk + 4], PSB[1][:, L * 64 + 4 * blk: L * 64 + 4 * blk + 4],
                 pv("adab", L * 48 + 4 * blk, 4), ALU.add)

        def mod_vec(L, v):
            return MOD[:, L, v * 8:(v + 1) * 8]

        def derive(L, which):
            if which == "mix":
                P.stt("dve", DER[:, L, 0, :], mod_vec(L, 1), 1.0, pv("nmg", L * 8, 8), ALU.add, ALU.mult)
            elif which == "gate_m":
                P.ts("dve", DER[:, L, 1, :], mod_vec(L, 2), 1.0, None, ALU.add)
                if L == 0:
                    P.tt("dve", DER[:, L, 2, :], DER[:, L, 1, :], pv("b2", 0, 8), ALU.mult)
                else:
                    P.tt("dve", DER[:, L, 2, :], DER[:, L, 1, :], pv("pls", 0, 8), ALU.mult)
            elif which == "ffn":
                P.stt("dve", DER[:, L, 3, :], mod_vec(L, 4), 1.0, pv("nfg", L * 8, 8), ALU.add, ALU.mult)
            elif which == "gate_f":
                P.ts("dve", DER[:, L, 4, :], mod_vec(L, 5), 1.0, None, ALU.add)

        ft_ctr = [0]
        ft_n = [4]

        def ft():
            ft_ctr[0] += 1
            return FT[2 + ft_ctr[0] % ft_n[0]]

        RS = FT[0]
        MEAN = FT[1]

        def rsqrt_into(dst, src):
            P.act(dst, src, AF.Ln, bias=EPSB[:, 0:1])
            P.act(dst, dst, AF.Exp, scale=-0.5)

        def prep_steps(t0, n, gm, sh, dst_fn, pool_style=False, rs=None, rs_off=0, sq=None):
            rs_t = RS if rs is None else rs
            SQb = SQ if sq is None else sq
            steps = []

            def s_sq():
                for c in range(NCH):
                    P.act(SQb[:, c, :n], X[:, c, t0:t0 + n], AF.Square)
            steps.append(s_sq)

            def s_stat():
                for c in range(NCH):
                    P.mm(PSB[0][:, :n], ONESD, SQb[:, c, :n], c == 0, c == NCH - 1)
                rsqrt_into(rs_t[:, rs_off:rs_off + n], PSB[0][:, :n])
            steps.append(s_stat)
            if pool_style:
                return steps
            for c in range(NCH):
                def s_h(c=c):
                    t = ft()
                    P.tt("dve", t[:, :n], X[:, c, t0:t0 + n], rs_t[:, rs_off:rs_off + n], ALU.mult)
                    P.act(dst_fn(c), t[:, :n], AF.Identity, bias=sh[:, c:c + 1], scale=gm[:, c:c + 1])
                steps.append(s_h)
            return steps

        def run_interleaved(groups, steps, start=0):
            steps = list(steps)
            ng = len(groups)
            for gi, g in enumerate(groups):
                g()
                if gi >= start and steps:
                    remaining_groups = ng - gi
                    k = -(-len(steps) // remaining_groups)
                    for _ in range(k):
                        if steps:
                            steps.pop(0)()
            for s in steps:
                s()

        for blk in range(2):
            ada_slots[(0, blk)] = ada_dma(0, blk)
        for i, blk in enumerate((2, 3)):
            base = (NCH * NT * 4 + 63) // 64 * 64 + i * NCH * ADA_BLK * 2
            tv = view(base, [NCH, ADA_BLK], BF16)
            ada_slots[(0, blk)] = ada_dma(0, blk, buf=tv, key="adat%d" % i)
        w1_ops = []
        for b in range(4):
            w1_ops.append(P.dma("pool", W1[:, b], w1_d[b].rearrange("p (k c) -> p k c", k=NCH), "w1_%d" % b))
        for j, (t0, n) in enumerate(TILES):
            if j >= 2:
                P.dma("sp", X[:, :, t0:t0 + n], x_d[:, :, t0:t0 + n], "x%d" % j, after=[w1_ops[1]])
        for blk in range(4):
            mod_mm(0, blk)
        P.memset("dve", ACTB[:, :, 0:PAD], 0.0)
        for blk in range(2):
            ada_slots[(0, blk + 4)] = ada_dma(0, blk + 4) if blk else ada_dma_after(0, blk + 4, [w1_ops[3]])
        P.dma("pool", W2, w2_d.rearrange("p (k c) -> p k c", k=NCH), "w2")
        derive(0, "mix")
        next_ada = [4]

        def more_mod0(k):
            for _ in range(k):
                blk = next_ada[0]
                if blk >= N_ADA_BLK:
                    return
                mod_mm(0, blk)
                if blk + 2 < N_ADA_BLK and (0, blk + 2) not in ada_slots:
                    ada_slots[(0, blk + 2)] = ada_dma(0, blk + 2)
                next_ada[0] += 1

        gm0 = DER[:, 0, 0, :]
        sh0 = mod_vec(0, 0)
        U = ACTB

        def a1_prep(j):
            t0, n = TILES[j]
            hb = Hb[j % 2]
            return prep_steps(t0, n, gm0, sh0, lambda c: hb[:, c, :n])

        for s in a1_prep(0):
            s()
        bank_ctr = [0]
        bank_set = [[2, 3, 4, 5, 6, 7]]

        def nb():
            bank_ctr[0] += 1
            bs = bank_set[0]
            return PSB[bs[bank_ctr[0] % len(bs)]]

        for j, (t0, n) in enumerate(TILES):
            hb = Hb[j % 2]
            groups = []
            for oc in range(NCH):
                def g(oc=oc):
                    blk, jj = oc // 2, oc % 2
                    pa, pg = nb(), nb()
                    for kc in range(NCH):
                        P.mm(pa[:, :n], W1[:, blk, kc, jj * 128:(jj + 1) * 128], hb[:, kc, :n], kc == 0, kc == NCH - 1)
                    for kc in range(NCH):
                        P.mm(pg[:, :n], W1[:, blk, kc, 256 + jj * 128:256 + (jj + 1) * 128], hb[:, kc, :n],
                             kc == 0, kc == NCH - 1)
                    sg = ft()
                    P.act(sg[:, :n], pg[:, :n], AF.Sigmoid, bias=pv("b1", 8 + oc))
                    P.stt("dve", U[:, oc, PAD + t0:PAD + t0 + n], pa[:, :n], pv("b1", oc), sg[:, :n], ALU.add, ALU.mult)
                groups.append(g)
            steps = a1_prep(j + 1) if j + 1 < len(TILES) else []
            run_interleaved(groups, steps)
            if j == 0:
                P.ts("dve", U[:, :, PAD:PAD + HALO], U[:, :, PAD:PAD + HALO], pv("mask"), None, ALU.mult)
            more_mod0(2)
        more_mod0(12)
        for blk in range(2):
            ada_slots[(1, blk)] = ada_dma(1, blk)
        derive(0, "gate_m")
        derive(0, "ffn")
        derive(0, "gate_f")

        ring_ctr = [0]

        def ring_dma(L, p):
            f0, nf = PIECES_L[L][p]
            slot = ring_ctr[0] % 2
            ring_ctr[0] += 1
            wg_s, wu_s, wd_s = ring[slot]
            off = NCH * 128 * f0
            P.dma("pool", wg_s[:, :, :nf * 128],
                  wg_d[L, :, off:off + NCH * nf * 128].rearrange("p (k c) -> p k c", k=NCH), "rg%d" % slot)
            P.dma("pool", wu_s[:, :, :nf * 128],
                  wu_d[L, :, off:off + NCH * nf * 128].rearrange("p (k c) -> p k c", k=NCH), "ru%d" % slot)
            P.dma("pool", wd_s[:, :nf, :],
                  wd_d[L, :, f0 * D:(f0 + nf) * D].rearrange("p (f c) -> p f c", f=nf), "rd%d" % slot)
            return slot

        T_D = 4

        def build_diag(c):
            dg = DIAG[c % 2]
            for k in range(T_D, KW):
                P.ts("dve", dg[:, k, :], IDENT, pv("wdw", k * 8 + c), None, ALU.mult)

        build_diag(0)
        ring_slots = {}
        for c in range(NCH):
            if c + 1 < NCH:
                build_diag(c + 1)
            if c == 1:
                ring_slots[(0, 0)] = ring_dma(0, 0)
            dg = DIAG[c % 2]
            for j in reversed(range(len(TILES))):
                t0, n = TILES[j]
                pc = nb()
                for k in range(T_D, KW):
                    P.mm(pc[:, :n], dg[:, k, :], U[:, c, PAD + t0 - 30 + k:PAD + t0 - 30 + k + n], k == T_D, k == KW - 1)
                acc = ft()
                for k in range(T_D):
                    src = U[:, c, PAD + t0 - 30 + k:PAD + t0 - 30 + k + n]
                    if k == 0:
                        P.ts("dve", acc[:, :n], src, pv("wdw", k * 8 + c), None, ALU.mult)
                    else:
                        P.stt("dve", acc[:, :n], src, pv("wdw", k * 8 + c), acc[:, :n], ALU.mult, ALU.add)
                P.stt("dve", U[:, c, PAD + t0:PAD + t0 + n], pc[:, :n], pv("bdw", c), acc[:, :n], ALU.add, ALU.add)
                if OPT_A3:
                    P.act(X[:, c, t0:t0 + n], X[:, c, t0:t0 + n], AF.Identity, bias=DER[:, 0, 2, c:c + 1])

        V = ACTB
        g_m1 = DER[:, 0, 1, :]
        gb2 = DER[:, 0, 2, :]

        MEANb = [FT[1], FT[5]]
        RSb = [FT[0], FT[4]]

        def a3_stats(j):
            t0, n = TILES[j]
            mean_t, rs_t = MEANb[j % 2], RSb[j % 2]
            steps = []

            def s_sq():
                for c in range(NCH):
                    P.act(SQ[:, c, :n], V[:, c, PAD + t0:PAD + t0 + n], AF.Square)
            steps.append(s_sq)

            def s_stat():
                for c in range(NCH):
                    P.mm(PSB[0][:, :n], ONESD, V[:, c, PAD + t0:PAD + t0 + n], c == 0, c == NCH - 1)
                for c in range(NCH):
                    P.mm(PSB[1][:, :n], ONESD, SQ[:, c, :n], c == 0, c == NCH - 1)
                P.copy("act", mean_t[:, :n], PSB[0][:, :n])
                t = ft()
                P.tt("dve", t[:, :n], mean_t[:, :n], mean_t[:, :n], ALU.mult)
                P.tt("dve", t[:, :n], PSB[1][:, :n], t[:, :n], ALU.subtract)
                P.ts("dve", t[:, :n], t[:, :n], 0.0, None, ALU.max)
                rsqrt_into(rs_t[:, :n], t[:, :n])
            steps.append(s_stat)
            return steps

        def a3_ln(j):
            t0, n = TILES[j]
            mean_t, rs_t = MEANb[j % 2], RSb[j % 2]
            sbuf = Hb[j % 2]
            steps = []
            for c in range(NCH):
                def s_ln(c=c):
                    t = ft()
                    P.tt("dve", t[:, :n], V[:, c, PAD + t0:PAD + t0 + n], mean_t[:, :n], ALU.subtract)
                    P.tt("dve", t[:, :n], t[:, :n], rs_t[:, :n], ALU.mult)
                    P.act(sbuf[:, c, :n], t[:, :n], AF.Silu, bias=pv("lnb", c), scale=pv("lng", c))
                steps.append(s_ln)
            return steps

        ft_n[0] = 2
        for s_ in a3_stats(0) + a3_stats(1) + a3_ln(0):
            s_()
        for j, (t0, n) in enumerate(TILES):
            sbuf = Hb[j % 2]
            groups = []
            for do in range(NCH):
                def g(do=do):
                    po = nb()
                    for kc in range(NCH):
                        P.mm(po[:, :n], W2[:, kc, do * 128:(do + 1) * 128], sbuf[:, kc, :n], kc == 0, kc == NCH - 1)
                    P.stt("dve", X[:, do, t0:t0 + n], po[:, :n], g_m1[:, do:do + 1], X[:, do, t0:t0 + n],
                          ALU.mult, ALU.add)
                groups.append(g)
            steps = []
            ln_s = a3_ln(j + 1) if j + 1 < len(TILES) else []
            st_s = a3_stats(j + 2) if j + 2 < len(TILES) else []
            steps += st_s[:1] + ln_s[:3] + st_s[1:] + ln_s[3:]
            if j + 1 == len(TILES) and stop_after != "A":
                t0f, nf0 = TILES[0]
                steps += prep_steps(t0f, nf0, DER[:, 0, 3, :], mod_vec(0, 3),
                                    lambda c: ACTB[:, c, PAD + t0f:PAD + t0f + nf0], sq=SQ, rs=FT[4])
            run_interleaved(groups, steps)
        ft_n[0] = 4
        bank_set[0] = [2, 3, 4, 5]

        H2 = ACTB

        def ffn(L, tiles, l1_mod=False, final=False, first_prepped=False, all_prepped=False, post_down=None, prep_fn=None, lead_steps=None):
            gmf = DER[:, L, 3, :]
            shf = mod_vec(L, 3)
            g_f1 = DER[:, L, 4, :]

            def h2_prep(j):
                t0, n = TILES[j]
                return prep_steps(t0, n, gmf, shf, lambda c: H2[:, c, PAD + t0:PAD + t0 + n], sq=Hb[0])

            if not (first_prepped or all_prepped):
                for s in h2_prep(tiles[0]):
                    s()
            leftover = []
            for p, (f0, nf) in enumerate(PIECES_L[L]):
                if (L, p) not in ring_slots:
                    ring_slots[(L, p)] = ring_dma(L, p)
                slot = ring_slots.pop((L, p))
                wg_s, wu_s, wd_s = ring[slot]
                nxt = None
                if l1_mod:
                    for blk in (2 * p, 2 * p + 1):
                        if (1, blk) not in ada_slots:
                            ada_slots[(1, blk)] = ada_dma(1, blk)
                    for blk in (2 * p, 2 * p + 1):
                        mod_mm(1, blk)
                    for blk in (2 * p + 2, 2 * p + 3):
                        if OPT_ADA and blk < N_ADA_BLK and (1, blk) not in ada_slots:
                            ada_slots[(1, blk)] = ada_dma(1, blk)

                def gu(j, zb):
                    t0, n = TILES[j]
                    groups = []
                    for fi in range(nf):
                        def g(fi=fi):
                            pg, pu = nb(), nb()
                            for kc in range(NCH):
                                P.mm(pg[:, :n], wg_s[:, kc, fi * 128:(fi + 1) * 128], H2[:, kc, PAD + t0:PAD + t0 + n],
                                     kc == 0, kc == NCH - 1)
                            for kc in range(NCH):
                                P.mm(pu[:, :n], wu_s[:, kc, fi * 128:(fi + 1) * 128], H2[:, kc, PAD + t0:PAD + t0 + n],
                                     kc == 0, kc == NCH - 1)
                            sg = ft()
                            P.act(sg[:, :n], pg[:, :n], AF.Silu)
                            P.tt("dve", Z[zb][:, fi, :n], pu[:, :n], sg[:, :n], ALU.mult)
                        groups.append(g)
                    return groups

                def down(j, zb):
                    t0, n = TILES[j]
                    groups = []
                    for do in range(NCH):
                        def g(do=do):
                            pd = PSB[6 + do % 2]
                            for fi in range(nf):
                                P.mm(pd[:, :n], wd_s[:, fi, do * 128:(do + 1) * 128], Z[zb][:, fi, :n],
                                     fi == 0, fi == nf - 1)
                            P.stt("dve", X[:, do, t0:t0 + n], pd[:, :n], g_f1[:, do:do + 1], X[:, do, t0:t0 + n],
                                  ALU.mult, ALU.add)
                        groups.append(g)
                    return groups

                prev = None
                last_piece = (p == N_PIECES - 1) and post_down is not None
                if p >= 1 and not last_piece:
                    bank_set[0] = [0, 2, 3, 4, 5] if l1_mod else [0, 1, 2, 3, 4, 5]
                else:
                    bank_set[0] = [2, 3, 4, 5]
                finished = []
                for ti, j in enumerate(tiles):
                    zb = ti % 2
                    steps = []
                    if p == 0 and ti + 1 < len(tiles) and not all_prepped:
                        steps = (prep_fn or h2_prep)(tiles[ti + 1])
                    if last_piece and finished:
                        steps = steps + post_down(finished.pop(0))
                    if p == 0 and lead_steps:
                        k_ = -(-len(lead_steps) // max(1, len(tiles) - 1 - ti))
                        steps = steps + lead_steps[:k_]
                        del lead_steps[:k_]
                    groups = gu(j, zb)
                    if prev is not None:
                        groups = groups + down(*prev)
                        finished.append(prev[0])
                    run_interleaved(groups, steps)
                    prev = (j, zb)
                dgroups = down(*prev)
                finished.append(prev[0])
                if last_piece:
                    run_interleaved(dgroups, post_down(finished.pop(0)))
                    for jj in finished:
                        leftover.extend(post_down(jj))
                else:
                    for g in dgroups:
                        g()
                nn = None
                if p + 2 < N_PIECES:
                    nn = (L, p + 2)
                elif L == 0:
                    nn = (1, p + 2 - N_PIECES)
                if nn is not None and nn not in ring_slots:
                    ring_slots[nn] = ring_dma(*nn)
                if l1_mod:
                    for blk in (2 * p + 2, 2 * p + 3):
                        if blk < N_ADA_BLK and (1, blk) not in ada_slots:
                            ada_slots[(1, blk)] = ada_dma(1, blk)
            bank_set[0] = [2, 3, 4, 5]
            return leftover

        gm1 = DER[:, 1, 0, :]
        gpool = DER[:, 1, 2, :]
        RSX = FT[0]
        Hb1_off = H_off + NCH * 512 * 2
        HS = [view(Hb1_off + i * 528 * 4, [528], F32) for i in range(2)]
        XC = view(Hb1_off + 2 * 528 * 4, [NCH, 16], F32)
        T16 = view(Hb1_off + 2 * 528 * 4 + NCH * 16 * 4, [16], F32)
        SA = [FT[4], FT[5]]

        def pool_setup():
            derive(1, "mix")
            derive(1, "gate_m")
            derive(1, "ffn")
            derive(1, "gate_f")
            P.dma("pool", PW, pw_d.rearrange("p (g k c) -> p g k c", g=4, k=2), "pw")
            t0, n = TILES[0]
            for s in prep_steps(t0, n, None, None, None, pool_style=True, rs=RSX, rs_off=16, sq=Hb[0]):
                s()
            P.ts("dve", RSX[:, 0:16], RSX[:, 16 + n - 16:16 + n], pv("mask"), None, ALU.mult)
            P.copy("dve", XC, X[:, :, t0 + n - 16:t0 + n])

        def pool_tile_steps(j, with_h2=True):
            t0, n = TILES[j]
            steps = list(prep_steps(t0, n, None, None, None, pool_style=True, rs=RSX, rs_off=16, sq=Hb[0]))
            for c in range(NCH):
                g = c // 2
                w = POOL_W[g]
                hs = HS[c % 2]
                mdst = ACTB[:, c, PAD + t0:PAD + t0 + n]
                steps.append(lambda c=c, hs=hs: P.stt("dve", hs[:, 0:16], XC[:, c, :], gm1[:, c:c + 1], RSX[:, 0:16],
                                                      ALU.mult, ALU.mult))
                steps.append(lambda c=c, hs=hs: P.stt("dve", hs[:, 16:16 + n], X[:, c, t0:t0 + n], gm1[:, c:c + 1],
                                                      RSX[:, 16:16 + n], ALU.mult, ALU.mult))
                a = hs
                s_ = 1
                k = 0
                while s_ < w:
                    b = SA[k % 2]
                    steps.append(lambda a=a, b=b, s_=s_: P.tt("dve", b[:, s_:16 + n], a[:, s_:16 + n],
                                                              a[:, 0:16 + n - s_], ALU.add))
                    a = b
                    s_ *= 2
                    k += 1
                steps.append(lambda a=a, hs=hs, mdst=mdst, w=w: P.stt("dve", mdst, a[:, 16:16 + n], 1.0 / w,
                                                                      hs[:, 16:16 + n], ALU.mult, ALU.subtract))
                if j == 1:
                    def s_edge(a=a, hs=hs, c=c, g=g):
                        P.tt("dve", T16, a[:, 16:32], pv("pscale", g * 16, 16), ALU.mult)
                        P.tt("dve", ACTB[:, c, PAD + t0:PAD + t0 + 16], T16, hs[:, 16:32], ALU.subtract)
                    steps.append(s_edge)

            def s_carry():
                P.copy("dve", T16, RSX[:, n:n + 16])
                P.copy("dve", RSX[:, 0:16], T16)
                P.copy("dve", XC, X[:, :, t0 + n - 16:t0 + n])
            steps.append(s_carry)
            for g in range(4):
                def s_mm(g=g):
                    for do in range(2):
                        pp = nb()
                        for kc in range(2):
                            P.mm(pp[:, :n], PW[:, g, kc, do * 128:(do + 1) * 128],
                                 ACTB[:, 2 * g + kc, PAD + t0:PAD + t0 + n], kc == 0, kc == 1)
                        co = 2 * g + do
                        P.stt("dve", X[:, co, t0:t0 + n], pp[:, :n], gpool[:, co:co + 1], X[:, co, t0:t0 + n],
                              ALU.mult, ALU.add)
                steps.append(s_mm)
            if with_h2:
                steps += prep_steps(t0, n, DER[:, 1, 3, :], mod_vec(1, 3),
                                    lambda c: ACTB[:, c, PAD + t0:PAD + t0 + n], sq=Hb[0], rs=FT[1])
            return steps

        def pool_post_down(j):
            if j == 0:
                return [pool_setup]
            return pool_tile_steps(j)

        ring_slots[(0, 1)] = ring_dma(0, 1)
        ft_n[0] = 2
        pool_left = []
        if stop_after == "F0":
            ffn(0, list(range(len(TILES))), l1_mod=True, first_prepped=True)
        elif stop_after == "P":
            ffn(0, list(range(len(TILES))), l1_mod=True, first_prepped=True)
            pool_setup()
            for j in range(1, len(TILES)):
                for st_ in pool_tile_steps(j, with_h2=False):
                    st_()
        elif stop_after != "A":
            pool_left = ffn(0, list(range(len(TILES))), l1_mod=True, first_prepped=True, post_down=pool_post_down)

        outs = []

        def final_steps(j):
            t0, n = TILES[j]
            steps = []
            if stop_after is None:
                steps += prep_steps(t0, n, None, None, None, pool_style=True, sq=Hb[0])
                for c in range(NCH):
                    def s_f(c=c):
                        P.stt("dve", X[:, c, t0:t0 + n], X[:, c, t0:t0 + n], pv("fing", c), RS[:, :n], ALU.mult, ALU.mult)
                    steps.append(s_f)

            def s_out():
                outs.append(P.dma("sp", y_d[:, :, t0 - HALO:t0 - HALO + n], X[:, :, t0:t0 + n], "y%d" % j))
            steps.append(s_out)
            return steps

        if stop_after not in ("A", "F0", "P"):
            for st_ in ffn(1, list(range(1, len(TILES))), all_prepped=True, lead_steps=pool_left,
                           post_down=final_steps if OPT_FINAL else None):
                st_()
            if not OPT_FINAL:
                for j in range(1, len(TILES)):
                    for st_ in final_steps(j):
                        st_()
        else:
            for j in range(1, len(TILES)):
                for st_ in final_steps(j):
                    st_()
        P.barrier_wait("sp", outs)
        P.emit()
    return nc


def _chunked(v):
    v = np.asarray(v, np.float32)
    lead = v.shape[:-1]
    n = v.shape[-1] // 128
    return np.moveaxis(v.reshape(lead + (n, 128)), -1, 0)


def _kmajor(w):
    K, N = w.shape
    return np.ascontiguousarray(w.reshape(K // 128, 128, N).transpose(1, 0, 2))


def prepare_inputs(x, c, ada_w, ada_b, norm_mix_g, norm_ffn_g, conv_w1, conv_b1, conv_wdw, conv_bdw,
                   conv_ln_g, conv_ln_b, conv_w2, conv_b2, pool_w, pool_ls, ffn_w_gate, ffn_w_up,
                   ffn_w_down, final_g):
    f = np.float32
    x = np.asarray(x, f)
    B, S, _ = x.shape
    n_cores = 8
    per_seq = S // TOK
    adaw = np.stack([np.stack([_kmajor(np.asarray(ada_w[L], f)[:, b * ADA_BLK:(b + 1) * ADA_BLK]).reshape(128, -1)
                               for b in range(N_ADA_BLK)]) for L in range(2)])
    w1 = np.asarray(conv_w1[0], f)
    blocks = []
    for b in range(4):
        cols = np.concatenate([np.arange((2 * b) * 128, (2 * b + 2) * 128),
                               D + np.arange((2 * b) * 128, (2 * b + 2) * 128)])
        blocks.append(_kmajor(w1[:, cols]).reshape(128, -1))
    w1l = np.stack(blocks)
    w2l = _kmajor(np.asarray(conv_w2[0], f)).reshape(128, -1)

    def piece_major(w, L):
        parts = []
        for (f0, nf) in PIECES_L[L]:
            parts.append(_kmajor(w[:, f0 * 128:(f0 + nf) * 128]).reshape(128, -1))
        return np.concatenate(parts, axis=1)
    wgl = np.stack([piece_major(np.asarray(ffn_w_gate[L], f), L) for L in range(2)])
    wul = np.stack([piece_major(np.asarray(ffn_w_up[L], f), L) for L in range(2)])
    wdl = np.stack([_kmajor(np.asarray(ffn_w_down[L], f)).reshape(128, -1) for L in range(2)])
    pw = np.asarray(pool_w[0], f)
    pwl = np.ascontiguousarray(pw.reshape(4, 2, 128, 256).transpose(2, 0, 1, 3)).reshape(128, -1)

    in_maps = []
    for core in range(n_cores):
        b = core // per_seq
        k = core % per_seq
        start = k * TOK
        xs = np.zeros((NT, D), f)
        if k > 0:
            xs[:] = x[b, start - HALO:start + TOK]
        else:
            xs[HALO:] = x[b, :TOK]
        xl = np.ascontiguousarray(xs.reshape(NT, NCH, 128).transpose(2, 1, 0))
        pvv = np.zeros((128, NPV), f)

        def put(name, arr):
            arr = np.asarray(arr, f).reshape(128, -1)
            pvv[:, PV[name]:PV[name] + arr.shape[1]] = arr
        put("c", _chunked(np.asarray(c, f)[b]))
        put("adab", _chunked(np.asarray(ada_b, f)))
        put("nmg", _chunked(np.asarray(norm_mix_g, f)))
        put("nfg", _chunked(np.asarray(norm_ffn_g, f)))
        put("b1", _chunked(np.asarray(conv_b1[0], f)))
        put("wdw", _chunked(np.asarray(conv_wdw[0], f)))
        put("bdw", _chunked(np.asarray(conv_bdw[0], f)))
        put("lng", _chunked(np.asarray(conv_ln_g[0], f)))
        put("lnb", _chunked(np.asarray(conv_ln_b[0], f)))
        put("b2", _chunked(np.asarray(conv_b2[0], f)))
        put("pls", _chunked(np.asarray(pool_ls[0], f)))
        put("fing", _chunked(np.asarray(final_g, f)))
        pvv[:, PV["mask"]] = 0.0 if k == 0 else 1.0
        psc = np.zeros((4, 16), f)
        for g, w in enumerate(POOL_W):
            for t in range(16):
                psc[g, t] = 1.0 / (min(w, t + 1) if k == 0 else w)
        pvv[:, PV["pscale"]:PV["pscale"] + 64] = psc.reshape(1, 64)
        in_maps.append({"x": xl, "pv": pvv, "adaw": adaw, "w1": w1l, "w2": w2l, "wg": wgl, "wu": wul,
                        "wd": wdl, "pw": pwl})
    return in_maps


def assemble(results, B, S):
    per_seq = S // TOK
    out = np.empty((B, S, D), np.float32)
    for core, r in enumerate(results):
        b = core // per_seq
        k = core % per_seq
        y = r["y"]
        out[b, k * TOK:(k + 1) * TOK] = y.transpose(2, 1, 0).reshape(TOK, D)
    return out


_NC_CACHE = {}
STOP_AFTER = None


def kernel(**inputs):
    x = inputs["x"]
    B, S, _ = x.shape
    in_maps = prepare_inputs(**inputs)
    if STOP_AFTER not in _NC_CACHE:
        _NC_CACHE[STOP_AFTER] = build_nc(STOP_AFTER)
    res = run_bass_kernel_spmd(_NC_CACHE[STOP_AFTER], in_maps, core_ids=list(range(8)))
    return assemble(res.results, B, S)
```

```python
import numpy as np
import concourse.bass as bass
import concourse.mybir as mybir
from concourse.bass_utils import run_bass_kernel_spmd

F32 = mybir.dt.float32
BF16 = mybir.dt.bfloat16
U8 = mybir.dt.uint8
AF = mybir.ActivationFunctionType
ALU = mybir.AluOpType
DSIZE = {F32: 4, BF16: 2, U8: 1}

ENGINES = ("pe", "act", "dve", "pool", "sp")
import os
OPT_A3 = os.environ.get("K_OPT_A3", "1") == "1"
OPT_ADA = os.environ.get("K_OPT_ADA", "1") == "1"
OPT_FINAL = os.environ.get("K_OPT_FINAL", "1") == "1"


def _ap_intervals(ap, cap=64):
    es = DSIZE[ap.dtype]
    pat = ap.ap
    pstep = pat[0][0]
    off = ap.offset
    base = off % pstep if pstep > 0 else off
    dims = [(s, n) for (s, n) in pat[1:] if n > 1]
    dims.sort(key=lambda d: -d[0])
    run = 1
    while dims and dims[-1][0] == run:
        run *= dims[-1][1]
        dims.pop()
    nout = 1
    for _, n in dims:
        nout *= n
    if nout > cap or any(s < run for s, _ in dims):
        ext = run + sum(s * (n - 1) for s, n in dims)
        return ap.tensor.name, [(base * es, (base + ext) * es)]
    starts = [base]
    for s, n in dims:
        starts = [b + s * i for b in starts for i in range(n)]
    return ap.tensor.name, [(b * es, (b + run) * es) for b in starts]


class _Op:
    __slots__ = ("eng", "fn", "idx", "waits", "sig", "semval", "dma_key", "name")

    def __init__(self, eng, fn, dma_key=None, name=""):
        self.eng = eng
        self.fn = fn
        self.idx = -1
        self.waits = []
        self.sig = False
        self.semval = 0
        self.dma_key = dma_key
        self.name = name


class Prog:
    def __init__(self, nc, tracked=("sb", "ps")):
        self.nc = nc
        self.tracked = set(tracked)
        self.ops = {e: [] for e in ENGINES}
        self.wr = {s: [] for s in tracked}
        self.rd = {s: [] for s in tracked}
        self.waited = {e: {} for e in ENGINES}
        self.dma_count = {}
        self.all_ops = []

    PS_BANK = 2048

    def _deps_for(self, ins, outs, eng=None):
        deps = []
        for ap in ins:
            sp, ivs = _ap_intervals(ap)
            if sp not in self.tracked:
                continue
            for lo, hi in ivs:
                for (a, b, op) in self.wr[sp]:
                    if a < hi and lo < b:
                        deps.append(op)
        for ap in outs:
            sp, ivs = _ap_intervals(ap)
            if sp not in self.tracked:
                continue
            for lo, hi in ivs:
                for (a, b, op) in self.wr[sp]:
                    if a < hi and lo < b:
                        deps.append(op)
                for (a, b, op) in self.rd[sp]:
                    if a < hi and lo < b:
                        deps.append(op)
        if "ps" in self.tracked:
            B = self.PS_BANK
            for ap in list(ins) + list(outs):
                sp, ivs = _ap_intervals(ap)
                if sp != "ps":
                    continue
                for lo, hi in ivs:
                    b0, b1 = lo // B, (hi - 1) // B
                    for lst in (self.wr[sp], self.rd[sp]):
                        for (a, b, op) in lst:
                            if op.eng != eng and a // B <= b1 and b0 <= (b - 1) // B:
                                deps.append(op)
        return deps

    @staticmethod
    def _cut(lst, lo, hi):
        out = []
        for (a, b, op) in lst:
            if a < hi and lo < b:
                if a < lo:
                    out.append((a, lo, op))
                if hi < b:
                    out.append((hi, b, op))
            else:
                out.append((a, b, op))
        return out

    def _update(self, op, ins, outs):
        for ap in outs:
            sp, ivs = _ap_intervals(ap)
            if sp not in self.tracked:
                continue
            for lo, hi in ivs:
                self.wr[sp] = self._cut(self.wr[sp], lo, hi)
                self.rd[sp] = self._cut(self.rd[sp], lo, hi)
                self.wr[sp].append((lo, hi, op))
        for ap in ins:
            sp, ivs = _ap_intervals(ap)
            if sp not in self.tracked:
                continue
            for lo, hi in ivs:
                if op.dma_key is None:
                    self.rd[sp] = [
                        (a, b, o) for (a, b, o) in self.rd[sp]
                        if not (o.eng == op.eng and o.dma_key is None and lo <= a and b <= hi)
                    ]
                self.rd[sp].append((lo, hi, op))

    def add(self, eng, fn, ins=(), outs=(), dma_key=None, extra_deps=(), name=""):
        op = _Op(eng, fn, dma_key, name)
        lst = self.ops[eng]
        op.idx = len(lst)
        deps = self._deps_for(ins, outs, eng) + list(extra_deps)
        best = {}
        dma_deps = []
        for d in deps:
            if d is op:
                continue
            if d.dma_key is not None:
                if d not in dma_deps:
                    dma_deps.append(d)
                continue
            if d.eng == eng and eng == "pe":
                continue
            cur = best.get(d.eng)
            if cur is None or d.idx > cur.idx:
                best[d.eng] = d
        w = self.waited[eng]
        for peng, d in best.items():
            if w.get(peng, -1) >= d.idx:
                continue
            w[peng] = d.idx
            d.sig = True
            op.waits.append(d)
        for d in dma_deps:
            key = ("dma", d.dma_key)
            if w.get(key, -1) >= d.semval:
                continue
            w[key] = d.semval
            op.waits.append(d)
        if dma_key is not None:
            n = self.dma_count.get(dma_key, 0) + 1
            self.dma_count[dma_key] = n
            op.semval = 16 * n
        lst.append(op)
        self.all_ops.append(op)
        self._update(op, ins, outs)
        return op

    def mm(self, out, lhsT, rhs, start=True, stop=True, name=""):
        return self.add("pe", lambda e: e.matmul(out, lhsT, rhs, start=start, stop=stop),
                        ins=[lhsT, rhs], outs=[out], name=name)

    def act(self, out, in_, func, bias=None, scale=None, eng="act", name=""):
        ins = [in_]
        kw = {}
        if bias is not None:
            kw["bias"] = bias
            if not isinstance(bias, (int, float)):
                ins.append(bias)
        if scale is not None:
            kw["scale"] = scale
            if not isinstance(scale, (int, float)):
                ins.append(scale)
        return self.add(eng, lambda e: e.activation(out, in_, func, **kw), ins=ins, outs=[out], name=name)

    def tt(self, eng, out, a, b, op, name=""):
        return self.add(eng, lambda e: e.tensor_tensor(out, a, b, op), ins=[a, b], outs=[out], name=name)

    def ts(self, eng, out, a, s1, s2, op0, op1=None, name=""):
        ins = [a] + [s for s in (s1, s2) if s is not None and not isinstance(s, (int, float))]
        if op1 is None:
            return self.add(eng, lambda e: e.tensor_scalar(out, a, s1, None, op0), ins=ins, outs=[out], name=name)
        return self.add(eng, lambda e: e.tensor_scalar(out, a, s1, s2, op0, op1), ins=ins, outs=[out], name=name)

    def stt(self, eng, out, in0, scalar, in1, op0, op1, name=""):
        ins = [in0, in1] + ([] if isinstance(scalar, (int, float)) else [scalar])
        return self.add(eng, lambda e: e.scalar_tensor_tensor(out, in0, scalar, in1, op0, op1),
                        ins=ins, outs=[out], name=name)

    def copy(self, eng, out, in_, name=""):
        if eng == "act":
            return self.add(eng, lambda e: e.copy(out, in_), ins=[in_], outs=[out], name=name)
        return self.add(eng, lambda e: e.tensor_copy(out, in_), ins=[in_], outs=[out], name=name)

    def memset(self, eng, ap, val, name=""):
        return self.add(eng, lambda e: e.memset(ap, val), ins=[], outs=[ap], name=name)

    def dma(self, queue, out, in_, key, name="", after=()):
        return self.add(queue, lambda e: e.dma_start(out=out, in_=in_), ins=[in_], outs=[out],
                        dma_key=key, extra_deps=after, name=name)

    def barrier_wait(self, eng, ops):
        return self.add(eng, None, extra_deps=ops)

    def emit(self):
        nc = self.nc
        for e in ENGINES:
            n = 0
            for op in self.ops[e]:
                if op.dma_key is None and op.sig:
                    n += 1
                    op.semval = n
        import contextlib
        with contextlib.ExitStack() as st:
            esem = {e: st.enter_context(nc.semaphore("s_" + e)) for e in ENGINES}
            dsem = {k: st.enter_context(nc.semaphore("d_%d" % i))
                    for i, k in enumerate(self.dma_count)}
            block = st.enter_context(nc.Block())

            def run(ename, eng):
                for op in self.ops[ename]:
                    for d in op.waits:
                        if d.dma_key is not None:
                            eng.wait_ge(dsem[d.dma_key], d.semval)
                        else:
                            eng.wait_ge(esem[d.eng], d.semval)
                    if op.fn is None:
                        continue
                    ins = op.fn(eng)
                    if op.dma_key is not None:
                        ins.then_inc(dsem[op.dma_key], 16)
                    elif op.sig:
                        ins.then_inc(esem[ename], 1)

            @block.tensor
            def _(e):
                run("pe", e)

            @block.scalar
            def _(e):
                run("act", e)

            @block.vector
            def _(e):
                run("dve", e)

            @block.gpsimd
            def _(e):
                run("pool", e)

            @block.sync
            def _(e):
                run("sp", e)


D = 1024
NCH = 8
FF = 2816
NFC = 22
KW = 31
HALO = 64
TOK = 2048
NT = HALO + TOK
PAD = 32
TILES = [(0, 64)] + [(HALO + 512 * i, 512) for i in range(4)]
PIECES_L = {0: [(0, 3), (3, 3), (6, 4), (10, 4), (14, 4), (18, 4)],
            1: [(0, 4), (4, 4), (8, 4), (12, 4), (16, 3), (19, 3)]}
N_PIECES = 6
NF = 4
EPS = 1e-6
ADA_BLK = 512
N_ADA_BLK = 6 * D // ADA_BLK
POOL_W = (2, 4, 8, 16)

PV = {}
_o = 0
for _n, _w in (("c", 8), ("adab", 96), ("nmg", 16), ("nfg", 16), ("b1", 16), ("wdw", 248),
               ("bdw", 8), ("lng", 8), ("lnb", 8), ("b2", 8), ("pls", 8), ("fing", 8),
               ("mask", 1), ("pscale", 64)):
    PV[_n] = _o
    _o += _w
NPV = _o


def build_nc(stop_after=None):
    nc = bass.Bass("TRN2", target_bir_lowering=False)
    x_d = nc.dram_tensor("x", [128, NCH, NT], F32, kind="ExternalInput").ap()
    pv_d = nc.dram_tensor("pv", [128, NPV], F32, kind="ExternalInput").ap()
    adaw_d = nc.dram_tensor("adaw", [2, N_ADA_BLK, 128, NCH * ADA_BLK], F32, kind="ExternalInput").ap()
    w1_d = nc.dram_tensor("w1", [4, 128, NCH * 512], F32, kind="ExternalInput").ap()
    w2_d = nc.dram_tensor("w2", [128, NCH * D], F32, kind="ExternalInput").ap()
    wg_d = nc.dram_tensor("wg", [2, 128, NCH * FF], F32, kind="ExternalInput").ap()
    wu_d = nc.dram_tensor("wu", [2, 128, NCH * FF], F32, kind="ExternalInput").ap()
    wd_d = nc.dram_tensor("wd", [2, 128, NFC * D], F32, kind="ExternalInput").ap()
    pw_d = nc.dram_tensor("pw", [128, 4 * 2 * 256], F32, kind="ExternalInput").ap()
    y_d = nc.dram_tensor("y", [128, NCH, TOK], F32, kind="ExternalOutput").ap()

    import contextlib
    with contextlib.ExitStack() as st:
        SB_BYTES = 211500
        sb = st.enter_context(nc.sbuf_tensor("sb", [128, SB_BYTES], U8))
        ps = st.enter_context(nc.psum_tensor("ps", [128, 4096], F32))
        P = Prog(nc)
        cur = [0]

        def carve(nbytes):
            a = cur[0]
            cur[0] = a + (nbytes + 63) // 64 * 64
            assert cur[0] <= SB_BYTES, ("SBUF overflow", cur[0])
            return a

        def view(off, shape, dt):
            n = int(np.prod(shape)) * DSIZE[dt]
            a = sb[:, off:off + n].bitcast(dt)
            if len(shape) == 2:
                a = a.rearrange("p (a b) -> p a b", a=shape[0])
            elif len(shape) == 3:
                a = a.rearrange("p (a b c) -> p a b c", a=shape[0], b=shape[1])
            return a

        X = view(carve(NCH * NT * 4), [NCH, NT], F32)
        AW = PAD + NT
        ACTB = view(carve(NCH * AW * 2), [NCH, AW], BF16)
        PVS = view(carve(NPV * 4), [NPV], F32)
        MOD = view(carve(2 * 48 * 4), [2, 48], F32)
        DER = view(carve(2 * 6 * 8 * 4), [2, 6, 8], F32)
        CACT = view(carve(8 * 2), [8], BF16)
        CSIL = view(carve(8 * 4), [8], F32)
        IDENT = view(carve(128 * 2), [128], BF16)
        IDENTF = view(carve(128 * 4), [128], F32)
        ONESD = view(carve(128 * 2), [128], BF16)
        EPSB = view(carve(4), [1], F32)
        SQ_off = carve(NCH * 512 * 2)
        SQ = view(SQ_off, [NCH, 512], BF16)
        Z = [view(SQ_off + i * NF * 512 * 2, [NF, 512], BF16) for i in range(2)]
        H_off = carve(2 * NCH * 512 * 2)
        Hb = [view(H_off + i * NCH * 512 * 2, [NCH, 512], BF16) for i in range(2)]
        DIAG = [view(H_off + i * KW * 128 * 2, [KW, 128], BF16) for i in range(2)]
        FT = [view(carve(528 * 4), [528], F32) for _ in range(6)]
        RING_SLOT = 3 * NCH * NF * 128 * 2
        W_off = carve(2 * RING_SLOT)
        ring = []
        for s in range(2):
            b = W_off + s * RING_SLOT
            ring.append((view(b, [NCH, NF * 128], BF16),
                         view(b + NCH * NF * 128 * 2, [NCH, NF * 128], BF16),
                         view(b + 2 * NCH * NF * 128 * 2, [NF, D], BF16)))
        W1 = view(W_off, [4, NCH, 512], BF16)
        W2 = view(W_off + 4 * NCH * 512 * 2, [NCH, D], BF16)
        ADA_off = carve(2 * NCH * ADA_BLK * 2)
        ADA = [view(ADA_off + i * NCH * ADA_BLK * 2, [NCH, ADA_BLK], BF16) for i in range(2)]
        PW = view(ADA_off, [4, 2, 256], BF16)
        PSB = [ps[:, b * 512:(b + 1) * 512] for b in range(8)]

        def pv(name, i=0, n=1):
            return PVS[:, PV[name] + i: PV[name] + i + n]

        P.dma("sp", PVS, pv_d, "pv")
        for j, (t0, n) in enumerate(TILES[:2]):
            P.dma("sp", X[:, :, t0:t0 + n], x_d[:, :, t0:t0 + n], "x%d" % j)
        P.memset("pool", IDENTF, 0.0)
        P.memset("dve", ONESD, 1.0 / D)
        P.memset("dve", EPSB, EPS)
        P.add("pool", lambda e: e.affine_select(IDENTF, IDENTF, pattern=[[-1, 128]], compare_op=ALU.not_equal,
                                                fill=1.0, base=0, channel_multiplier=1),
              ins=[IDENTF], outs=[IDENTF])
        P.copy("pool", IDENT, IDENTF)
        P.act(CSIL, pv("c", 0, 8), AF.Silu)
        P.copy("dve", CACT, CSIL)

        ada_ctr = [0]

        def ada_dma(L, blk, buf=None, key=None):
            if buf is None:
                slot = ada_ctr[0] % 2
                ada_ctr[0] += 1
                buf, key = ADA[slot], "ada%d" % slot
            P.dma("pool", buf, adaw_d[L, blk].rearrange("p (k c) -> p k c", k=NCH), key)
            return buf

        ada_slots = {}

        def ada_dma_after(L, blk, after):
            slot = ada_ctr[0] % 2
            ada_ctr[0] += 1
            P.dma("pool", ADA[slot], adaw_d[L, blk].rearrange("p (k c) -> p k c", k=NCH), "ada%d" % slot, after=after)
            return ADA[slot]

        def mod_mm(L, blk):
            abuf = ada_slots.pop((L, blk))
            for j in range(4):
                col = 4 * blk + j
                for kc in range(NCH):
                    P.mm(PSB[1][:, L * 64 + col: L * 64 + col + 1], abuf[:, kc, j * 128:(j + 1) * 128],
                         CACT[:, kc:kc + 1], kc == 0, kc == NCH - 1)
            P.tt("dve", MOD[:, L, 4 * blk:4 * blk + 4], PSB[1][:, L * 64 + 4 * blk: L * 64 + 4 * blk + 4],
                 pv("adab", L * 48 + 4 * blk, 4), ALU.add)

        def mod_vec(L, v):
            return MOD[:, L, v * 8:(v + 1) * 8]

        def derive(L, which):
            if which == "mix":
                P.stt("dve", DER[:, L, 0, :], mod_vec(L, 1), 1.0, pv("nmg", L * 8, 8), ALU.add, ALU.mult)
            elif which == "gate_m":
                P.ts("dve", DER[:, L, 1, :], mod_vec(L, 2), 1.0, None, ALU.add)
                if L == 0:
                    P.tt("dve", DER[:, L, 2, :], DER[:, L, 1, :], pv("b2", 0, 8), ALU.mult)
                else:
                    P.tt("dve", DER[:, L, 2, :], DER[:, L, 1, :], pv("pls", 0, 8), ALU.mult)
            elif which == "ffn":
                P.stt("dve", DER[:, L, 3, :], mod_vec(L, 4), 1.0, pv("nfg", L * 8, 8), ALU.add, ALU.mult)
            elif which == "gate_f":
                P.ts("dve", DER[:, L, 4, :], mod_vec(L, 5), 1.0, None, ALU.add)

        ft_ctr = [0]
        ft_n = [4]

        def ft():
            ft_ctr[0] += 1
            return FT[2 + ft_ctr[0] % ft_n[0]]

        RS = FT[0]
        MEAN = FT[1]

        def rsqrt_into(dst, src):
            P.act(dst, src, AF.Ln, bias=EPSB[:, 0:1])
            P.act(dst, dst, AF.Exp, scale=-0.5)

        def prep_steps(t0, n, gm, sh, dst_fn, pool_style=False, rs=None, rs_off=0, sq=None):
            rs_t = RS if rs is None else rs
            SQb = SQ if sq is None else sq
            steps = []

            def s_sq():
                for c in range(NCH):
                    P.act(SQb[:, c, :n], X[:, c, t0:t0 + n], AF.Square)
            steps.append(s_sq)

            def s_stat():
                for c in range(NCH):
                    P.mm(PSB[0][:, :n], ONESD, SQb[:, c, :n], c == 0, c == NCH - 1)
                rsqrt_into(rs_t[:, rs_off:rs_off + n], PSB[0][:, :n])
            steps.append(s_stat)
            if pool_style:
                return steps
            for c in range(NCH):
                def s_h(c=c):
                    t = ft()
                    P.tt("dve", t[:, :n], X[:, c, t0:t0 + n], rs_t[:, rs_off:rs_off + n], ALU.mult)
                    P.act(dst_fn(c), t[:, :n], AF.Identity, bias=sh[:, c:c + 1], scale=gm[:, c:c + 1])
                steps.append(s_h)
            return steps

        def run_interleaved(groups, steps, start=0):
            steps = list(steps)
            ng = len(groups)
            for gi, g in enumerate(groups):
                g()
                if gi >= start and steps:
                    remaining_groups = ng - gi
                    k = -(-len(steps) // remaining_groups)
                    for _ in range(k):
                        if steps:
                            steps.pop(0)()
            for s in steps:
                s()

        for blk in range(2):
            ada_slots[(0, blk)] = ada_dma(0, blk)
        for i, blk in enumerate((2, 3)):
            base = (NCH * NT * 4 + 63) // 64 * 64 + i * NCH * ADA_BLK * 2
            tv = view(base, [NCH, ADA_BLK], BF16)
            ada_slots[(0, blk)] = ada_dma(0, blk, buf=tv, key="adat%d" % i)
        w1_ops = []
        for b in range(4):
            w1_ops.append(P.dma("pool", W1[:, b], w1_d[b].rearrange("p (k c) -> p k c", k=NCH), "w1_%d" % b))
        for j, (t0, n) in enumerate(TILES):
            if j >= 2:
                P.dma("sp", X[:, :, t0:t0 + n], x_d[:, :, t0:t0 + n], "x%d" % j, after=[w1_ops[1]])
        for blk in range(4):
            mod_mm(0, blk)
        P.memset("dve", ACTB[:, :, 0:PAD], 0.0)
        for blk in range(2):
            ada_slots[(0, blk + 4)] = ada_dma(0, blk + 4) if blk else ada_dma_after(0, blk + 4, [w1_ops[3]])
        P.dma("pool", W2, w2_d.rearrange("p (k c) -> p k c", k=NCH), "w2")
        derive(0, "mix")
        next_ada = [4]

        def more_mod0(k):
            for _ in range(k):
                blk = next_ada[0]
                if blk >= N_ADA_BLK:
                    return
                mod_mm(0, blk)
                if blk + 2 < N_ADA_BLK and (0, blk + 2) not in ada_slots:
                    ada_slots[(0, blk + 2)] = ada_dma(0, blk + 2)
                next_ada[0] += 1

        gm0 = DER[:, 0, 0, :]
        sh0 = mod_vec(0, 0)
        U = ACTB

        def a1_prep(j):
            t0, n = TILES[j]
            hb = Hb[j % 2]
            return prep_steps(t0, n, gm0, sh0, lambda c: hb[:, c, :n])

        for s in a1_prep(0):
            s()
        bank_ctr = [0]
        bank_set = [[2, 3, 4, 5, 6, 7]]

        def nb():
            bank_ctr[0] += 1
            bs = bank_set[0]
            return PSB[bs[bank_ctr[0] % len(bs)]]

        for j, (t0, n) in enumerate(TILES):
            hb = Hb[j % 2]
            groups = []
            for oc in range(NCH):
                def g(oc=oc):
                    blk, jj = oc // 2, oc % 2
                    pa, pg = nb(), nb()
                    for kc in range(NCH):
                        P.mm(pa[:, :n], W1[:, blk, kc, jj * 128:(jj + 1) * 128], hb[:, kc, :n], kc == 0, kc == NCH - 1)
                    for kc in range(NCH):
                        P.mm(pg[:, :n], W1[:, blk, kc, 256 + jj * 128:256 + (jj + 1) * 128], hb[:, kc, :n],
                             kc == 0, kc == NCH - 1)
                    sg = ft()
                    P.act(sg[:, :n], pg[:, :n], AF.Sigmoid, bias=pv("b1", 8 + oc))
                    P.stt("dve", U[:, oc, PAD + t0:PAD + t0 + n], pa[:, :n], pv("b1", oc), sg[:, :n], ALU.add, ALU.mult)
                groups.append(g)
            steps = a1_prep(j + 1) if j + 1 < len(TILES) else []
            run_interleaved(groups, steps)
            if j == 0:
                P.ts("dve", U[:, :, PAD:PAD + HALO], U[:, :, PAD:PAD + HALO], pv("mask"), None, ALU.mult)
            more_mod0(2)
        more_mod0(12)
        for blk in range(2):
            ada_slots[(1, blk)] = ada_dma(1, blk)
        derive(0, "gate_m")
        derive(0, "ffn")
        derive(0, "gate_f")

        ring_ctr = [0]

        def ring_dma(L, p):
            f0, nf = PIECES_L[L][p]
            slot = ring_ctr[0] % 2
            ring_ctr[0] += 1
            wg_s, wu_s, wd_s = ring[slot]
            off = NCH * 128 * f0
            P.dma("pool", wg_s[:, :, :nf * 128],
                  wg_d[L, :, off:off + NCH * nf * 128].rearrange("p (k c) -> p k c", k=NCH), "rg%d" % slot)
            P.dma("pool", wu_s[:, :, :nf * 128],
                  wu_d[L, :, off:off + NCH * nf * 128].rearrange("p (k c) -> p k c", k=NCH), "ru%d" % slot)
            P.dma("pool", wd_s[:, :nf, :],
                  wd_d[L, :, f0 * D:(f0 + nf) * D].rearrange("p (f c) -> p f c", f=nf), "rd%d" % slot)
            return slot

        T_D = 4

        def build_diag(c):
            dg = DIAG[c % 2]
            for k in range(T_D, KW):
                P.ts("dve", dg[:, k, :], IDENT, pv("wdw", k * 8 + c), None, ALU.mult)

        V = ACTB
        g_m1 = DER[:, 0, 1, :]
        gb2 = DER[:, 0, 2, :]

        MEANb = [FT[1], FT[5]]
        RSb = [FT[0], FT[4]]

        def a3_stats(j):
            t0, n = TILES[j]
            mean_t, rs_t = MEANb[j % 2], RSb[j % 2]
            steps = []

            def s_sq():
                for c in range(NCH):
                    P.act(SQ[:, c, :n], V[:, c, PAD + t0:PAD + t0 + n], AF.Square)
            steps.append(s_sq)

            def s_stat():
                for c in range(NCH):
                    P.mm(PSB[0][:, :n], ONESD, V[:, c, PAD + t0:PAD + t0 + n], c == 0, c == NCH - 1)
                for c in range(NCH):
                    P.mm(PSB[1][:, :n], ONESD, SQ[:, c, :n], c == 0, c == NCH - 1)
                P.copy("act", mean_t[:, :n], PSB[0][:, :n])
                t = ft()
                P.tt("dve", t[:, :n], mean_t[:, :n], mean_t[:, :n], ALU.mult)
                P.tt("dve", t[:, :n], PSB[1][:, :n], t[:, :n], ALU.subtract)
                P.ts("dve", t[:, :n], t[:, :n], 0.0, None, ALU.max)
                rsqrt_into(rs_t[:, :n], t[:, :n])
            steps.append(s_stat)
            return steps

        def a3_ln(j):
            t0, n = TILES[j]
            mean_t, rs_t = MEANb[j % 2], RSb[j % 2]
            sbuf = Hb[j % 2]
            steps = []
            for c in range(NCH):
                def s_ln(c=c):
                    t = ft()
                    P.tt("dve", t[:, :n], V[:, c, PAD + t0:PAD + t0 + n], mean_t[:, :n], ALU.subtract)
                    P.tt("dve", t[:, :n], t[:, :n], rs_t[:, :n], ALU.mult)
                    P.act(sbuf[:, c, :n], t[:, :n], AF.Silu, bias=pv("lnb", c), scale=pv("lng", c))
                steps.append(s_ln)
            return steps

        A3_ORD = [4, 3, 2, 1, 0]

        build_diag(0)
        ring_slots = {}
        for c in range(NCH):
            if c + 1 < NCH:
                build_diag(c + 1)
            if c == 1:
                ring_slots[(0, 0)] = ring_dma(0, 0)
            dg = DIAG[c % 2]
            if c == NCH - 1:
                ft_n[0] = 2
            for j in reversed(range(len(TILES))):
                t0, n = TILES[j]
                pc = nb()
                for k in range(T_D, KW):
                    P.mm(pc[:, :n], dg[:, k, :], U[:, c, PAD + t0 - 30 + k:PAD + t0 - 30 + k + n], k == T_D, k == KW - 1)
                acc = ft()
                for k in range(T_D):
                    src = U[:, c, PAD + t0 - 30 + k:PAD + t0 - 30 + k + n]
                    if k == 0:
                        P.ts("dve", acc[:, :n], src, pv("wdw", k * 8 + c), None, ALU.mult)
                    else:
                        P.stt("dve", acc[:, :n], src, pv("wdw", k * 8 + c), acc[:, :n], ALU.mult, ALU.add)
                P.stt("dve", U[:, c, PAD + t0:PAD + t0 + n], pc[:, :n], pv("bdw", c), acc[:, :n], ALU.add, ALU.add)
                if OPT_A3:
                    P.act(X[:, c, t0:t0 + n], X[:, c, t0:t0 + n], AF.Identity, bias=DER[:, 0, 2, c:c + 1])
                if c == NCH - 1:
                    if j == A3_ORD[0]:
                        for s_ in a3_stats(A3_ORD[0]):
                            s_()
                    elif j == A3_ORD[1]:
                        for s_ in a3_stats(A3_ORD[1]) + a3_ln(A3_ORD[0]):
                            s_()

        for idx, j in enumerate(A3_ORD):
            t0, n = TILES[j]
            sbuf = Hb[j % 2]
            groups = []
            for do in range(NCH):
                def g(do=do, t0=t0, n=n, sbuf=sbuf):
                    po = nb()
                    for kc in range(NCH):
                        P.mm(po[:, :n], W2[:, kc, do * 128:(do + 1) * 128], sbuf[:, kc, :n], kc == 0, kc == NCH - 1)
                    P.stt("dve", X[:, do, t0:t0 + n], po[:, :n], g_m1[:, do:do + 1], X[:, do, t0:t0 + n],
                          ALU.mult, ALU.add)
                groups.append(g)
            steps = []
            ln_s = a3_ln(A3_ORD[idx + 1]) if idx + 1 < len(A3_ORD) else []
            st_s = a3_stats(A3_ORD[idx + 2]) if idx + 2 < len(A3_ORD) else []
            steps += st_s[:1] + ln_s[:3] + st_s[1:] + ln_s[3:]
            if idx + 1 == len(A3_ORD) and stop_after != "A":
                t0f, nf0 = TILES[A3_ORD[0]]
                steps += prep_steps(t0f, nf0, DER[:, 0, 3, :], mod_vec(0, 3),
                                    lambda c: ACTB[:, c, PAD + t0f:PAD + t0f + nf0], sq=SQ, rs=FT[4])
            run_interleaved(groups, steps)
        ft_n[0] = 4
        bank_set[0] = [2, 3, 4, 5]

        H2 = ACTB

        def ffn(L, tiles, l1_mod=False, final=False, first_prepped=False, all_prepped=False, post_down=None, prep_fn=None, lead_steps=None, tiles_last=None):
            gmf = DER[:, L, 3, :]
            shf = mod_vec(L, 3)
            g_f1 = DER[:, L, 4, :]

            def h2_prep(j):
                t0, n = TILES[j]
                return prep_steps(t0, n, gmf, shf, lambda c: H2[:, c, PAD + t0:PAD + t0 + n], sq=Hb[0])

            if not (first_prepped or all_prepped):
                for s in h2_prep(tiles[0]):
                    s()
            leftover = []
            for p, (f0, nf) in enumerate(PIECES_L[L]):
                if (L, p) not in ring_slots:
                    ring_slots[(L, p)] = ring_dma(L, p)
                slot = ring_slots.pop((L, p))
                wg_s, wu_s, wd_s = ring[slot]
                nxt = None
                if l1_mod:
                    for blk in (2 * p, 2 * p + 1):
                        if (1, blk) not in ada_slots:
                            ada_slots[(1, blk)] = ada_dma(1, blk)
                    for blk in (2 * p, 2 * p + 1):
                        mod_mm(1, blk)
                    for blk in (2 * p + 2, 2 * p + 3):
                        if OPT_ADA and blk < N_ADA_BLK and (1, blk) not in ada_slots:
                            ada_slots[(1, blk)] = ada_dma(1, blk)

                def gu(j, zb):
                    t0, n = TILES[j]
                    groups = []
                    for fi in range(nf):
                        def g(fi=fi):
                            pg, pu = nb(), nb()
                            for kc in range(NCH):
                                P.mm(pg[:, :n], wg_s[:, kc, fi * 128:(fi + 1) * 128], H2[:, kc, PAD + t0:PAD + t0 + n],
                                     kc == 0, kc == NCH - 1)
                            for kc in range(NCH):
                                P.mm(pu[:, :n], wu_s[:, kc, fi * 128:(fi + 1) * 128], H2[:, kc, PAD + t0:PAD + t0 + n],
                                     kc == 0, kc == NCH - 1)
                            sg = ft()
                            P.act(sg[:, :n], pg[:, :n], AF.Silu)
                            P.tt("dve", Z[zb][:, fi, :n], pu[:, :n], sg[:, :n], ALU.mult)
                        groups.append(g)
                    return groups

                def down(j, zb):
                    t0, n = TILES[j]
                    groups = []
                    for do in range(NCH):
                        def g(do=do):
                            pd = PSB[6 + do % 2]
                            for fi in range(nf):
                                P.mm(pd[:, :n], wd_s[:, fi, do * 128:(do + 1) * 128], Z[zb][:, fi, :n],
                                     fi == 0, fi == nf - 1)
                            P.stt("dve", X[:, do, t0:t0 + n], pd[:, :n], g_f1[:, do:do + 1], X[:, do, t0:t0 + n],
                                  ALU.mult, ALU.add)
                        groups.append(g)
                    return groups

                prev = None
                last_piece = (p == N_PIECES - 1) and post_down is not None
                tiles_all = tiles
                if p == N_PIECES - 1 and tiles_last is not None:
                    tiles = tiles_last
                if p >= 1 and not last_piece:
                    bank_set[0] = [0, 2, 3, 4, 5] if l1_mod else [0, 1, 2, 3, 4, 5]
                else:
                    bank_set[0] = [2, 3, 4, 5]
                finished = []
                for ti, j in enumerate(tiles):
                    zb = ti % 2
                    steps = []
                    if p == 0 and ti + 1 < len(tiles) and not all_prepped:
                        steps = (prep_fn or h2_prep)(tiles[ti + 1])
                    if last_piece and finished:
                        steps = steps + post_down(finished.pop(0))
                    if p == 0 and lead_steps:
                        k_ = -(-len(lead_steps) // max(1, len(tiles) - 1 - ti))
                        steps = steps + lead_steps[:k_]
                        del lead_steps[:k_]
                    groups = gu(j, zb)
                    if prev is not None:
                        groups = groups + down(*prev)
                        finished.append(prev[0])
                    run_interleaved(groups, steps)
                    prev = (j, zb)
                dgroups = down(*prev)
                finished.append(prev[0])
                if last_piece:
                    run_interleaved(dgroups, post_down(finished.pop(0)))
                    for jj in finished:
                        leftover.extend(post_down(jj))
                else:
                    for g in dgroups:
                        g()
                tiles = tiles_all
                nn = None
                if p + 2 < N_PIECES:
                    nn = (L, p + 2)
                elif L == 0:
                    nn = (1, p + 2 - N_PIECES)
                if nn is not None and nn not in ring_slots:
                    ring_slots[nn] = ring_dma(*nn)
                if l1_mod:
                    for blk in (2 * p + 2, 2 * p + 3):
                        if blk < N_ADA_BLK and (1, blk) not in ada_slots:
                            ada_slots[(1, blk)] = ada_dma(1, blk)
            bank_set[0] = [2, 3, 4, 5]
            return leftover

        gm1 = DER[:, 1, 0, :]
        gpool = DER[:, 1, 2, :]
        RSX = FT[0]
        Hb1_off = H_off + NCH * 512 * 2
        HS = [view(Hb1_off + i * 528 * 4, [528], F32) for i in range(2)]
        XC = view(Hb1_off + 2 * 528 * 4, [NCH, 16], F32)
        T16 = view(Hb1_off + 2 * 528 * 4 + NCH * 16 * 4, [16], F32)
        SA = [FT[4], FT[5]]

        def pool_setup():
            derive(1, "mix")
            derive(1, "gate_m")
            derive(1, "ffn")
            derive(1, "gate_f")
            P.dma("pool", PW, pw_d.rearrange("p (g k c) -> p g k c", g=4, k=2), "pw")
            t0, n = TILES[0]
            for s in prep_steps(t0, n, None, None, None, pool_style=True, rs=RSX, rs_off=16, sq=Hb[0]):
                s()
            P.ts("dve", RSX[:, 0:16], RSX[:, 16 + n - 16:16 + n], pv("mask"), None, ALU.mult)
            P.copy("dve", XC, X[:, :, t0 + n - 16:t0 + n])

        def pool_tile_steps(j, with_h2=True):
            t0, n = TILES[j]
            steps = list(prep_steps(t0, n, None, None, None, pool_style=True, rs=RSX, rs_off=16, sq=Hb[0]))
            for c in range(NCH):
                g = c // 2
                w = POOL_W[g]
                hs = HS[c % 2]
                mdst = ACTB[:, c, PAD + t0:PAD + t0 + n]
                steps.append(lambda c=c, hs=hs: P.stt("dve", hs[:, 0:16], XC[:, c, :], gm1[:, c:c + 1], RSX[:, 0:16],
                                                      ALU.mult, ALU.mult))
                steps.append(lambda c=c, hs=hs: P.stt("dve", hs[:, 16:16 + n], X[:, c, t0:t0 + n], gm1[:, c:c + 1],
                                                      RSX[:, 16:16 + n], ALU.mult, ALU.mult))
                a = hs
                s_ = 1
                k = 0
                while s_ < w:
                    b = SA[k % 2]
                    steps.append(lambda a=a, b=b, s_=s_: P.tt("dve", b[:, s_:16 + n], a[:, s_:16 + n],
                                                              a[:, 0:16 + n - s_], ALU.add))
                    a = b
                    s_ *= 2
                    k += 1
                steps.append(lambda a=a, hs=hs, mdst=mdst, w=w: P.stt("dve", mdst, a[:, 16:16 + n], 1.0 / w,
                                                                      hs[:, 16:16 + n], ALU.mult, ALU.subtract))
                if j == 1:
                    def s_edge(a=a, hs=hs, c=c, g=g):
                        P.tt("dve", T16, a[:, 16:32], pv("pscale", g * 16, 16), ALU.mult)
                        P.tt("dve", ACTB[:, c, PAD + t0:PAD + t0 + 16], T16, hs[:, 16:32], ALU.subtract)
                    steps.append(s_edge)

            def s_carry():
                P.copy("dve", T16, RSX[:, n:n + 16])
                P.copy("dve", RSX[:, 0:16], T16)
                P.copy("dve", XC, X[:, :, t0 + n - 16:t0 + n])
            steps.append(s_carry)
            for g in range(4):
                def s_mm(g=g):
                    for do in range(2):
                        pp = nb()
                        for kc in range(2):
                            P.mm(pp[:, :n], PW[:, g, kc, do * 128:(do + 1) * 128],
                                 ACTB[:, 2 * g + kc, PAD + t0:PAD + t0 + n], kc == 0, kc == 1)
                        co = 2 * g + do
                        P.stt("dve", X[:, co, t0:t0 + n], pp[:, :n], gpool[:, co:co + 1], X[:, co, t0:t0 + n],
                              ALU.mult, ALU.add)
                steps.append(s_mm)
            if with_h2:
                steps += prep_steps(t0, n, DER[:, 1, 3, :], mod_vec(1, 3),
                                    lambda c: ACTB[:, c, PAD + t0:PAD + t0 + n], sq=Hb[0], rs=FT[1])
            return steps

        def pool_post_down(j):
            if j == 0:
                return [pool_setup]
            return pool_tile_steps(j)

        ring_slots[(0, 1)] = ring_dma(0, 1)
        ft_n[0] = 2
        pool_left = []
        if stop_after == "F0":
            ffn(0, list(A3_ORD), l1_mod=True, first_prepped=True)
        elif stop_after == "P":
            ffn(0, list(A3_ORD), l1_mod=True, first_prepped=True)
            pool_setup()
            for j in range(1, len(TILES)):
                for st_ in pool_tile_steps(j, with_h2=False):
                    st_()
        elif stop_after != "A":
            pool_left = ffn(0, list(A3_ORD), l1_mod=True, first_prepped=True, post_down=pool_post_down,
                            tiles_last=list(range(len(TILES))))

        outs = []

        def final_steps(j):
            t0, n = TILES[j]
            steps = []
            if stop_after is None:
                steps += prep_steps(t0, n, None, None, None, pool_style=True, sq=Hb[0])
                for c in range(NCH):
                    def s_f(c=c):
                        P.stt("dve", X[:, c, t0:t0 + n], X[:, c, t0:t0 + n], pv("fing", c), RS[:, :n], ALU.mult, ALU.mult)
                    steps.append(s_f)

            def s_out():
                outs.append(P.dma("sp", y_d[:, :, t0 - HALO:t0 - HALO + n], X[:, :, t0:t0 + n], "y%d" % j))
            steps.append(s_out)
            return steps

        if stop_after not in ("A", "F0", "P"):
            for st_ in ffn(1, list(range(1, len(TILES))), all_prepped=True, lead_steps=pool_left,
                           post_down=final_steps if OPT_FINAL else None):
                st_()
            if not OPT_FINAL:
                for j in range(1, len(TILES)):
                    for st_ in final_steps(j):
                        st_()
        else:
            for j in range(1, len(TILES)):
                for st_ in final_steps(j):
                    st_()
        P.barrier_wait("sp", outs)
        P.emit()
    return nc


def _chunked(v):
    v = np.asarray(v, np.float32)
    lead = v.shape[:-1]
    n = v.shape[-1] // 128
    return np.moveaxis(v.reshape(lead + (n, 128)), -1, 0)


def _kmajor(w):
    K, N = w.shape
    return np.ascontiguousarray(w.reshape(K // 128, 128, N).transpose(1, 0, 2))


def prepare_inputs(x, c, ada_w, ada_b, norm_mix_g, norm_ffn_g, conv_w1, conv_b1, conv_wdw, conv_bdw,
                   conv_ln_g, conv_ln_b, conv_w2, conv_b2, pool_w, pool_ls, ffn_w_gate, ffn_w_up,
                   ffn_w_down, final_g):
    f = np.float32
    x = np.asarray(x, f)
    B, S, _ = x.shape
    n_cores = 8
    per_seq = S // TOK
    adaw = np.stack([np.stack([_kmajor(np.asarray(ada_w[L], f)[:, b * ADA_BLK:(b + 1) * ADA_BLK]).reshape(128, -1)
                               for b in range(N_ADA_BLK)]) for L in range(2)])
    w1 = np.asarray(conv_w1[0], f)
    blocks = []
    for b in range(4):
        cols = np.concatenate([np.arange((2 * b) * 128, (2 * b + 2) * 128),
                               D + np.arange((2 * b) * 128, (2 * b + 2) * 128)])
        blocks.append(_kmajor(w1[:, cols]).reshape(128, -1))
    w1l = np.stack(blocks)
    w2l = _kmajor(np.asarray(conv_w2[0], f)).reshape(128, -1)

    def piece_major(w, L):
        parts = []
        for (f0, nf) in PIECES_L[L]:
            parts.append(_kmajor(w[:, f0 * 128:(f0 + nf) * 128]).reshape(128, -1))
        return np.concatenate(parts, axis=1)
    wgl = np.stack([piece_major(np.asarray(ffn_w_gate[L], f), L) for L in range(2)])
    wul = np.stack([piece_major(np.asarray(ffn_w_up[L], f), L) for L in range(2)])
    wdl = np.stack([_kmajor(np.asarray(ffn_w_down[L], f)).reshape(128, -1) for L in range(2)])
    pw = np.asarray(pool_w[0], f)
    pwl = np.ascontiguousarray(pw.reshape(4, 2, 128, 256).transpose(2, 0, 1, 3)).reshape(128, -1)

    in_maps = []
    for core in range(n_cores):
        b = core // per_seq
        k = core % per_seq
        start = k * TOK
        xs = np.zeros((NT, D), f)
        if k > 0:
            xs[:] = x[b, start - HALO:start + TOK]
        else:
            xs[HALO:] = x[b, :TOK]
        xl = np.ascontiguousarray(xs.reshape(NT, NCH, 128).transpose(2, 1, 0))
        pvv = np.zeros((128, NPV), f)

        def put(name, arr):
            arr = np.asarray(arr, f).reshape(128, -1)
            pvv[:, PV[name]:PV[name] + arr.shape[1]] = arr
        put("c", _chunked(np.asarray(c, f)[b]))
        put("adab", _chunked(np.asarray(ada_b, f)))
        put("nmg", _chunked(np.asarray(norm_mix_g, f)))
        put("nfg", _chunked(np.asarray(norm_ffn_g, f)))
        put("b1", _chunked(np.asarray(conv_b1[0], f)))
        put("wdw", _chunked(np.asarray(conv_wdw[0], f)))
        put("bdw", _chunked(np.asarray(conv_bdw[0], f)))
        put("lng", _chunked(np.asarray(conv_ln_g[0], f)))
        put("lnb", _chunked(np.asarray(conv_ln_b[0], f)))
        put("b2", _chunked(np.asarray(conv_b2[0], f)))
        put("pls", _chunked(np.asarray(pool_ls[0], f)))
        put("fing", _chunked(np.asarray(final_g, f)))
        pvv[:, PV["mask"]] = 0.0 if k == 0 else 1.0
        psc = np.zeros((4, 16), f)
        for g, w in enumerate(POOL_W):
            for t in range(16):
                psc[g, t] = 1.0 / (min(w, t + 1) if k == 0 else w)
        pvv[:, PV["pscale"]:PV["pscale"] + 64] = psc.reshape(1, 64)
        in_maps.append({"x": xl, "pv": pvv, "adaw": adaw, "w1": w1l, "w2": w2l, "wg": wgl, "wu": wul,
                        "wd": wdl, "pw": pwl})
    return in_maps


def assemble(results, B, S):
    per_seq = S // TOK
    out = np.empty((B, S, D), np.float32)
    for core, r in enumerate(results):
        b = core // per_seq
        k = core % per_seq
        y = r["y"]
        out[b, k * TOK:(k + 1) * TOK] = y.transpose(2, 1, 0).reshape(TOK, D)
    return out


_NC_CACHE = {}
STOP_AFTER = None


def kernel(**inputs):
    x = inputs["x"]
    B, S, _ = x.shape
    in_maps = prepare_inputs(**inputs)
    if STOP_AFTER not in _NC_CACHE:
        _NC_CACHE[STOP_AFTER] = build_nc(STOP_AFTER)
    res = run_bass_kernel_spmd(_NC_CACHE[STOP_AFTER], in_maps, core_ids=list(range(8)))
    return assemble(res.results, B, S)
```

```python
import numpy as np
import concourse.bass as bass
import concourse.mybir as mybir
from concourse.bass_utils import run_bass_kernel_spmd

F32 = mybir.dt.float32
BF16 = mybir.dt.bfloat16
U8 = mybir.dt.uint8
AF = mybir.ActivationFunctionType
ALU = mybir.AluOpType
DSIZE = {F32: 4, BF16: 2, U8: 1}

ENGINES = ("pe", "act", "dve", "pool", "sp")
import os
OPT_A3 = os.environ.get("K_OPT_A3", "1") == "1"
OPT_ADA = os.environ.get("K_OPT_ADA", "1") == "1"
OPT_FINAL = os.environ.get("K_OPT_FINAL", "1") == "1"
OPT_CHST = os.environ.get("K_OPT_CHST", "0") == "1"


def _ap_intervals(ap, cap=64):
    es = DSIZE[ap.dtype]
    pat = ap.ap
    pstep = pat[0][0]
    off = ap.offset
    base = off % pstep if pstep > 0 else off
    dims = [(s, n) for (s, n) in pat[1:] if n > 1]
    dims.sort(key=lambda d: -d[0])
    run = 1
    while dims and dims[-1][0] == run:
        run *= dims[-1][1]
        dims.pop()
    nout = 1
    for _, n in dims:
        nout *= n
    if nout > cap or any(s < run for s, _ in dims):
        ext = run + sum(s * (n - 1) for s, n in dims)
        return ap.tensor.name, [(base * es, (base + ext) * es)]
    starts = [base]
    for s, n in dims:
        starts = [b + s * i for b in starts for i in range(n)]
    return ap.tensor.name, [(b * es, (b + run) * es) for b in starts]


class _Op:
    __slots__ = ("eng", "fn", "idx", "waits", "sig", "semval", "dma_key", "name")

    def __init__(self, eng, fn, dma_key=None, name=""):
        self.eng = eng
        self.fn = fn
        self.idx = -1
        self.waits = []
        self.sig = False
        self.semval = 0
        self.dma_key = dma_key
        self.name = name


class Prog:
    def __init__(self, nc, tracked=("sb", "ps")):
        self.nc = nc
        self.tracked = set(tracked)
        self.ops = {e: [] for e in ENGINES}
        self.wr = {s: [] for s in tracked}
        self.rd = {s: [] for s in tracked}
        self.waited = {e: {} for e in ENGINES}
        self.dma_count = {}
        self.all_ops = []

    PS_BANK = 2048

    def _deps_for(self, ins, outs, eng=None):
        deps = []
        for ap in ins:
            sp, ivs = _ap_intervals(ap)
            if sp not in self.tracked:
                continue
            for lo, hi in ivs:
                for (a, b, op) in self.wr[sp]:
                    if a < hi and lo < b:
                        deps.append(op)
        for ap in outs:
            sp, ivs = _ap_intervals(ap)
            if sp not in self.tracked:
                continue
            for lo, hi in ivs:
                for (a, b, op) in self.wr[sp]:
                    if a < hi and lo < b:
                        deps.append(op)
                for (a, b, op) in self.rd[sp]:
                    if a < hi and lo < b:
                        deps.append(op)
        if "ps" in self.tracked:
            B = self.PS_BANK
            for ap in list(ins) + list(outs):
                sp, ivs = _ap_intervals(ap)
                if sp != "ps":
                    continue
                for lo, hi in ivs:
                    b0, b1 = lo // B, (hi - 1) // B
                    for lst in (self.wr[sp], self.rd[sp]):
                        for (a, b, op) in lst:
                            if op.eng != eng and a // B <= b1 and b0 <= (b - 1) // B:
                                deps.append(op)
        return deps

    @staticmethod
    def _cut(lst, lo, hi):
        out = []
        for (a, b, op) in lst:
            if a < hi and lo < b:
                if a < lo:
                    out.append((a, lo, op))
                if hi < b:
                    out.append((hi, b, op))
            else:
                out.append((a, b, op))
        return out

    def _update(self, op, ins, outs):
        for ap in outs:
            sp, ivs = _ap_intervals(ap)
            if sp not in self.tracked:
                continue
            for lo, hi in ivs:
                self.wr[sp] = self._cut(self.wr[sp], lo, hi)
                self.rd[sp] = self._cut(self.rd[sp], lo, hi)
                self.wr[sp].append((lo, hi, op))
        for ap in ins:
            sp, ivs = _ap_intervals(ap)
            if sp not in self.tracked:
                continue
            for lo, hi in ivs:
                if op.dma_key is None:
                    self.rd[sp] = [
                        (a, b, o) for (a, b, o) in self.rd[sp]
                        if not (o.eng == op.eng and o.dma_key is None and lo <= a and b <= hi)
                    ]
                self.rd[sp].append((lo, hi, op))

    def add(self, eng, fn, ins=(), outs=(), dma_key=None, extra_deps=(), name=""):
        op = _Op(eng, fn, dma_key, name)
        lst = self.ops[eng]
        op.idx = len(lst)
        deps = self._deps_for(ins, outs, eng) + list(extra_deps)
        best = {}
        dma_deps = []
        for d in deps:
            if d is op:
                continue
            if d.dma_key is not None:
                if d not in dma_deps:
                    dma_deps.append(d)
                continue
            if d.eng == eng and eng == "pe":
                continue
            cur = best.get(d.eng)
            if cur is None or d.idx > cur.idx:
                best[d.eng] = d
        w = self.waited[eng]
        for peng, d in best.items():
            if w.get(peng, -1) >= d.idx:
                continue
            w[peng] = d.idx
            d.sig = True
            op.waits.append(d)
        for d in dma_deps:
            key = ("dma", d.dma_key)
            if w.get(key, -1) >= d.semval:
                continue
            w[key] = d.semval
            op.waits.append(d)
        if dma_key is not None:
            n = self.dma_count.get(dma_key, 0) + 1
            self.dma_count[dma_key] = n
            op.semval = 16 * n
        lst.append(op)
        self.all_ops.append(op)
        self._update(op, ins, outs)
        return op

    def mm(self, out, lhsT, rhs, start=True, stop=True, name=""):
        return self.add("pe", lambda e: e.matmul(out, lhsT, rhs, start=start, stop=stop),
                        ins=[lhsT, rhs], outs=[out], name=name)

    def act(self, out, in_, func, bias=None, scale=None, eng="act", name=""):
        ins = [in_]
        kw = {}
        if bias is not None:
            kw["bias"] = bias
            if not isinstance(bias, (int, float)):
                ins.append(bias)
        if scale is not None:
            kw["scale"] = scale
            if not isinstance(scale, (int, float)):
                ins.append(scale)
        return self.add(eng, lambda e: e.activation(out, in_, func, **kw), ins=ins, outs=[out], name=name)

    def tt(self, eng, out, a, b, op, name=""):
        return self.add(eng, lambda e: e.tensor_tensor(out, a, b, op), ins=[a, b], outs=[out], name=name)

    def ts(self, eng, out, a, s1, s2, op0, op1=None, name=""):
        ins = [a] + [s for s in (s1, s2) if s is not None and not isinstance(s, (int, float))]
        if op1 is None:
            return self.add(eng, lambda e: e.tensor_scalar(out, a, s1, None, op0), ins=ins, outs=[out], name=name)
        return self.add(eng, lambda e: e.tensor_scalar(out, a, s1, s2, op0, op1), ins=ins, outs=[out], name=name)

    def stt(self, eng, out, in0, scalar, in1, op0, op1, name=""):
        ins = [in0, in1] + ([] if isinstance(scalar, (int, float)) else [scalar])
        return self.add(eng, lambda e: e.scalar_tensor_tensor(out, in0, scalar, in1, op0, op1),
                        ins=ins, outs=[out], name=name)

    def copy(self, eng, out, in_, name=""):
        if eng == "act":
            return self.add(eng, lambda e: e.copy(out, in_), ins=[in_], outs=[out], name=name)
        return self.add(eng, lambda e: e.tensor_copy(out, in_), ins=[in_], outs=[out], name=name)

    def memset(self, eng, ap, val, name=""):
        return self.add(eng, lambda e: e.memset(ap, val), ins=[], outs=[ap], name=name)

    def dma(self, queue, out, in_, key, name="", after=()):
        return self.add(queue, lambda e: e.dma_start(out=out, in_=in_), ins=[in_], outs=[out],
                        dma_key=key, extra_deps=after, name=name)

    def barrier_wait(self, eng, ops):
        return self.add(eng, None, extra_deps=ops)

    def emit(self):
        nc = self.nc
        for e in ENGINES:
            n = 0
            for op in self.ops[e]:
                if op.dma_key is None and op.sig:
                    n += 1
                    op.semval = n
        import contextlib
        with contextlib.ExitStack() as st:
            esem = {e: st.enter_context(nc.semaphore("s_" + e)) for e in ENGINES}
            dsem = {k: st.enter_context(nc.semaphore("d_%d" % i))
                    for i, k in enumerate(self.dma_count)}
            block = st.enter_context(nc.Block())

            def run(ename, eng):
                for op in self.ops[ename]:
                    for d in op.waits:
                        if d.dma_key is not None:
                            eng.wait_ge(dsem[d.dma_key], d.semval)
                        else:
                            eng.wait_ge(esem[d.eng], d.semval)
                    if op.fn is None:
                        continue
                    ins = op.fn(eng)
                    if op.dma_key is not None:
                        ins.then_inc(dsem[op.dma_key], 16)
                    elif op.sig:
                        ins.then_inc(esem[ename], 1)

            @block.tensor
            def _(e):
                run("pe", e)

            @block.scalar
            def _(e):
                run("act", e)

            @block.vector
            def _(e):
                run("dve", e)

            @block.gpsimd
            def _(e):
                run("pool", e)

            @block.sync
            def _(e):
                run("sp", e)


D = 1024
NCH = 8
FF = 2816
NFC = 22
KW = 31
HALO = 64
TOK = 2048
NT = HALO + TOK
PAD = 32
TILES = [(0, 64)] + [(HALO + 512 * i, 512) for i in range(4)]
PIECES_L = {0: [(0, 3), (3, 3), (6, 4), (10, 4), (14, 4), (18, 4)],
            1: [(0, 4), (4, 4), (8, 4), (12, 4), (16, 3), (19, 3)]}
N_PIECES = 6
NF = 4
EPS = 1e-6
ADA_BLK = 512
N_ADA_BLK = 6 * D // ADA_BLK
POOL_W = (2, 4, 8, 16)

PV = {}
_o = 0
for _n, _w in (("c", 8), ("adab", 96), ("nmg", 16), ("nfg", 16), ("b1", 16), ("wdw", 248),
               ("bdw", 8), ("lng", 8), ("lnb", 8), ("b2", 8), ("pls", 8), ("fing", 8),
               ("mask", 1), ("pscale", 64)):
    PV[_n] = _o
    _o += _w
NPV = _o


def build_nc(stop_after=None):
    nc = bass.Bass("TRN2", target_bir_lowering=False)
    x_d = nc.dram_tensor("x", [128, NCH, NT], F32, kind="ExternalInput").ap()
    pv_d = nc.dram_tensor("pv", [128, NPV], F32, kind="ExternalInput").ap()
    adaw_d = nc.dram_tensor("adaw", [2, N_ADA_BLK, 128, NCH * ADA_BLK], F32, kind="ExternalInput").ap()
    w1_d = nc.dram_tensor("w1", [4, 128, NCH * 512], F32, kind="ExternalInput").ap()
    w2_d = nc.dram_tensor("w2", [128, NCH * D], F32, kind="ExternalInput").ap()
    wg_d = nc.dram_tensor("wg", [2, 128, NCH * FF], F32, kind="ExternalInput").ap()
    wu_d = nc.dram_tensor("wu", [2, 128, NCH * FF], F32, kind="ExternalInput").ap()
    wd_d = nc.dram_tensor("wd", [2, 128, NFC * D], F32, kind="ExternalInput").ap()
    pw_d = nc.dram_tensor("pw", [128, 4 * 2 * 256], F32, kind="ExternalInput").ap()
    y_d = nc.dram_tensor("y", [128, NCH, TOK], F32, kind="ExternalOutput").ap()

    import contextlib
    with contextlib.ExitStack() as st:
        SB_BYTES = 211500
        sb = st.enter_context(nc.sbuf_tensor("sb", [128, SB_BYTES], U8))
        ps = st.enter_context(nc.psum_tensor("ps", [128, 4096], F32))
        P = Prog(nc)
        cur = [0]

        def carve(nbytes):
            a = cur[0]
            cur[0] = a + (nbytes + 63) // 64 * 64
            assert cur[0] <= SB_BYTES, ("SBUF overflow", cur[0])
            return a

        def view(off, shape, dt):
            n = int(np.prod(shape)) * DSIZE[dt]
            a = sb[:, off:off + n].bitcast(dt)
            if len(shape) == 2:
                a = a.rearrange("p (a b) -> p a b", a=shape[0])
            elif len(shape) == 3:
                a = a.rearrange("p (a b c) -> p a b c", a=shape[0], b=shape[1])
            return a

        X = view(carve(NCH * NT * 4), [NCH, NT], F32)
        AW = PAD + NT
        ACTB = view(carve(NCH * AW * 2), [NCH, AW], BF16)
        PVS = view(carve(NPV * 4), [NPV], F32)
        MOD = view(carve(2 * 48 * 4), [2, 48], F32)
        DER = view(carve(2 * 6 * 8 * 4), [2, 6, 8], F32)
        CACT = view(carve(8 * 2), [8], BF16)
        CSIL = view(carve(8 * 4), [8], F32)
        IDENT = view(carve(128 * 2), [128], BF16)
        IDENTF = view(carve(128 * 4), [128], F32)
        ONESD = view(carve(128 * 2), [128], BF16)
        EPSB = view(carve(4), [1], F32)
        SQ_off = carve(NCH * 512 * 2)
        SQ = view(SQ_off, [NCH, 512], BF16)
        Z = [view(SQ_off + i * NF * 512 * 2, [NF, 512], BF16) for i in range(2)]
        H_off = carve(2 * NCH * 512 * 2)
        Hb = [view(H_off + i * NCH * 512 * 2, [NCH, 512], BF16) for i in range(2)]
        DIAG = [view(H_off + i * KW * 128 * 2, [KW, 128], BF16) for i in range(2)]
        FT = [view(carve(528 * 4), [528], F32) for _ in range(6)]
        RING_SLOT = 3 * NCH * NF * 128 * 2
        W_off = carve(2 * RING_SLOT)
        ring = []
        for s in range(2):
            b = W_off + s * RING_SLOT
            ring.append((view(b, [NCH, NF * 128], BF16),
                         view(b + NCH * NF * 128 * 2, [NCH, NF * 128], BF16),
                         view(b + 2 * NCH * NF * 128 * 2, [NF, D], BF16)))
        W1 = view(W_off, [4, NCH, 512], BF16)
        W2 = view(W_off + 4 * NCH * 512 * 2, [NCH, D], BF16)
        ADA_off = carve(2 * NCH * ADA_BLK * 2)
        ADA = [view(ADA_off + i * NCH * ADA_BLK * 2, [NCH, ADA_BLK], BF16) for i in range(2)]
        PW = view(ADA_off, [4, 2, 256], BF16)
        PSB = [ps[:, b * 512:(b + 1) * 512] for b in range(8)]

        def pv(name, i=0, n=1):
            return PVS[:, PV[name] + i: PV[name] + i + n]

        P.dma("sp", PVS, pv_d, "pv")
        for j, (t0, n) in enumerate(TILES[:2]):
            P.dma("sp", X[:, :, t0:t0 + n], x_d[:, :, t0:t0 + n], "x%d" % j)
        P.memset("pool", IDENTF, 0.0)
        P.memset("dve", ONESD, 1.0 / D)
        P.memset("dve", EPSB, EPS)
        P.add("pool", lambda e: e.affine_select(IDENTF, IDENTF, pattern=[[-1, 128]], compare_op=ALU.not_equal,
                                                fill=1.0, base=0, channel_multiplier=1),
              ins=[IDENTF], outs=[IDENTF])
        P.copy("pool", IDENT, IDENTF)
        P.act(CSIL, pv("c", 0, 8), AF.Silu)
        P.copy("dve", CACT, CSIL)

        ada_ctr = [0]

        def ada_dma(L, blk, buf=None, key=None):
            if buf is None:
                slot = ada_ctr[0] % 2
                ada_ctr[0] += 1
                buf, key = ADA[slot], "ada%d" % slot
            P.dma("pool", buf, adaw_d[L, blk].rearrange("p (k c) -> p k c", k=NCH), key)
            return buf

        ada_slots = {}

        def ada_dma_after(L, blk, after):
            slot = ada_ctr[0] % 2
            ada_ctr[0] += 1
            P.dma("pool", ADA[slot], adaw_d[L, blk].rearrange("p (k c) -> p k c", k=NCH), "ada%d" % slot, after=after)
            return ADA[slot]

        def mod_mm(L, blk):
            abuf = ada_slots.pop((L, blk))
            for j in range(4):
                col = 4 * blk + j
                for kc in range(NCH):
                    P.mm(PSB[1][:, L * 64 + col: L * 64 + col + 1], abuf[:, kc, j * 128:(j + 1) * 128],
                         CACT[:, kc:kc + 1], kc == 0, kc == NCH - 1)
            P.tt("dve", MOD[:, L, 4 * blk:4 * blk + 4], PSB[1][:, L * 64 + 4 * blk: L * 64 + 4 * blk + 4],
                 pv("adab", L * 48 + 4 * blk, 4), ALU.add)

        def mod_vec(L, v):
            return MOD[:, L, v * 8:(v + 1) * 8]

        def derive(L, which):
            if which == "mix":
                P.stt("dve", DER[:, L, 0, :], mod_vec(L, 1), 1.0, pv("nmg", L * 8, 8), ALU.add, ALU.mult)
            elif which == "gate_m":
                P.ts("dve", DER[:, L, 1, :], mod_vec(L, 2), 1.0, None, ALU.add)
                if L == 0:
                    P.tt("dve", DER[:, L, 2, :], DER[:, L, 1, :], pv("b2", 0, 8), ALU.mult)
                else:
                    P.tt("dve", DER[:, L, 2, :], DER[:, L, 1, :], pv("pls", 0, 8), ALU.mult)
            elif which == "ffn":
                P.stt("dve", DER[:, L, 3, :], mod_vec(L, 4), 1.0, pv("nfg", L * 8, 8), ALU.add, ALU.mult)
            elif which == "gate_f":
                P.ts("dve", DER[:, L, 4, :], mod_vec(L, 5), 1.0, None, ALU.add)

        ft_ctr = [0]
        ft_n = [4]

        def ft():
            ft_ctr[0] += 1
            return FT[2 + ft_ctr[0] % ft_n[0]]

        RS = FT[0]
        MEAN = FT[1]

        def rsqrt_into(dst, src):
            P.act(dst, src, AF.Ln, bias=EPSB[:, 0:1])
            P.act(dst, dst, AF.Exp, scale=-0.5)

        def prep_steps(t0, n, gm, sh, dst_fn, pool_style=False, rs=None, rs_off=0, sq=None):
            rs_t = RS if rs is None else rs
            SQb = SQ if sq is None else sq
            steps = []

            def s_sq():
                for c in range(NCH):
                    P.act(SQb[:, c, :n], X[:, c, t0:t0 + n], AF.Square)
            steps.append(s_sq)

            def s_stat():
                for c in range(NCH):
                    P.mm(PSB[0][:, :n], ONESD, SQb[:, c, :n], c == 0, c == NCH - 1)
                rsqrt_into(rs_t[:, rs_off:rs_off + n], PSB[0][:, :n])
            steps.append(s_stat)
            if pool_style:
                return steps
            for c in range(NCH):
                def s_h(c=c):
                    t = ft()
                    P.tt("dve", t[:, :n], X[:, c, t0:t0 + n], rs_t[:, rs_off:rs_off + n], ALU.mult)
                    P.act(dst_fn(c), t[:, :n], AF.Identity, bias=sh[:, c:c + 1], scale=gm[:, c:c + 1])
                steps.append(s_h)
            return steps

        def run_interleaved(groups, steps, start=0):
            steps = list(steps)
            ng = len(groups)
            for gi, g in enumerate(groups):
                g()
                if gi >= start and steps:
                    remaining_groups = ng - gi
                    k = -(-len(steps) // remaining_groups)
                    for _ in range(k):
                        if steps:
                            steps.pop(0)()
            for s in steps:
                s()

        for blk in range(2):
            ada_slots[(0, blk)] = ada_dma(0, blk)
        for i, blk in enumerate((2, 3)):
            base = (NCH * NT * 4 + 63) // 64 * 64 + i * NCH * ADA_BLK * 2
            tv = view(base, [NCH, ADA_BLK], BF16)
            ada_slots[(0, blk)] = ada_dma(0, blk, buf=tv, key="adat%d" % i)
        w1_ops = []
        for b in range(4):
            w1_ops.append(P.dma("pool", W1[:, b], w1_d[b].rearrange("p (k c) -> p k c", k=NCH), "w1_%d" % b))
        for j, (t0, n) in enumerate(TILES):
            if j >= 2:
                P.dma("sp", X[:, :, t0:t0 + n], x_d[:, :, t0:t0 + n], "x%d" % j, after=[w1_ops[1]])
        for blk in range(4):
            mod_mm(0, blk)
        P.memset("dve", ACTB[:, :, 0:PAD], 0.0)
        for blk in range(2):
            ada_slots[(0, blk + 4)] = ada_dma(0, blk + 4) if blk else ada_dma_after(0, blk + 4, [w1_ops[3]])
        P.dma("pool", W2, w2_d.rearrange("p (k c) -> p k c", k=NCH), "w2")
        derive(0, "mix")
        next_ada = [4]

        def more_mod0(k):
            for _ in range(k):
                blk = next_ada[0]
                if blk >= N_ADA_BLK:
                    return
                mod_mm(0, blk)
                if blk + 2 < N_ADA_BLK and (0, blk + 2) not in ada_slots:
                    ada_slots[(0, blk + 2)] = ada_dma(0, blk + 2)
                next_ada[0] += 1

        gm0 = DER[:, 0, 0, :]
        sh0 = mod_vec(0, 0)
        U = ACTB

        def a1_prep(j):
            t0, n = TILES[j]
            hb = Hb[j % 2]
            return prep_steps(t0, n, gm0, sh0, lambda c: hb[:, c, :n])

        for s in a1_prep(0):
            s()
        bank_ctr = [0]
        bank_set = [[2, 3, 4, 5, 6, 7]]

        def nb():
            bank_ctr[0] += 1
            bs = bank_set[0]
            return PSB[bs[bank_ctr[0] % len(bs)]]

        for j, (t0, n) in enumerate(TILES):
            hb = Hb[j % 2]
            groups = []
            for oc in range(NCH):
                def g(oc=oc):
                    blk, jj = oc // 2, oc % 2
                    pa, pg = nb(), nb()
                    for kc in range(NCH):
                        P.mm(pa[:, :n], W1[:, blk, kc, jj * 128:(jj + 1) * 128], hb[:, kc, :n], kc == 0, kc == NCH - 1)
                    for kc in range(NCH):
                        P.mm(pg[:, :n], W1[:, blk, kc, 256 + jj * 128:256 + (jj + 1) * 128], hb[:, kc, :n],
                             kc == 0, kc == NCH - 1)
                    sg = ft()
                    P.act(sg[:, :n], pg[:, :n], AF.Sigmoid, bias=pv("b1", 8 + oc))
                    P.stt("dve", U[:, oc, PAD + t0:PAD + t0 + n], pa[:, :n], pv("b1", oc), sg[:, :n], ALU.add, ALU.mult)
                groups.append(g)
            steps = a1_prep(j + 1) if j + 1 < len(TILES) else []
            run_interleaved(groups, steps)
            if j == 0:
                P.ts("dve", U[:, :, PAD:PAD + HALO], U[:, :, PAD:PAD + HALO], pv("mask"), None, ALU.mult)
            more_mod0(2)
        more_mod0(12)
        for blk in range(2):
            ada_slots[(1, blk)] = ada_dma(1, blk)
        derive(0, "gate_m")
        derive(0, "ffn")
        derive(0, "gate_f")

        ring_ctr = [0]

        def ring_dma(L, p):
            f0, nf = PIECES_L[L][p]
            slot = ring_ctr[0] % 2
            ring_ctr[0] += 1
            wg_s, wu_s, wd_s = ring[slot]
            off = NCH * 128 * f0
            P.dma("pool", wg_s[:, :, :nf * 128],
                  wg_d[L, :, off:off + NCH * nf * 128].rearrange("p (k c) -> p k c", k=NCH), "rg%d" % slot)
            P.dma("pool", wu_s[:, :, :nf * 128],
                  wu_d[L, :, off:off + NCH * nf * 128].rearrange("p (k c) -> p k c", k=NCH), "ru%d" % slot)
            P.dma("pool", wd_s[:, :nf, :],
                  wd_d[L, :, f0 * D:(f0 + nf) * D].rearrange("p (f c) -> p f c", f=nf), "rd%d" % slot)
            return slot

        T_D = 4

        def build_diag(c):
            dg = DIAG[c % 2]
            for k in range(T_D, KW):
                P.ts("dve", dg[:, k, :], IDENT, pv("wdw", k * 8 + c), None, ALU.mult)

        V = ACTB
        g_m1 = DER[:, 0, 1, :]
        gb2 = DER[:, 0, 2, :]

        MEANb = [FT[1], FT[5]]
        RSb = [FT[0], FT[4]]

        def a3_stats(j):
            t0, n = TILES[j]
            mean_t, rs_t = MEANb[j % 2], RSb[j % 2]
            steps = []

            def s_sq():
                for c in range(NCH):
                    P.act(SQ[:, c, :n], V[:, c, PAD + t0:PAD + t0 + n], AF.Square)
            steps.append(s_sq)

            def s_stat():
                for c in range(NCH):
                    P.mm(PSB[0][:, :n], ONESD, V[:, c, PAD + t0:PAD + t0 + n], c == 0, c == NCH - 1)
                for c in range(NCH):
                    P.mm(PSB[1][:, :n], ONESD, SQ[:, c, :n], c == 0, c == NCH - 1)
                P.copy("act", mean_t[:, :n], PSB[0][:, :n])
                t = ft()
                P.tt("dve", t[:, :n], mean_t[:, :n], mean_t[:, :n], ALU.mult)
                P.tt("dve", t[:, :n], PSB[1][:, :n], t[:, :n], ALU.subtract)
                P.ts("dve", t[:, :n], t[:, :n], 0.0, None, ALU.max)
                rsqrt_into(rs_t[:, :n], t[:, :n])
            steps.append(s_stat)
            return steps

        def a3_ln(j):
            t0, n = TILES[j]
            mean_t, rs_t = MEANb[j % 2], RSb[j % 2]
            sbuf = Hb[j % 2]
            steps = []
            for c in range(NCH):
                def s_ln(c=c):
                    t = ft()
                    P.tt("dve", t[:, :n], V[:, c, PAD + t0:PAD + t0 + n], mean_t[:, :n], ALU.subtract)
                    P.tt("dve", t[:, :n], t[:, :n], rs_t[:, :n], ALU.mult)
                    P.act(sbuf[:, c, :n], t[:, :n], AF.Silu, bias=pv("lnb", c), scale=pv("lng", c))
                steps.append(s_ln)
            return steps

        A3_ORD = [4, 3, 2, 1, 0]

        build_diag(0)
        ring_slots = {}
        for c in range(NCH):
            if c + 1 < NCH:
                build_diag(c + 1)
            if c == 1:
                ring_slots[(0, 0)] = ring_dma(0, 0)
            dg = DIAG[c % 2]
            if c == NCH - 1:
                ft_n[0] = 2
            for j in reversed(range(len(TILES))):
                t0, n = TILES[j]
                pc = nb()
                for k in range(T_D, KW):
                    P.mm(pc[:, :n], dg[:, k, :], U[:, c, PAD + t0 - 30 + k:PAD + t0 - 30 + k + n], k == T_D, k == KW - 1)
                acc = ft()
                for k in range(T_D):
                    src = U[:, c, PAD + t0 - 30 + k:PAD + t0 - 30 + k + n]
                    if k == 0:
                        P.ts("dve", acc[:, :n], src, pv("wdw", k * 8 + c), None, ALU.mult)
                    else:
                        P.stt("dve", acc[:, :n], src, pv("wdw", k * 8 + c), acc[:, :n], ALU.mult, ALU.add)
                P.stt("dve", U[:, c, PAD + t0:PAD + t0 + n], pc[:, :n], pv("bdw", c), acc[:, :n], ALU.add, ALU.add)
                if OPT_A3:
                    P.act(X[:, c, t0:t0 + n], X[:, c, t0:t0 + n], AF.Identity, bias=DER[:, 0, 2, c:c + 1])
                if c == NCH - 1:
                    if j == A3_ORD[0]:
                        for s_ in a3_stats(A3_ORD[0]):
                            s_()
                    elif j == A3_ORD[1]:
                        for s_ in a3_stats(A3_ORD[1]) + a3_ln(A3_ORD[0]):
                            s_()

        for idx, j in enumerate(A3_ORD):
            t0, n = TILES[j]
            sbuf = Hb[j % 2]
            groups = []
            for do in range(NCH):
                def g(do=do, t0=t0, n=n, sbuf=sbuf):
                    po = nb()
                    for kc in range(NCH):
                        P.mm(po[:, :n], W2[:, kc, do * 128:(do + 1) * 128], sbuf[:, kc, :n], kc == 0, kc == NCH - 1)
                    P.stt("dve", X[:, do, t0:t0 + n], po[:, :n], g_m1[:, do:do + 1], X[:, do, t0:t0 + n],
                          ALU.mult, ALU.add)
                groups.append(g)
            steps = []
            ln_s = a3_ln(A3_ORD[idx + 1]) if idx + 1 < len(A3_ORD) else []
            st_s = a3_stats(A3_ORD[idx + 2]) if idx + 2 < len(A3_ORD) else []
            steps += st_s[:1] + ln_s[:3] + st_s[1:] + ln_s[3:]
            if idx + 1 == len(A3_ORD) and stop_after != "A":
                t0f, nf0 = TILES[A3_ORD[0]]
                steps += prep_steps(t0f, nf0, DER[:, 0, 3, :], mod_vec(0, 3),
                                    lambda c: ACTB[:, c, PAD + t0f:PAD + t0f + nf0], sq=SQ, rs=FT[4])
            run_interleaved(groups, steps)
        ft_n[0] = 4
        bank_set[0] = [2, 3, 4, 5]

        H2 = ACTB

        def ffn(L, tiles, l1_mod=False, final=False, first_prepped=False, all_prepped=False, post_down=None, prep_fn=None, lead_steps=None, tiles_last=None):
            gmf = DER[:, L, 3, :]
            shf = mod_vec(L, 3)
            g_f1 = DER[:, L, 4, :]

            def h2_prep(j):
                t0, n = TILES[j]
                return prep_steps(t0, n, gmf, shf, lambda c: H2[:, c, PAD + t0:PAD + t0 + n], sq=Hb[0])

            if not (first_prepped or all_prepped):
                for s in h2_prep(tiles[0]):
                    s()
            leftover = []
            for p, (f0, nf) in enumerate(PIECES_L[L]):
                if (L, p) not in ring_slots:
                    ring_slots[(L, p)] = ring_dma(L, p)
                slot = ring_slots.pop((L, p))
                wg_s, wu_s, wd_s = ring[slot]
                nxt = None
                if l1_mod:
                    for blk in (2 * p, 2 * p + 1):
                        if (1, blk) not in ada_slots:
                            ada_slots[(1, blk)] = ada_dma(1, blk)
                    for blk in (2 * p, 2 * p + 1):
                        mod_mm(1, blk)
                    for blk in (2 * p + 2, 2 * p + 3):
                        if OPT_ADA and blk < N_ADA_BLK and (1, blk) not in ada_slots:
                            ada_slots[(1, blk)] = ada_dma(1, blk)

                def gu(j, zb):
                    t0, n = TILES[j]
                    groups = []
                    for fi in range(nf):
                        def g(fi=fi):
                            pg, pu = nb(), nb()
                            for kc in range(NCH):
                                P.mm(pg[:, :n], wg_s[:, kc, fi * 128:(fi + 1) * 128], H2[:, kc, PAD + t0:PAD + t0 + n],
                                     kc == 0, kc == NCH - 1)
                            for kc in range(NCH):
                                P.mm(pu[:, :n], wu_s[:, kc, fi * 128:(fi + 1) * 128], H2[:, kc, PAD + t0:PAD + t0 + n],
                                     kc == 0, kc == NCH - 1)
                            sg = ft()
                            P.act(sg[:, :n], pg[:, :n], AF.Silu)
                            P.tt("dve", Z[zb][:, fi, :n], pu[:, :n], sg[:, :n], ALU.mult)
                        groups.append(g)
                    return groups

                def down(j, zb):
                    t0, n = TILES[j]
                    groups = []
                    for do in range(NCH):
                        def g(do=do):
                            pd = PSB[6 + do % 2]
                            for fi in range(nf):
                                P.mm(pd[:, :n], wd_s[:, fi, do * 128:(do + 1) * 128], Z[zb][:, fi, :n],
                                     fi == 0, fi == nf - 1)
                            P.stt("dve", X[:, do, t0:t0 + n], pd[:, :n], g_f1[:, do:do + 1], X[:, do, t0:t0 + n],
                                  ALU.mult, ALU.add)
                        groups.append(g)
                    return groups

                prev = None
                last_piece = (p == N_PIECES - 1) and post_down is not None
                tiles_all = tiles
                if p == N_PIECES - 1 and tiles_last is not None:
                    tiles = tiles_last
                if p >= 1 and not last_piece:
                    bank_set[0] = [0, 2, 3, 4, 5] if l1_mod else [0, 1, 2, 3, 4, 5]
                else:
                    bank_set[0] = [2, 3, 4, 5]
                finished = []
                for ti, j in enumerate(tiles):
                    zb = ti % 2
                    steps = []
                    if p == 0 and ti + 1 < len(tiles) and not all_prepped:
                        steps = (prep_fn or h2_prep)(tiles[ti + 1])
                    if last_piece and finished:
                        steps = steps + post_down(finished.pop(0))
                    if p == 0 and lead_steps:
                        k_ = -(-len(lead_steps) // max(1, len(tiles) - 1 - ti))
                        steps = steps + lead_steps[:k_]
                        del lead_steps[:k_]
                    groups = gu(j, zb)
                    if prev is not None:
                        groups = groups + down(*prev)
                        finished.append(prev[0])
                    run_interleaved(groups, steps)
                    prev = (j, zb)
                dgroups = down(*prev)
                finished.append(prev[0])
                if last_piece:
                    run_interleaved(dgroups, post_down(finished.pop(0)))
                    for jj in finished:
                        leftover.extend(post_down(jj))
                else:
                    for g in dgroups:
                        g()
                tiles = tiles_all
                nn = None
                if p + 2 < N_PIECES:
                    nn = (L, p + 2)
                elif L == 0:
                    nn = (1, p + 2 - N_PIECES)
                if nn is not None and nn not in ring_slots:
                    ring_slots[nn] = ring_dma(*nn)
                if l1_mod:
                    for blk in (2 * p + 2, 2 * p + 3):
                        if blk < N_ADA_BLK and (1, blk) not in ada_slots:
                            ada_slots[(1, blk)] = ada_dma(1, blk)
            bank_set[0] = [2, 3, 4, 5]
            return leftover

        gm1 = DER[:, 1, 0, :]
        gpool = DER[:, 1, 2, :]
        RSX = FT[0]
        Hb1_off = H_off + NCH * 512 * 2
        HS = [view(Hb1_off + i * 528 * 4, [528], F32) for i in range(2)]
        XC = view(Hb1_off + 2 * 528 * 4, [NCH, 16], F32)
        T16 = view(Hb1_off + 2 * 528 * 4 + NCH * 16 * 4, [16], F32)
        SA = [FT[4], FT[5]]

        def pool_setup():
            derive(1, "mix")
            derive(1, "gate_m")
            derive(1, "ffn")
            derive(1, "gate_f")
            P.dma("pool", PW, pw_d.rearrange("p (g k c) -> p g k c", g=4, k=2), "pw")
            t0, n = TILES[0]
            for s in prep_steps(t0, n, None, None, None, pool_style=True, rs=RSX, rs_off=16, sq=Hb[0]):
                s()
            P.ts("dve", RSX[:, 0:16], RSX[:, 16 + n - 16:16 + n], pv("mask"), None, ALU.mult)
            for c in range(NCH):
                P.stt("dve", XC[:, c, :], X[:, c, t0 + n - 16:t0 + n], gm1[:, c:c + 1], RSX[:, 0:16], ALU.mult, ALU.mult)

        def pool_tile_steps(j, with_h2=True):
            t0, n = TILES[j]
            steps = list(prep_steps(t0, n, None, None, None, pool_style=True, rs=RSX, rs_off=16, sq=Hb[0]))
            for c in range(NCH):
                g = c // 2
                w = POOL_W[g]
                hs = HS[c % 2]
                mdst = ACTB[:, c, PAD + t0:PAD + t0 + n]
                def s_hs(c=c, hs=hs):
                    P.copy("act", hs[:, 0:16], XC[:, c, :])
                    P.stt("dve", hs[:, 16:16 + n], X[:, c, t0:t0 + n], gm1[:, c:c + 1], RSX[:, 16:16 + n],
                          ALU.mult, ALU.mult)
                    P.copy("act", XC[:, c, :], hs[:, n:n + 16])
                steps.append(s_hs)
                a = hs
                s_ = 1
                k = 0
                while s_ < w:
                    b = SA[k % 2]
                    steps.append(lambda a=a, b=b, s_=s_: P.tt("dve", b[:, s_:16 + n], a[:, s_:16 + n],
                                                              a[:, 0:16 + n - s_], ALU.add))
                    a = b
                    s_ *= 2
                    k += 1
                steps.append(lambda a=a, hs=hs, mdst=mdst, w=w: P.stt("dve", mdst, a[:, 16:16 + n], 1.0 / w,
                                                                      hs[:, 16:16 + n], ALU.mult, ALU.subtract))
                if j == 1:
                    def s_edge(a=a, hs=hs, c=c, g=g):
                        P.tt("dve", T16, a[:, 16:32], pv("pscale", g * 16, 16), ALU.mult)
                        P.tt("dve", ACTB[:, c, PAD + t0:PAD + t0 + 16], T16, hs[:, 16:32], ALU.subtract)
                    steps.append(s_edge)

            for g in range(4):
                def s_mm(g=g):
                    for do in range(2):
                        pp = nb()
                        for kc in range(2):
                            P.mm(pp[:, :n], PW[:, g, kc, do * 128:(do + 1) * 128],
                                 ACTB[:, 2 * g + kc, PAD + t0:PAD + t0 + n], kc == 0, kc == 1)
                        co = 2 * g + do
                        P.stt("dve", X[:, co, t0:t0 + n], pp[:, :n], gpool[:, co:co + 1], X[:, co, t0:t0 + n],
                              ALU.mult, ALU.add)
                steps.append(s_mm)
            if with_h2:
                steps += prep_steps(t0, n, DER[:, 1, 3, :], mod_vec(1, 3),
                                    lambda c: ACTB[:, c, PAD + t0:PAD + t0 + n], sq=Hb[0], rs=FT[1])
            return steps

        def pool_post_down(j):
            if j == 0:
                return [pool_setup]
            return pool_tile_steps(j)

        ring_slots[(0, 1)] = ring_dma(0, 1)
        ft_n[0] = 2
        pool_left = []
        if stop_after == "F0":
            ffn(0, list(A3_ORD), l1_mod=True, first_prepped=True)
        elif stop_after == "P":
            ffn(0, list(A3_ORD), l1_mod=True, first_prepped=True)
            pool_setup()
            for j in range(1, len(TILES)):
                for st_ in pool_tile_steps(j, with_h2=False):
                    st_()
        elif stop_after != "A":
            pool_left = ffn(0, list(A3_ORD), l1_mod=True, first_prepped=True, post_down=pool_post_down,
                            tiles_last=list(range(len(TILES))))

        outs = []

        def final_steps(j):
            t0, n = TILES[j]
            steps = []
            if stop_after is None:
                steps += prep_steps(t0, n, None, None, None, pool_style=True, sq=Hb[0])
                for c in range(NCH):
                    def s_f(c=c):
                        P.stt("dve", X[:, c, t0:t0 + n], X[:, c, t0:t0 + n], pv("fing", c), RS[:, :n], ALU.mult, ALU.mult)
                        if OPT_CHST:
                            outs.append(P.dma("sp", y_d[:, c, t0 - HALO:t0 - HALO + n], X[:, c, t0:t0 + n],
                                              "y%d_%d" % (j, c)))
                    steps.append(s_f)
                if OPT_CHST:
                    return steps

            def s_out():
                outs.append(P.dma("sp", y_d[:, :, t0 - HALO:t0 - HALO + n], X[:, :, t0:t0 + n], "y%d" % j))
            steps.append(s_out)
            return steps

        if stop_after not in ("A", "F0", "P"):
            for st_ in ffn(1, list(range(1, len(TILES))), all_prepped=True, lead_steps=pool_left,
                           post_down=final_steps if OPT_FINAL else None):
                st_()
            if not OPT_FINAL:
                for j in range(1, len(TILES)):
                    for st_ in final_steps(j):
                        st_()
        else:
            for j in range(1, len(TILES)):
                for st_ in final_steps(j):
                    st_()
        P.barrier_wait("sp", outs)
        P.emit()
    return nc


def _chunked(v):
    v = np.asarray(v, np.float32)
    lead = v.shape[:-1]
    n = v.shape[-1] // 128
    return np.moveaxis(v.reshape(lead + (n, 128)), -1, 0)


def _kmajor(w):
    K, N = w.shape
    return np.ascontiguousarray(w.reshape(K // 128, 128, N).transpose(1, 0, 2))


def prepare_inputs(x, c, ada_w, ada_b, norm_mix_g, norm_ffn_g, conv_w1, conv_b1, conv_wdw, conv_bdw,
                   conv_ln_g, conv_ln_b, conv_w2, conv_b2, pool_w, pool_ls, ffn_w_gate, ffn_w_up,
                   ffn_w_down, final_g):
    f = np.float32
    x = np.asarray(x, f)
    B, S, _ = x.shape
    n_cores = 8
    per_seq = S // TOK
    adaw = np.stack([np.stack([_kmajor(np.asarray(ada_w[L], f)[:, b * ADA_BLK:(b + 1) * ADA_BLK]).reshape(128, -1)
                               for b in range(N_ADA_BLK)]) for L in range(2)])
    w1 = np.asarray(conv_w1[0], f)
    blocks = []
    for b in range(4):
        cols = np.concatenate([np.arange((2 * b) * 128, (2 * b + 2) * 128),
                               D + np.arange((2 * b) * 128, (2 * b + 2) * 128)])
        blocks.append(_kmajor(w1[:, cols]).reshape(128, -1))
    w1l = np.stack(blocks)
    w2l = _kmajor(np.asarray(conv_w2[0], f)).reshape(128, -1)

    def piece_major(w, L):
        parts = []
        for (f0, nf) in PIECES_L[L]:
            parts.append(_kmajor(w[:, f0 * 128:(f0 + nf) * 128]).reshape(128, -1))
        return np.concatenate(parts, axis=1)
    wgl = np.stack([piece_major(np.asarray(ffn_w_gate[L], f), L) for L in range(2)])
    wul = np.stack([piece_major(np.asarray(ffn_w_up[L], f), L) for L in range(2)])
    wdl = np.stack([_kmajor(np.asarray(ffn_w_down[L], f)).reshape(128, -1) for L in range(2)])
    pw = np.asarray(pool_w[0], f)
    pwl = np.ascontiguousarray(pw.reshape(4, 2, 128, 256).transpose(2, 0, 1, 3)).reshape(128, -1)

    in_maps = []
    for core in range(n_cores):
        b = core // per_seq
        k = core % per_seq
        start = k * TOK
        xs = np.zeros((NT, D), f)
        if k > 0:
            xs[:] = x[b, start - HALO:start + TOK]
        else:
            xs[HALO:] = x[b, :TOK]
        xl = np.ascontiguousarray(xs.reshape(NT, NCH, 128).transpose(2, 1, 0))
        pvv = np.zeros((128, NPV), f)

        def put(name, arr):
            arr = np.asarray(arr, f).reshape(128, -1)
            pvv[:, PV[name]:PV[name] + arr.shape[1]] = arr
        put("c", _chunked(np.asarray(c, f)[b]))
        put("adab", _chunked(np.asarray(ada_b, f)))
        put("nmg", _chunked(np.asarray(norm_mix_g, f)))
        put("nfg", _chunked(np.asarray(norm_ffn_g, f)))
        put("b1", _chunked(np.asarray(conv_b1[0], f)))
        put("wdw", _chunked(np.asarray(conv_wdw[0], f)))
        put("bdw", _chunked(np.asarray(conv_bdw[0], f)))
        put("lng", _chunked(np.asarray(conv_ln_g[0], f)))
        put("lnb", _chunked(np.asarray(conv_ln_b[0], f)))
        put("b2", _chunked(np.asarray(conv_b2[0], f)))
        put("pls", _chunked(np.asarray(pool_ls[0], f)))
        put("fing", _chunked(np.asarray(final_g, f)))
        pvv[:, PV["mask"]] = 0.0 if k == 0 else 1.0
        psc = np.zeros((4, 16), f)
        for g, w in enumerate(POOL_W):
            for t in range(16):
                psc[g, t] = 1.0 / (min(w, t + 1) if k == 0 else w)
        pvv[:, PV["pscale"]:PV["pscale"] + 64] = psc.reshape(1, 64)
        in_maps.append({"x": xl, "pv": pvv, "adaw": adaw, "w1": w1l, "w2": w2l, "wg": wgl, "wu": wul,
                        "wd": wdl, "pw": pwl})
    return in_maps


def assemble(results, B, S):
    per_seq = S // TOK
    out = np.empty((B, S, D), np.float32)
    for core, r in enumerate(results):
        b = core // per_seq
        k = core % per_seq
        y = r["y"]
        out[b, k * TOK:(k + 1) * TOK] = y.transpose(2, 1, 0).reshape(TOK, D)
    return out


_NC_CACHE = {}
STOP_AFTER = None


def kernel(**inputs):
    x = inputs["x"]
    B, S, _ = x.shape
    in_maps = prepare_inputs(**inputs)
    if STOP_AFTER not in _NC_CACHE:
        _NC_CACHE[STOP_AFTER] = build_nc(STOP_AFTER)
    res = run_bass_kernel_spmd(_NC_CACHE[STOP_AFTER], in_maps, core_ids=list(range(8)))
    return assemble(res.results, B, S)
```

```python
import numpy as np
import concourse.bass as bass
import concourse.mybir as mybir
from concourse.bass_utils import run_bass_kernel_spmd

F32 = mybir.dt.float32
BF16 = mybir.dt.bfloat16
U8 = mybir.dt.uint8
AF = mybir.ActivationFunctionType
ALU = mybir.AluOpType
DSIZE = {F32: 4, BF16: 2, U8: 1}

ENGINES = ("pe", "act", "dve", "pool", "sp")
import os
OPT_A3 = os.environ.get("K_OPT_A3", "1") == "1"
OPT_ADA = os.environ.get("K_OPT_ADA", "1") == "1"
OPT_FINAL = os.environ.get("K_OPT_FINAL", "1") == "1"
OPT_CHST = os.environ.get("K_OPT_CHST", "0") == "1"


def _ap_intervals(ap, cap=64):
    es = DSIZE[ap.dtype]
    pat = ap.ap
    pstep = pat[0][0]
    off = ap.offset
    base = off % pstep if pstep > 0 else off
    dims = [(s, n) for (s, n) in pat[1:] if n > 1]
    dims.sort(key=lambda d: -d[0])
    run = 1
    while dims and dims[-1][0] == run:
        run *= dims[-1][1]
        dims.pop()
    nout = 1
    for _, n in dims:
        nout *= n
    if nout > cap or any(s < run for s, _ in dims):
        ext = run + sum(s * (n - 1) for s, n in dims)
        return ap.tensor.name, [(base * es, (base + ext) * es)]
    starts = [base]
    for s, n in dims:
        starts = [b + s * i for b in starts for i in range(n)]
    return ap.tensor.name, [(b * es, (b + run) * es) for b in starts]


class _Op:
    __slots__ = ("eng", "fn", "idx", "waits", "sig", "semval", "dma_key", "name")

    def __init__(self, eng, fn, dma_key=None, name=""):
        self.eng = eng
        self.fn = fn
        self.idx = -1
        self.waits = []
        self.sig = False
        self.semval = 0
        self.dma_key = dma_key
        self.name = name


class Prog:
    def __init__(self, nc, tracked=("sb", "ps")):
        self.nc = nc
        self.tracked = set(tracked)
        self.ops = {e: [] for e in ENGINES}
        self.wr = {s: [] for s in tracked}
        self.rd = {s: [] for s in tracked}
        self.waited = {e: {} for e in ENGINES}
        self.dma_count = {}
        self.all_ops = []

    PS_BANK = 2048

    def _deps_for(self, ins, outs, eng=None):
        deps = []
        for ap in ins:
            sp, ivs = _ap_intervals(ap)
            if sp not in self.tracked:
                continue
            for lo, hi in ivs:
                for (a, b, op) in self.wr[sp]:
                    if a < hi and lo < b:
                        deps.append(op)
        for ap in outs:
            sp, ivs = _ap_intervals(ap)
            if sp not in self.tracked:
                continue
            for lo, hi in ivs:
                for (a, b, op) in self.wr[sp]:
                    if a < hi and lo < b:
                        deps.append(op)
                for (a, b, op) in self.rd[sp]:
                    if a < hi and lo < b:
                        deps.append(op)
        if "ps" in self.tracked:
            B = self.PS_BANK
            for ap in list(ins) + list(outs):
                sp, ivs = _ap_intervals(ap)
                if sp != "ps":
                    continue
                for lo, hi in ivs:
                    b0, b1 = lo // B, (hi - 1) // B
                    for lst in (self.wr[sp], self.rd[sp]):
                        for (a, b, op) in lst:
                            if op.eng != eng and a // B <= b1 and b0 <= (b - 1) // B:
                                deps.append(op)
        return deps

    @staticmethod
    def _cut(lst, lo, hi):
        out = []
        for (a, b, op) in lst:
            if a < hi and lo < b:
                if a < lo:
                    out.append((a, lo, op))
                if hi < b:
                    out.append((hi, b, op))
            else:
                out.append((a, b, op))
        return out

    def _update(self, op, ins, outs):
        for ap in outs:
            sp, ivs = _ap_intervals(ap)
            if sp not in self.tracked:
                continue
            for lo, hi in ivs:
                self.wr[sp] = self._cut(self.wr[sp], lo, hi)
                self.rd[sp] = self._cut(self.rd[sp], lo, hi)
                self.wr[sp].append((lo, hi, op))
        for ap in ins:
            sp, ivs = _ap_intervals(ap)
            if sp not in self.tracked:
                continue
            for lo, hi in ivs:
                if op.dma_key is None:
                    self.rd[sp] = [
                        (a, b, o) for (a, b, o) in self.rd[sp]
                        if not (o.eng == op.eng and o.dma_key is None and lo <= a and b <= hi)
                    ]
                self.rd[sp].append((lo, hi, op))

    def add(self, eng, fn, ins=(), outs=(), dma_key=None, extra_deps=(), name=""):
        op = _Op(eng, fn, dma_key, name)
        lst = self.ops[eng]
        op.idx = len(lst)
        deps = self._deps_for(ins, outs, eng) + list(extra_deps)
        best = {}
        dma_deps = []
        for d in deps:
            if d is op:
                continue
            if d.dma_key is not None:
                if d not in dma_deps:
                    dma_deps.append(d)
                continue
            if d.eng == eng and eng == "pe":
                continue
            cur = best.get(d.eng)
            if cur is None or d.idx > cur.idx:
                best[d.eng] = d
        w = self.waited[eng]
        for peng, d in best.items():
            if w.get(peng, -1) >= d.idx:
                continue
            w[peng] = d.idx
            d.sig = True
            op.waits.append(d)
        for d in dma_deps:
            key = ("dma", d.dma_key)
            if w.get(key, -1) >= d.semval:
                continue
            w[key] = d.semval
            op.waits.append(d)
        if dma_key is not None:
            n = self.dma_count.get(dma_key, 0) + 1
            self.dma_count[dma_key] = n
            op.semval = 16 * n
        lst.append(op)
        self.all_ops.append(op)
        self._update(op, ins, outs)
        return op

    def mm(self, out, lhsT, rhs, start=True, stop=True, name=""):
        return self.add("pe", lambda e: e.matmul(out, lhsT, rhs, start=start, stop=stop),
                        ins=[lhsT, rhs], outs=[out], name=name)

    def act(self, out, in_, func, bias=None, scale=None, eng="act", name=""):
        ins = [in_]
        kw = {}
        if bias is not None:
            kw["bias"] = bias
            if not isinstance(bias, (int, float)):
                ins.append(bias)
        if scale is not None:
            kw["scale"] = scale
            if not isinstance(scale, (int, float)):
                ins.append(scale)
        return self.add(eng, lambda e: e.activation(out, in_, func, **kw), ins=ins, outs=[out], name=name)

    def tt(self, eng, out, a, b, op, name=""):
        return self.add(eng, lambda e: e.tensor_tensor(out, a, b, op), ins=[a, b], outs=[out], name=name)

    def ts(self, eng, out, a, s1, s2, op0, op1=None, name=""):
        ins = [a] + [s for s in (s1, s2) if s is not None and not isinstance(s, (int, float))]
        if op1 is None:
            return self.add(eng, lambda e: e.tensor_scalar(out, a, s1, None, op0), ins=ins, outs=[out], name=name)
        return self.add(eng, lambda e: e.tensor_scalar(out, a, s1, s2, op0, op1), ins=ins, outs=[out], name=name)

    def stt(self, eng, out, in0, scalar, in1, op0, op1, name=""):
        ins = [in0, in1] + ([] if isinstance(scalar, (int, float)) else [scalar])
        return self.add(eng, lambda e: e.scalar_tensor_tensor(out, in0, scalar, in1, op0, op1),
                        ins=ins, outs=[out], name=name)

    def copy(self, eng, out, in_, name=""):
        if eng == "act":
            return self.add(eng, lambda e: e.copy(out, in_), ins=[in_], outs=[out], name=name)
        return self.add(eng, lambda e: e.tensor_copy(out, in_), ins=[in_], outs=[out], name=name)

    def memset(self, eng, ap, val, name=""):
        return self.add(eng, lambda e: e.memset(ap, val), ins=[], outs=[ap], name=name)

    def dma(self, queue, out, in_, key, name="", after=()):
        return self.add(queue, lambda e: e.dma_start(out=out, in_=in_), ins=[in_], outs=[out],
                        dma_key=key, extra_deps=after, name=name)

    def barrier_wait(self, eng, ops):
        return self.add(eng, None, extra_deps=ops)

    def emit(self):
        nc = self.nc
        for e in ENGINES:
            n = 0
            for op in self.ops[e]:
                if op.dma_key is None and op.sig:
                    n += 1
                    op.semval = n
        import contextlib
        with contextlib.ExitStack() as st:
            esem = {e: st.enter_context(nc.semaphore("s_" + e)) for e in ENGINES}
            dsem = {k: st.enter_context(nc.semaphore("d_%d" % i))
                    for i, k in enumerate(self.dma_count)}
            block = st.enter_context(nc.Block())

            def run(ename, eng):
                for op in self.ops[ename]:
                    for d in op.waits:
                        if d.dma_key is not None:
                            eng.wait_ge(dsem[d.dma_key], d.semval)
                        else:
                            eng.wait_ge(esem[d.eng], d.semval)
                    if op.fn is None:
                        continue
                    ins = op.fn(eng)
                    if op.dma_key is not None:
                        ins.then_inc(dsem[op.dma_key], 16)
                    elif op.sig:
                        ins.then_inc(esem[ename], 1)

            @block.tensor
            def _(e):
                run("pe", e)

            @block.scalar
            def _(e):
                run("act", e)

            @block.vector
            def _(e):
                run("dve", e)

            @block.gpsimd
            def _(e):
                run("pool", e)

            @block.sync
            def _(e):
                run("sp", e)


D = 1024
NCH = 8
FF = 2816
NFC = 22
KW = 31
HALO = 64
TOK = 2048
NT = HALO + TOK
PAD = 32
TILES = [(0, 64)] + [(HALO + 512 * i, 512) for i in range(4)]
PIECES_L = {0: [(0, 3), (3, 3), (6, 4), (10, 4), (14, 4), (18, 4)],
            1: [(0, 4), (4, 4), (8, 4), (12, 4), (16, 3), (19, 3)]}
N_PIECES = 6
NF = 4
EPS = 1e-6
ADA_BLK = 512
N_ADA_BLK = 6 * D // ADA_BLK
POOL_W = (2, 4, 8, 16)

PV = {}
_o = 0
for _n, _w in (("c", 8), ("adab", 96), ("nmg", 16), ("nfg", 16), ("b1", 16), ("wdw", 248),
               ("bdw", 8), ("lng", 8), ("lnb", 8), ("b2", 8), ("pls", 8), ("fing", 8),
               ("mask", 1), ("pscale", 64)):
    PV[_n] = _o
    _o += _w
NPV = _o


def build_nc(stop_after=None):
    nc = bass.Bass("TRN2", target_bir_lowering=False)
    x_d = nc.dram_tensor("x", [128, NCH, NT], F32, kind="ExternalInput").ap()
    pv_d = nc.dram_tensor("pv", [128, NPV], F32, kind="ExternalInput").ap()
    adaw_d = nc.dram_tensor("adaw", [2, N_ADA_BLK, 128, NCH * ADA_BLK], F32, kind="ExternalInput").ap()
    w1_d = nc.dram_tensor("w1", [4, 128, NCH * 512], F32, kind="ExternalInput").ap()
    w2_d = nc.dram_tensor("w2", [128, NCH * D], F32, kind="ExternalInput").ap()
    wg_d = nc.dram_tensor("wg", [2, 128, NCH * FF], F32, kind="ExternalInput").ap()
    wu_d = nc.dram_tensor("wu", [2, 128, NCH * FF], F32, kind="ExternalInput").ap()
    wd_d = nc.dram_tensor("wd", [2, 128, NFC * D], F32, kind="ExternalInput").ap()
    pw_d = nc.dram_tensor("pw", [128, 4 * 2 * 256], F32, kind="ExternalInput").ap()
    y_d = nc.dram_tensor("y", [128, NCH, TOK], F32, kind="ExternalOutput").ap()

    import contextlib
    with contextlib.ExitStack() as st:
        SB_BYTES = 211500
        sb = st.enter_context(nc.sbuf_tensor("sb", [128, SB_BYTES], U8))
        ps = st.enter_context(nc.psum_tensor("ps", [128, 4096], F32))
        P = Prog(nc)
        cur = [0]

        def carve(nbytes):
            a = cur[0]
            cur[0] = a + (nbytes + 63) // 64 * 64
            assert cur[0] <= SB_BYTES, ("SBUF overflow", cur[0])
            return a

        def view(off, shape, dt):
            n = int(np.prod(shape)) * DSIZE[dt]
            a = sb[:, off:off + n].bitcast(dt)
            if len(shape) == 2:
                a = a.rearrange("p (a b) -> p a b", a=shape[0])
            elif len(shape) == 3:
                a = a.rearrange("p (a b c) -> p a b c", a=shape[0], b=shape[1])
            return a

        X = view(carve(NCH * NT * 4), [NCH, NT], F32)
        AW = PAD + NT
        ACTB = view(carve(NCH * AW * 2), [NCH, AW], BF16)
        PVS = view(carve(NPV * 4), [NPV], F32)
        MOD = view(carve(2 * 48 * 4), [2, 48], F32)
        DER = view(carve(2 * 6 * 8 * 4), [2, 6, 8], F32)
        CACT = view(carve(8 * 2), [8], BF16)
        CSIL = view(carve(8 * 4), [8], F32)
        IDENT = view(carve(128 * 2), [128], BF16)
        IDENTF = view(carve(128 * 4), [128], F32)
        ONESD = view(carve(128 * 2), [128], BF16)
        EPSB = view(carve(4), [1], F32)
        SQ_off = carve(NCH * 512 * 2)
        SQ = view(SQ_off, [NCH, 512], BF16)
        Z = [view(SQ_off + i * NF * 512 * 2, [NF, 512], BF16) for i in range(2)]
        H_off = carve(2 * NCH * 512 * 2)
        Hb = [view(H_off + i * NCH * 512 * 2, [NCH, 512], BF16) for i in range(2)]
        DIAG = [view(H_off + i * KW * 128 * 2, [KW, 128], BF16) for i in range(2)]
        FT = [view(carve(528 * 4), [528], F32) for _ in range(6)]
        RING_SLOT = 3 * NCH * NF * 128 * 2
        W_off = carve(2 * RING_SLOT)
        ring = []
        for s in range(2):
            b = W_off + s * RING_SLOT
            ring.append((view(b, [NCH, NF * 128], BF16),
                         view(b + NCH * NF * 128 * 2, [NCH, NF * 128], BF16),
                         view(b + 2 * NCH * NF * 128 * 2, [NF, D], BF16)))
        W1 = view(W_off, [4, NCH, 512], BF16)
        W2 = view(W_off + 4 * NCH * 512 * 2, [NCH, D], BF16)
        ADA_off = carve(2 * NCH * ADA_BLK * 2)
        ADA = [view(ADA_off + i * NCH * ADA_BLK * 2, [NCH, ADA_BLK], BF16) for i in range(2)]
        PW = view(ADA_off, [4, 2, 256], BF16)
        PSB = [ps[:, b * 512:(b + 1) * 512] for b in range(8)]

        def pv(name, i=0, n=1):
            return PVS[:, PV[name] + i: PV[name] + i + n]

        P.dma("sp", PVS, pv_d, "pv")
        for j, (t0, n) in enumerate(TILES[:2]):
            P.dma("sp", X[:, :, t0:t0 + n], x_d[:, :, t0:t0 + n], "x%d" % j)
        P.memset("pool", IDENTF, 0.0)
        P.memset("dve", ONESD, 1.0 / D)
        P.memset("dve", EPSB, EPS)
        P.add("pool", lambda e: e.affine_select(IDENTF, IDENTF, pattern=[[-1, 128]], compare_op=ALU.not_equal,
                                                fill=1.0, base=0, channel_multiplier=1),
              ins=[IDENTF], outs=[IDENTF])
        P.copy("pool", IDENT, IDENTF)
        P.act(CSIL, pv("c", 0, 8), AF.Silu)
        P.copy("dve", CACT, CSIL)

        ada_ctr = [0]

        def ada_dma(L, blk, buf=None, key=None):
            if buf is None:
                slot = ada_ctr[0] % 2
                ada_ctr[0] += 1
                buf, key = ADA[slot], "ada%d" % slot
            P.dma("pool", buf, adaw_d[L, blk].rearrange("p (k c) -> p k c", k=NCH), key)
            return buf

        ada_slots = {}

        def ada_dma_after(L, blk, after):
            slot = ada_ctr[0] % 2
            ada_ctr[0] += 1
            P.dma("pool", ADA[slot], adaw_d[L, blk].rearrange("p (k c) -> p k c", k=NCH), "ada%d" % slot, after=after)
            return ADA[slot]

        def mod_mm(L, blk):
            abuf = ada_slots.pop((L, blk))
            for j in range(4):
                col = 4 * blk + j
                for kc in range(NCH):
                    P.mm(PSB[1][:, L * 64 + col: L * 64 + col + 1], abuf[:, kc, j * 128:(j + 1) * 128],
                         CACT[:, kc:kc + 1], kc == 0, kc == NCH - 1)
            P.tt("dve", MOD[:, L, 4 * blk:4 * blk + 4], PSB[1][:, L * 64 + 4 * blk: L * 64 + 4 * blk + 4],
                 pv("adab", L * 48 + 4 * blk, 4), ALU.add)

        def mod_vec(L, v):
            return MOD[:, L, v * 8:(v + 1) * 8]

        def derive(L, which):
            if which == "mix":
                P.stt("dve", DER[:, L, 0, :], mod_vec(L, 1), 1.0, pv("nmg", L * 8, 8), ALU.add, ALU.mult)
            elif which == "gate_m":
                P.ts("dve", DER[:, L, 1, :], mod_vec(L, 2), 1.0, None, ALU.add)
                if L == 0:
                    P.tt("dve", DER[:, L, 2, :], DER[:, L, 1, :], pv("b2", 0, 8), ALU.mult)
                else:
                    P.tt("dve", DER[:, L, 2, :], DER[:, L, 1, :], pv("pls", 0, 8), ALU.mult)
            elif which == "ffn":
                P.stt("dve", DER[:, L, 3, :], mod_vec(L, 4), 1.0, pv("nfg", L * 8, 8), ALU.add, ALU.mult)
            elif which == "gate_f":
                P.ts("dve", DER[:, L, 4, :], mod_vec(L, 5), 1.0, None, ALU.add)

        ft_ctr = [0]
        ft_n = [4]

        def ft():
            ft_ctr[0] += 1
            return FT[2 + ft_ctr[0] % ft_n[0]]

        RS = FT[0]
        MEAN = FT[1]

        def rsqrt_into(dst, src):
            P.act(dst, src, AF.Ln, bias=EPSB[:, 0:1])
            P.act(dst, dst, AF.Exp, scale=-0.5)

        def prep_steps(t0, n, gm, sh, dst_fn, pool_style=False, rs=None, rs_off=0, sq=None):
            rs_t = RS if rs is None else rs
            SQb = SQ if sq is None else sq
            steps = []

            def s_sq():
                for c in range(NCH):
                    P.act(SQb[:, c, :n], X[:, c, t0:t0 + n], AF.Square)
            steps.append(s_sq)

            def s_stat():
                for c in range(NCH):
                    P.mm(PSB[0][:, :n], ONESD, SQb[:, c, :n], c == 0, c == NCH - 1)
                rsqrt_into(rs_t[:, rs_off:rs_off + n], PSB[0][:, :n])
            steps.append(s_stat)
            if pool_style:
                return steps
            for c in range(NCH):
                def s_h(c=c):
                    t = ft()
                    P.tt("dve", t[:, :n], X[:, c, t0:t0 + n], rs_t[:, rs_off:rs_off + n], ALU.mult)
                    P.act(dst_fn(c), t[:, :n], AF.Identity, bias=sh[:, c:c + 1], scale=gm[:, c:c + 1])
                steps.append(s_h)
            return steps

        def run_interleaved(groups, steps, start=0):
            steps = list(steps)
            ng = len(groups)
            for gi, g in enumerate(groups):
                g()
                if gi >= start and steps:
                    remaining_groups = ng - gi
                    k = -(-len(steps) // remaining_groups)
                    for _ in range(k):
                        if steps:
                            steps.pop(0)()
            for s in steps:
                s()

        for blk in range(2):
            ada_slots[(0, blk)] = ada_dma(0, blk)
        for i, blk in enumerate((2, 3)):
            base = (NCH * NT * 4 + 63) // 64 * 64 + i * NCH * ADA_BLK * 2
            tv = view(base, [NCH, ADA_BLK], BF16)
            ada_slots[(0, blk)] = ada_dma(0, blk, buf=tv, key="adat%d" % i)
        w1_ops = []
        for b in range(4):
            w1_ops.append(P.dma("pool", W1[:, b], w1_d[b].rearrange("p (k c) -> p k c", k=NCH), "w1_%d" % b))
        for j, (t0, n) in enumerate(TILES):
            if j >= 2:
                P.dma("sp", X[:, :, t0:t0 + n], x_d[:, :, t0:t0 + n], "x%d" % j, after=[w1_ops[1]])
        for blk in range(4):
            mod_mm(0, blk)
        P.memset("dve", ACTB[:, :, 0:PAD], 0.0)
        for blk in range(2):
            ada_slots[(0, blk + 4)] = ada_dma(0, blk + 4) if blk else ada_dma_after(0, blk + 4, [w1_ops[3]])
        P.dma("pool", W2, w2_d.rearrange("p (k c) -> p k c", k=NCH), "w2")
        derive(0, "mix")
        next_ada = [4]

        def more_mod0(k):
            for _ in range(k):
                blk = next_ada[0]
                if blk >= N_ADA_BLK:
                    return
                mod_mm(0, blk)
                if blk + 2 < N_ADA_BLK and (0, blk + 2) not in ada_slots:
                    ada_slots[(0, blk + 2)] = ada_dma(0, blk + 2)
                next_ada[0] += 1

        gm0 = DER[:, 0, 0, :]
        sh0 = mod_vec(0, 0)
        U = ACTB

        def a1_prep(j):
            t0, n = TILES[j]
            hb = Hb[j % 2]
            return prep_steps(t0, n, gm0, sh0, lambda c: hb[:, c, :n])

        for s in a1_prep(0):
            s()
        bank_ctr = [0]
        bank_set = [[2, 3, 4, 5, 6, 7]]

        def nb():
            bank_ctr[0] += 1
            bs = bank_set[0]
            return PSB[bs[bank_ctr[0] % len(bs)]]

        for j, (t0, n) in enumerate(TILES):
            hb = Hb[j % 2]
            groups = []
            for oc in range(NCH):
                def g(oc=oc):
                    blk, jj = oc // 2, oc % 2
                    pa, pg = nb(), nb()
                    for kc in range(NCH):
                        P.mm(pa[:, :n], W1[:, blk, kc, jj * 128:(jj + 1) * 128], hb[:, kc, :n], kc == 0, kc == NCH - 1)
                    for kc in range(NCH):
                        P.mm(pg[:, :n], W1[:, blk, kc, 256 + jj * 128:256 + (jj + 1) * 128], hb[:, kc, :n],
                             kc == 0, kc == NCH - 1)
                    sg = ft()
                    P.act(sg[:, :n], pg[:, :n], AF.Sigmoid, bias=pv("b1", 8 + oc))
                    P.stt("dve", U[:, oc, PAD + t0:PAD + t0 + n], pa[:, :n], pv("b1", oc), sg[:, :n], ALU.add, ALU.mult)
                groups.append(g)
            steps = a1_prep(j + 1) if j + 1 < len(TILES) else []
            run_interleaved(groups, steps)
            if j == 0:
                P.ts("dve", U[:, :, PAD:PAD + HALO], U[:, :, PAD:PAD + HALO], pv("mask"), None, ALU.mult)
            if j == 1:
                more_mod0(2)
        derive(0, "gate_m")

        def late_mod0():
            derive(0, "ffn")
            derive(0, "gate_f")
            for blk in range(2):
                ada_slots[(1, blk)] = ada_dma(1, blk)

        ring_ctr = [0]

        def ring_dma(L, p):
            f0, nf = PIECES_L[L][p]
            slot = ring_ctr[0] % 2
            ring_ctr[0] += 1
            wg_s, wu_s, wd_s = ring[slot]
            off = NCH * 128 * f0
            P.dma("pool", wg_s[:, :, :nf * 128],
                  wg_d[L, :, off:off + NCH * nf * 128].rearrange("p (k c) -> p k c", k=NCH), "rg%d" % slot)
            P.dma("pool", wu_s[:, :, :nf * 128],
                  wu_d[L, :, off:off + NCH * nf * 128].rearrange("p (k c) -> p k c", k=NCH), "ru%d" % slot)
            P.dma("pool", wd_s[:, :nf, :],
                  wd_d[L, :, f0 * D:(f0 + nf) * D].rearrange("p (f c) -> p f c", f=nf), "rd%d" % slot)
            return slot

        T_D = 4

        def build_diag(c):
            dg = DIAG[c % 2]
            for k in range(T_D, KW):
                P.ts("dve", dg[:, k, :], IDENT, pv("wdw", k * 8 + c), None, ALU.mult)

        V = ACTB
        g_m1 = DER[:, 0, 1, :]
        gb2 = DER[:, 0, 2, :]

        MEANb = [FT[1], FT[5]]
        RSb = [FT[0], FT[4]]

        def a3_stats(j):
            t0, n = TILES[j]
            mean_t, rs_t = MEANb[j % 2], RSb[j % 2]
            steps = []

            def s_sq():
                for c in range(NCH):
                    P.act(SQ[:, c, :n], V[:, c, PAD + t0:PAD + t0 + n], AF.Square)
            steps.append(s_sq)

            def s_stat():
                for c in range(NCH):
                    P.mm(PSB[0][:, :n], ONESD, V[:, c, PAD + t0:PAD + t0 + n], c == 0, c == NCH - 1)
                for c in range(NCH):
                    P.mm(PSB[1][:, :n], ONESD, SQ[:, c, :n], c == 0, c == NCH - 1)
                P.copy("act", mean_t[:, :n], PSB[0][:, :n])
                t = ft()
                P.tt("dve", t[:, :n], mean_t[:, :n], mean_t[:, :n], ALU.mult)
                P.tt("dve", t[:, :n], PSB[1][:, :n], t[:, :n], ALU.subtract)
                P.ts("dve", t[:, :n], t[:, :n], 0.0, None, ALU.max)
                rsqrt_into(rs_t[:, :n], t[:, :n])
            steps.append(s_stat)
            return steps

        def a3_ln(j):
            t0, n = TILES[j]
            mean_t, rs_t = MEANb[j % 2], RSb[j % 2]
            sbuf = Hb[j % 2]
            steps = []
            for c in range(NCH):
                def s_ln(c=c):
                    t = ft()
                    P.tt("dve", t[:, :n], V[:, c, PAD + t0:PAD + t0 + n], mean_t[:, :n], ALU.subtract)
                    P.tt("dve", t[:, :n], t[:, :n], rs_t[:, :n], ALU.mult)
                    P.act(sbuf[:, c, :n], t[:, :n], AF.Silu, bias=pv("lnb", c), scale=pv("lng", c))
                steps.append(s_ln)
            return steps

        A3_ORD = [4, 3, 2, 1, 0]

        build_diag(0)
        ring_slots = {}
        for c in range(NCH):
            if c + 1 < NCH:
                build_diag(c + 1)
            if c == 1:
                ring_slots[(0, 0)] = ring_dma(0, 0)
            dg = DIAG[c % 2]
            if c == NCH - 1:
                ft_n[0] = 2
            for j in reversed(range(len(TILES))):
                t0, n = TILES[j]
                pc = nb()
                for k in range(T_D, KW):
                    P.mm(pc[:, :n], dg[:, k, :], U[:, c, PAD + t0 - 30 + k:PAD + t0 - 30 + k + n], k == T_D, k == KW - 1)
                acc = ft()
                for k in range(T_D):
                    src = U[:, c, PAD + t0 - 30 + k:PAD + t0 - 30 + k + n]
                    if k == 0:
                        P.ts("dve", acc[:, :n], src, pv("wdw", k * 8 + c), None, ALU.mult)
                    else:
                        P.stt("dve", acc[:, :n], src, pv("wdw", k * 8 + c), acc[:, :n], ALU.mult, ALU.add)
                P.stt("dve", U[:, c, PAD + t0:PAD + t0 + n], pc[:, :n], pv("bdw", c), acc[:, :n], ALU.add, ALU.add)
                if OPT_A3:
                    P.act(X[:, c, t0:t0 + n], X[:, c, t0:t0 + n], AF.Identity, bias=DER[:, 0, 2, c:c + 1])
                if c == NCH - 1:
                    if j == A3_ORD[0]:
                        for s_ in a3_stats(A3_ORD[0]):
                            s_()
                    elif j == A3_ORD[1]:
                        for s_ in a3_stats(A3_ORD[1]) + a3_ln(A3_ORD[0]):
                            s_()

        for idx, j in enumerate(A3_ORD):
            t0, n = TILES[j]
            sbuf = Hb[j % 2]
            groups = []
            for do in range(NCH):
                def g(do=do, t0=t0, n=n, sbuf=sbuf):
                    po = nb()
                    for kc in range(NCH):
                        P.mm(po[:, :n], W2[:, kc, do * 128:(do + 1) * 128], sbuf[:, kc, :n], kc == 0, kc == NCH - 1)
                    P.stt("dve", X[:, do, t0:t0 + n], po[:, :n], g_m1[:, do:do + 1], X[:, do, t0:t0 + n],
                          ALU.mult, ALU.add)
                groups.append(g)
            steps = []
            ln_s = a3_ln(A3_ORD[idx + 1]) if idx + 1 < len(A3_ORD) else []
            st_s = a3_stats(A3_ORD[idx + 2]) if idx + 2 < len(A3_ORD) else []
            steps += st_s[:1] + ln_s[:3] + st_s[1:] + ln_s[3:]
            if idx < 3:
                steps = [lambda: more_mod0(2)] + steps
            elif idx == 3:
                steps = [late_mod0] + steps
            if idx + 1 == len(A3_ORD) and stop_after != "A":
                t0f, nf0 = TILES[A3_ORD[0]]
                steps += prep_steps(t0f, nf0, DER[:, 0, 3, :], mod_vec(0, 3),
                                    lambda c: ACTB[:, c, PAD + t0f:PAD + t0f + nf0], sq=SQ, rs=FT[4])
            steps = list(steps)
            k_ = -(-len(steps) // 5)
            for g in groups:
                g()
                for _ in range(k_):
                    if steps:
                        steps.pop(0)()
            for s_ in steps:
                s_()
        ft_n[0] = 4
        bank_set[0] = [2, 3, 4, 5]

        H2 = ACTB

        def ffn(L, tiles, l1_mod=False, final=False, first_prepped=False, all_prepped=False, post_down=None, prep_fn=None, lead_steps=None, tiles_last=None):
            gmf = DER[:, L, 3, :]
            shf = mod_vec(L, 3)
            g_f1 = DER[:, L, 4, :]

            def h2_prep(j):
                t0, n = TILES[j]
                return prep_steps(t0, n, gmf, shf, lambda c: H2[:, c, PAD + t0:PAD + t0 + n], sq=Hb[0])

            if not (first_prepped or all_prepped):
                for s in h2_prep(tiles[0]):
                    s()
            leftover = []
            for p, (f0, nf) in enumerate(PIECES_L[L]):
                if (L, p) not in ring_slots:
                    ring_slots[(L, p)] = ring_dma(L, p)
                slot = ring_slots.pop((L, p))
                wg_s, wu_s, wd_s = ring[slot]
                nxt = None
                if l1_mod:
                    for blk in (2 * p, 2 * p + 1):
                        if (1, blk) not in ada_slots:
                            ada_slots[(1, blk)] = ada_dma(1, blk)
                    for blk in (2 * p, 2 * p + 1):
                        mod_mm(1, blk)
                    for blk in (2 * p + 2, 2 * p + 3):
                        if OPT_ADA and blk < N_ADA_BLK and (1, blk) not in ada_slots:
                            ada_slots[(1, blk)] = ada_dma(1, blk)

                def gu(j, zb):
                    t0, n = TILES[j]
                    groups = []
                    for fi in range(nf):
                        def g(fi=fi):
                            pg, pu = nb(), nb()
                            for kc in range(NCH):
                                P.mm(pg[:, :n], wg_s[:, kc, fi * 128:(fi + 1) * 128], H2[:, kc, PAD + t0:PAD + t0 + n],
                                     kc == 0, kc == NCH - 1)
                            for kc in range(NCH):
                                P.mm(pu[:, :n], wu_s[:, kc, fi * 128:(fi + 1) * 128], H2[:, kc, PAD + t0:PAD + t0 + n],
                                     kc == 0, kc == NCH - 1)
                            sg = ft()
                            P.act(sg[:, :n], pg[:, :n], AF.Silu)
                            P.tt("dve", Z[zb][:, fi, :n], pu[:, :n], sg[:, :n], ALU.mult)
                        groups.append(g)
                    return groups

                def down(j, zb):
                    t0, n = TILES[j]
                    groups = []
                    for do in range(NCH):
                        def g(do=do):
                            pd = PSB[6 + do % 2]
                            for fi in range(nf):
                                P.mm(pd[:, :n], wd_s[:, fi, do * 128:(do + 1) * 128], Z[zb][:, fi, :n],
                                     fi == 0, fi == nf - 1)
                            P.stt("dve", X[:, do, t0:t0 + n], pd[:, :n], g_f1[:, do:do + 1], X[:, do, t0:t0 + n],
                                  ALU.mult, ALU.add)
                        groups.append(g)
                    return groups

                prev = None
                last_piece = (p == N_PIECES - 1) and post_down is not None
                tiles_all = tiles
                if p == N_PIECES - 1 and tiles_last is not None:
                    tiles = tiles_last
                if p >= 1 and not last_piece:
                    bank_set[0] = [0, 2, 3, 4, 5] if l1_mod else [0, 1, 2, 3, 4, 5]
                else:
                    bank_set[0] = [2, 3, 4, 5]
                finished = []
                for ti, j in enumerate(tiles):
                    zb = ti % 2
                    steps = []
                    if p == 0 and ti + 1 < len(tiles) and not all_prepped:
                        steps = (prep_fn or h2_prep)(tiles[ti + 1])
                    if last_piece and finished:
                        steps = steps + post_down(finished.pop(0))
                    if p == 0 and lead_steps:
                        k_ = -(-len(lead_steps) // max(1, len(tiles) - 1 - ti))
                        steps = steps + lead_steps[:k_]
                        del lead_steps[:k_]
                    groups = gu(j, zb)
                    if prev is not None:
                        groups = groups + down(*prev)
                        finished.append(prev[0])
                    run_interleaved(groups, steps)
                    prev = (j, zb)
                dgroups = down(*prev)
                finished.append(prev[0])
                if last_piece:
                    run_interleaved(dgroups, post_down(finished.pop(0)))
                    for jj in finished:
                        leftover.extend(post_down(jj))
                else:
                    for g in dgroups:
                        g()
                tiles = tiles_all
                nn = None
                if p + 2 < N_PIECES:
                    nn = (L, p + 2)
                elif L == 0:
                    nn = (1, p + 2 - N_PIECES)
                if nn is not None and nn not in ring_slots:
                    ring_slots[nn] = ring_dma(*nn)
                if l1_mod:
                    for blk in (2 * p + 2, 2 * p + 3):
                        if blk < N_ADA_BLK and (1, blk) not in ada_slots:
                            ada_slots[(1, blk)] = ada_dma(1, blk)
            bank_set[0] = [2, 3, 4, 5]
            return leftover

        gm1 = DER[:, 1, 0, :]
        gpool = DER[:, 1, 2, :]
        RSX = FT[0]
        Hb1_off = H_off + NCH * 512 * 2
        HS = [view(Hb1_off + i * 528 * 4, [528], F32) for i in range(2)]
        XC = view(Hb1_off + 2 * 528 * 4, [NCH, 16], F32)
        T16 = view(Hb1_off + 2 * 528 * 4 + NCH * 16 * 4, [16], F32)
        SA = [FT[4], FT[5]]

        def pool_setup():
            derive(1, "mix")
            derive(1, "gate_m")
            derive(1, "ffn")
            derive(1, "gate_f")
            P.dma("pool", PW, pw_d.rearrange("p (g k c) -> p g k c", g=4, k=2), "pw")
            t0, n = TILES[0]
            for s in prep_steps(t0, n, None, None, None, pool_style=True, rs=RSX, rs_off=16, sq=Hb[0]):
                s()
            P.ts("dve", RSX[:, 0:16], RSX[:, 16 + n - 16:16 + n], pv("mask"), None, ALU.mult)
            for c in range(NCH):
                P.stt("dve", XC[:, c, :], X[:, c, t0 + n - 16:t0 + n], gm1[:, c:c + 1], RSX[:, 0:16], ALU.mult, ALU.mult)

        def pool_tile_steps(j, with_h2=True):
            t0, n = TILES[j]
            steps = list(prep_steps(t0, n, None, None, None, pool_style=True, rs=RSX, rs_off=16, sq=Hb[0]))
            for c in range(NCH):
                g = c // 2
                w = POOL_W[g]
                hs = HS[c % 2]
                mdst = ACTB[:, c, PAD + t0:PAD + t0 + n]
                def s_hs(c=c, hs=hs):
                    P.copy("act", hs[:, 0:16], XC[:, c, :])
                    P.stt("dve", hs[:, 16:16 + n], X[:, c, t0:t0 + n], gm1[:, c:c + 1], RSX[:, 16:16 + n],
                          ALU.mult, ALU.mult)
                    P.copy("act", XC[:, c, :], hs[:, n:n + 16])
                steps.append(s_hs)
                a = hs
                s_ = 1
                k = 0
                while s_ < w:
                    b = SA[k % 2]
                    steps.append(lambda a=a, b=b, s_=s_: P.tt("dve", b[:, s_:16 + n], a[:, s_:16 + n],
                                                              a[:, 0:16 + n - s_], ALU.add))
                    a = b
                    s_ *= 2
                    k += 1
                steps.append(lambda a=a, hs=hs, mdst=mdst, w=w: P.stt("dve", mdst, a[:, 16:16 + n], 1.0 / w,
                                                                      hs[:, 16:16 + n], ALU.mult, ALU.subtract))
                if j == 1:
                    def s_edge(a=a, hs=hs, c=c, g=g):
                        P.tt("dve", T16, a[:, 16:32], pv("pscale", g * 16, 16), ALU.mult)
                        P.tt("dve", ACTB[:, c, PAD + t0:PAD + t0 + 16], T16, hs[:, 16:32], ALU.subtract)
                    steps.append(s_edge)

            for g in range(4):
                def s_mm(g=g):
                    for do in range(2):
                        pp = nb()
                        for kc in range(2):
                            P.mm(pp[:, :n], PW[:, g, kc, do * 128:(do + 1) * 128],
                                 ACTB[:, 2 * g + kc, PAD + t0:PAD + t0 + n], kc == 0, kc == 1)
                        co = 2 * g + do
                        P.stt("dve", X[:, co, t0:t0 + n], pp[:, :n], gpool[:, co:co + 1], X[:, co, t0:t0 + n],
                              ALU.mult, ALU.add)
                steps.append(s_mm)
            if with_h2:
                steps += prep_steps(t0, n, DER[:, 1, 3, :], mod_vec(1, 3),
                                    lambda c: ACTB[:, c, PAD + t0:PAD + t0 + n], sq=Hb[0], rs=FT[1])
            return steps

        def pool_post_down(j):
            if j == 0:
                return [pool_setup]
            return pool_tile_steps(j)

        ring_slots[(0, 1)] = ring_dma(0, 1)
        ft_n[0] = 2
        pool_left = []
        if stop_after == "F0":
            ffn(0, list(A3_ORD), l1_mod=True, first_prepped=True)
        elif stop_after == "P":
            ffn(0, list(A3_ORD), l1_mod=True, first_prepped=True)
            pool_setup()
            for j in range(1, len(TILES)):
                for st_ in pool_tile_steps(j, with_h2=False):
                    st_()
        elif stop_after != "A":
            pool_left = ffn(0, list(A3_ORD), l1_mod=True, first_prepped=True, post_down=pool_post_down,
                            tiles_last=list(range(len(TILES))))

        outs = []

        def final_steps(j):
            t0, n = TILES[j]
            steps = []
            if stop_after is None:
                steps += prep_steps(t0, n, None, None, None, pool_style=True, sq=Hb[0])
                for c in range(NCH):
                    def s_f(c=c):
                        P.stt("dve", X[:, c, t0:t0 + n], X[:, c, t0:t0 + n], pv("fing", c), RS[:, :n], ALU.mult, ALU.mult)
                        if OPT_CHST:
                            outs.append(P.dma("sp", y_d[:, c, t0 - HALO:t0 - HALO + n], X[:, c, t0:t0 + n],
                                              "y%d_%d" % (j, c)))
                    steps.append(s_f)
                if OPT_CHST:
                    return steps

            def s_out():
                outs.append(P.dma("sp", y_d[:, :, t0 - HALO:t0 - HALO + n], X[:, :, t0:t0 + n], "y%d" % j))
            steps.append(s_out)
            return steps

        if stop_after not in ("A", "F0", "P"):
            for st_ in ffn(1, list(range(1, len(TILES))), all_prepped=True, lead_steps=pool_left,
                           post_down=final_steps if OPT_FINAL else None):
                st_()
            if not OPT_FINAL:
                for j in range(1, len(TILES)):
                    for st_ in final_steps(j):
                        st_()
        else:
            for j in range(1, len(TILES)):
                for st_ in final_steps(j):
                    st_()
        P.barrier_wait("sp", outs)
        P.emit()
    return nc


def _chunked(v):
    v = np.asarray(v, np.float32)
    lead = v.shape[:-1]
    n = v.shape[-1] // 128
    return np.moveaxis(v.reshape(lead + (n, 128)), -1, 0)


def _kmajor(w):
    K, N = w.shape
    return np.ascontiguousarray(w.reshape(K // 128, 128, N).transpose(1, 0, 2))


def prepare_inputs(x, c, ada_w, ada_b, norm_mix_g, norm_ffn_g, conv_w1, conv_b1, conv_wdw, conv_bdw,
                   conv_ln_g, conv_ln_b, conv_w2, conv_b2, pool_w, pool_ls, ffn_w_gate, ffn_w_up,
                   ffn_w_down, final_g):
    f = np.float32
    x = np.asarray(x, f)
    B, S, _ = x.shape
    n_cores = 8
    per_seq = S // TOK
    adaw = np.stack([np.stack([_kmajor(np.asarray(ada_w[L], f)[:, b * ADA_BLK:(b + 1) * ADA_BLK]).reshape(128, -1)
                               for b in range(N_ADA_BLK)]) for L in range(2)])
    w1 = np.asarray(conv_w1[0], f)
    blocks = []
    for b in range(4):
        cols = np.concatenate([np.arange((2 * b) * 128, (2 * b + 2) * 128),
                               D + np.arange((2 * b) * 128, (2 * b + 2) * 128)])
        blocks.append(_kmajor(w1[:, cols]).reshape(128, -1))
    w1l = np.stack(blocks)
    w2l = _kmajor(np.asarray(conv_w2[0], f)).reshape(128, -1)

    def piece_major(w, L):
        parts = []
        for (f0, nf) in PIECES_L[L]:
            parts.append(_kmajor(w[:, f0 * 128:(f0 + nf) * 128]).reshape(128, -1))
        return np.concatenate(parts, axis=1)
    wgl = np.stack([piece_major(np.asarray(ffn_w_gate[L], f), L) for L in range(2)])
    wul = np.stack([piece_major(np.asarray(ffn_w_up[L], f), L) for L in range(2)])
    wdl = np.stack([_kmajor(np.asarray(ffn_w_down[L], f)).reshape(128, -1) for L in range(2)])
    pw = np.asarray(pool_w[0], f)
    pwl = np.ascontiguousarray(pw.reshape(4, 2, 128, 256).transpose(2, 0, 1, 3)).reshape(128, -1)

    in_maps = []
    for core in range(n_cores):
        b = core // per_seq
        k = core % per_seq
        start = k * TOK
        xs = np.zeros((NT, D), f)
        if k > 0:
            xs[:] = x[b, start - HALO:start + TOK]
        else:
            xs[HALO:] = x[b, :TOK]
        xl = np.ascontiguousarray(xs.reshape(NT, NCH, 128).transpose(2, 1, 0))
        pvv = np.zeros((128, NPV), f)

        def put(name, arr):
            arr = np.asarray(arr, f).reshape(128, -1)
            pvv[:, PV[name]:PV[name] + arr.shape[1]] = arr
        put("c", _chunked(np.asarray(c, f)[b]))
        put("adab", _chunked(np.asarray(ada_b, f)))
        put("nmg", _chunked(np.asarray(norm_mix_g, f)))
        put("nfg", _chunked(np.asarray(norm_ffn_g, f)))
        put("b1", _chunked(np.asarray(conv_b1[0], f)))
        put("wdw", _chunked(np.asarray(conv_wdw[0], f)))
        put("bdw", _chunked(np.asarray(conv_bdw[0], f)))
        put("lng", _chunked(np.asarray(conv_ln_g[0], f)))
        put("lnb", _chunked(np.asarray(conv_ln_b[0], f)))
        put("b2", _chunked(np.asarray(conv_b2[0], f)))
        put("pls", _chunked(np.asarray(pool_ls[0], f)))
        put("fing", _chunked(np.asarray(final_g, f)))
        pvv[:, PV["mask"]] = 0.0 if k == 0 else 1.0
        psc = np.zeros((4, 16), f)
        for g, w in enumerate(POOL_W):
            for t in range(16):
                psc[g, t] = 1.0 / (min(w, t + 1) if k == 0 else w)
        pvv[:, PV["pscale"]:PV["pscale"] + 64] = psc.reshape(1, 64)
        in_maps.append({"x": xl, "pv": pvv, "adaw": adaw, "w1": w1l, "w2": w2l, "wg": wgl, "wu": wul,
                        "wd": wdl, "pw": pwl})
    return in_maps


def assemble(results, B, S):
    per_seq = S // TOK
    out = np.empty((B, S, D), np.float32)
    for core, r in enumerate(results):
        b = core // per_seq
        k = core % per_seq
        y = r["y"]
        out[b, k * TOK:(k + 1) * TOK] = y.transpose(2, 1, 0).reshape(TOK, D)
    return out


_NC_CACHE = {}
STOP_AFTER = None


def kernel(**inputs):
    x = inputs["x"]
    B, S, _ = x.shape
    in_maps = prepare_inputs(**inputs)
    if STOP_AFTER not in _NC_CACHE:
        _NC_CACHE[STOP_AFTER] = build_nc(STOP_AFTER)
    res = run_bass_kernel_spmd(_NC_CACHE[STOP_AFTER], in_maps, core_ids=list(range(8)))
    return assemble(res.results, B, S)
```
